# Optimizing a Trainium2 kernel written in Bass

```python
import math
import jax
import jax.numpy as jnp
from jax import lax
import numpy as np


D_MODEL = 2048
BATCH = 16
SEQ = 2048
DEPTH = 2

GRID_W = 64
CTX_LEN = 256
EPS = 1e-6
NEG_INF = -1e30

N_HEADS = 16
N_KV_HEADS = 4
GQA_GROUP = N_HEADS // N_KV_HEADS
HEAD_DIM = 64
ATTN_W = N_HEADS * HEAD_DIM
KV_W = N_KV_HEADS * HEAD_DIM
WINDOW = 128
BLOCK = 128
ROPE_FREQS = HEAD_DIM // 4
ROPE_BASE = 10000.0

HYENA_W = D_MODEL // 4
HYENA_ORDER = 2
FILTER_BANDS = 16
FILTER_EMB = 1 + 2 * FILTER_BANDS
FILTER_HIDDEN = 64
FILTER_INNER = 2
DECAY_TARGET = 1e-2
FAST_DECAY_PCT = 0.3
SLOW_DECAY_PCT = 1.5

POOL_W = D_MODEL // 4
POOL_WINDOWS = (2, 4, 8, 16)
POOL_GROUP = POOL_W // len(POOL_WINDOWS)

N_BRANCH = 3
D_FF = 4 * D_MODEL

Q_OFF = 0
K_OFF = Q_OFF + ATTN_W
V_OFF = K_OFF + KV_W
HY_OFF = V_OFF + KV_W
POOL_OFF = HY_OFF + 3 * HYENA_W
GATE_OFF = POOL_OFF + POOL_W
IN_W = GATE_OFF + N_BRANCH * D_MODEL

kernel_name = 'hybrid_dit_attn_hyena_pool'


def _rms_norm(x, g):
    xf = x.astype(jnp.float32)
    y = xf * lax.rsqrt(jnp.mean(xf * xf, axis=-1, keepdims=True) + EPS)
    return y.astype(x.dtype) * g


def _modulation(cvec, w_mod, b_mod):
    m = jax.nn.silu(cvec) @ w_mod + b_mod
    return jnp.split(m, 6, axis=-1)


def _split_in(z):
    return (z[..., Q_OFF:K_OFF], z[..., K_OFF:V_OFF], z[..., V_OFF:HY_OFF],
            z[..., HY_OFF:POOL_OFF], z[..., POOL_OFF:GATE_OFF], z[..., GATE_OFF:IN_W])


def _heads(z, n_heads):
    return z.reshape(z.shape[0], z.shape[1], n_heads, HEAD_DIM)


def _axial_rope_angles(L):
    rows = L // GRID_W
    row = jnp.repeat(jnp.arange(rows, dtype=jnp.float32), GRID_W)
    col = jnp.tile(jnp.arange(GRID_W, dtype=jnp.float32), rows)
    inv = ROPE_BASE ** (-jnp.arange(ROPE_FREQS, dtype=jnp.float32) / ROPE_FREQS)
    return jnp.stack([row[:, None] * inv, col[:, None] * inv], axis=1)


def _apply_rope(x, ang):
    B, L, H, D = x.shape
    xs = x.reshape(B, L, H, 2, 2, ROPE_FREQS)
    cos = jnp.cos(ang)[None, :, None].astype(x.dtype)
    sin = jnp.sin(ang)[None, :, None].astype(x.dtype)
    x1, x2 = xs[..., 0, :], xs[..., 1, :]
    out = jnp.stack([x1 * cos - x2 * sin, x2 * cos + x1 * sin], axis=-2)
    return out.reshape(B, L, H, D)


def _sink_logits(sink, B, Q):
    s = sink.astype(jnp.float32).reshape(N_KV_HEADS, GQA_GROUP)[None, :, :, None, None]
    return jnp.broadcast_to(s, (B, N_KV_HEADS, GQA_GROUP, Q, 1))


def _latent_window_attention(q, k, v, kc, vc, sink):
    B, L, H, D = q.shape
    Lc = kc.shape[1]
    nb = L // BLOCK
    scale = D ** -0.5
    pad = ((0, 0), (BLOCK, BLOCK), (0, 0), (0, 0))
    kp = jnp.pad(k, pad)
    vp = jnp.pad(v, pad)
    qi = jnp.arange(BLOCK)[:, None]
    kj = jnp.arange(3 * BLOCK)[None, :]
    band = jnp.abs(BLOCK + qi - kj) <= WINDOW
    sink_b = _sink_logits(sink, B, BLOCK)

    def block(n):
        start = n * BLOCK
        qn = lax.dynamic_slice_in_dim(q, start, BLOCK, axis=1).reshape(B, BLOCK, N_KV_HEADS, GQA_GROUP, D)
        kn = lax.dynamic_slice_in_dim(kp, start, 3 * BLOCK, axis=1)
        vn = lax.dynamic_slice_in_dim(vp, start, 3 * BLOCK, axis=1)
        kpos = start - BLOCK + kj
        mask = band & (kpos >= 0) & (kpos < L)
        s_loc = jnp.einsum('bqhgd,bkhd->bhgqk', qn, kn, preferred_element_type=jnp.float32) * scale
        s_loc = jnp.where(mask, s_loc, NEG_INF)
        s_ctx = jnp.einsum('bqhgd,bchd->bhgqc', qn, kc, preferred_element_type=jnp.float32) * scale
        p = jax.nn.softmax(jnp.concatenate([s_loc, s_ctx, sink_b], axis=-1), axis=-1).astype(v.dtype)
        o = (jnp.einsum('bhgqk,bkhd->bqhgd', p[..., :3 * BLOCK], vn)
             + jnp.einsum('bhgqc,bchd->bqhgd', p[..., 3 * BLOCK:3 * BLOCK + Lc], vc))
        return o.reshape(B, BLOCK, H * D)

    out = lax.map(block, jnp.arange(nb))
    return out.transpose(1, 0, 2, 3).reshape(B, L, H * D)


def _context_attention(qc, kc, vc, sink):
    B, Lc, H, D = qc.shape
    qg = qc.reshape(B, Lc, N_KV_HEADS, GQA_GROUP, D)
    s = jnp.einsum('bqhgd,bkhd->bhgqk', qg, kc, preferred_element_type=jnp.float32) * D ** -0.5
    p = jax.nn.softmax(jnp.concatenate([s, _sink_logits(sink, B, Lc)], axis=-1), axis=-1).astype(vc.dtype)
    o = jnp.einsum('bhgqk,bkhd->bqhgd', p[..., :Lc], vc)
    return o.reshape(B, Lc, H * D)


def _hyena_filters(L, w0, b0, w1, b1, freq, w2):
    t = jnp.linspace(0.0, 1.0, L, dtype=jnp.float32)[:, None]
    w = 2.0 * math.pi * jnp.arange(L, dtype=jnp.float32)[:, None] / L
    bands = jnp.linspace(1e-4, FILTER_BANDS - 1, FILTER_BANDS, dtype=jnp.float32)[None, :]
    z = jnp.concatenate([t, jnp.cos(bands * w), -jnp.sin(bands * w)], axis=-1)
    h = jnp.sin(freq * (z @ w0 + b0))
    for i in range(FILTER_INNER):
        h = jnp.sin(freq * (h @ w1[i] + b1[i]))
    h = (h @ w2).reshape(L, HYENA_ORDER, 2, HYENA_W)
    deltas = jnp.linspace(math.log(DECAY_TARGET) / SLOW_DECAY_PCT, math.log(DECAY_TARGET) / FAST_DECAY_PCT,
                          HYENA_W, dtype=jnp.float32)
    decay = jnp.exp(-t * jnp.abs(deltas))
    return h * decay[:, None, None, :]


def _bidir_fftconv(u, h_fwd, h_bwd, d_skip):
    L, C = h_fwd.shape
    n = 2 * L
    k = jnp.concatenate([h_fwd, jnp.zeros((1, C), h_fwd.dtype), h_bwd[:L - 1][::-1]], axis=0)
    k_f = jnp.fft.rfft(k.astype(jnp.float32), n=n, axis=0)
    u_f = jnp.fft.rfft(u.astype(jnp.float32), n=n, axis=1)
    y = jnp.fft.irfft(u_f * k_f[None], n=n, axis=1)[:, :L]
    return (y + u.astype(jnp.float32) * d_skip).astype(u.dtype)


def _hyena_mixer(u, conv_w, conv_b, filters, d_skip):
    L = u.shape[1]
    up = jnp.pad(u, ((0, 0), (1, 1), (0, 0)))
    uc = up[:, :L] * conv_w[0] + up[:, 1:L + 1] * conv_w[1] + up[:, 2:] * conv_w[2] + conv_b
    v, x1, x2 = jnp.split(uc, 3, axis=-1)
    z = v
    for o, gate in enumerate((x1, x2)):
        z = gate * _bidir_fftconv(z, filters[:, o, 0], filters[:, o, 1], d_skip[o])
    return z


def _multiscale_pool(u, w_grp, scale):
    B, L, _ = u.shape
    ug = u.reshape(B, L, len(POOL_WINDOWS), POOL_GROUP)
    csum = jnp.cumsum(ug.astype(jnp.float32), axis=1)
    S = jnp.concatenate([jnp.zeros((B, 1, len(POOL_WINDOWS), POOL_GROUP), jnp.float32), csum], axis=1)
    t = jnp.arange(L)
    pooled = []
    for g, win in enumerate(POOL_WINDOWS):
        a = jnp.clip(t - win // 2, 0, L)
        b = jnp.clip(t + win // 2, 0, L)
        cnt = (b - a).astype(jnp.float32)[None, :, None]
        pooled.append((S[:, b, g] - S[:, a, g]) / cnt)
    pooled = jnp.stack(pooled, axis=2)
    y = jnp.einsum('blgc,gcd->blgd', pooled - ug.astype(jnp.float32), w_grp)
    return (y.reshape(B, L, POOL_W) * scale).astype(u.dtype)


def _merge(y_att, y_hy, y_pool, gates, w_att_o, w_hy_o, w_pool_o, w_out):
    g_att, g_hy, g_pool = jnp.split(jax.nn.sigmoid(gates), N_BRANCH, axis=-1)
    m = g_att * (y_att @ w_att_o) + g_hy * (y_hy @ w_hy_o) + g_pool * (y_pool @ w_pool_o)
    return m @ w_out


def _sqrelu_mlp(h, w1, w2):
    return jnp.square(jax.nn.relu(h @ w1)) @ w2


def setup_inputs(seed: int = 0) -> dict:
    key = jax.random.key(seed)
    ks = jax.random.split(key, 32)

    def nrm(k, shape, scale):
        return jax.random.normal(k, shape, jnp.float32) * scale

    def gain(k, shape):
        return 1.0 + 0.1 * jax.random.normal(k, shape, jnp.float32)

    return {
        'x': nrm(ks[0], (BATCH, SEQ, D_MODEL), 1.0),
        'c': nrm(ks[1], (BATCH, D_MODEL), 1.0),
        'ctx': nrm(ks[2], (BATCH, CTX_LEN, D_MODEL), 1.0),
        'c_ctx': nrm(ks[3], (D_MODEL,), 1.0),
        'norm1_g': gain(ks[4], (DEPTH, D_MODEL)),
        'norm2_g': gain(ks[5], (DEPTH, D_MODEL)),
        'w_mod': nrm(ks[6], (DEPTH, D_MODEL, 6 * D_MODEL), 0.5 * D_MODEL ** -0.5),
        'b_mod': nrm(ks[7], (DEPTH, 6 * D_MODEL), 0.02),
        'w_in': nrm(ks[8], (DEPTH, D_MODEL, IN_W), D_MODEL ** -0.5),
        'q_norm_g': gain(ks[9], (DEPTH, HEAD_DIM)),
        'k_norm_g': gain(ks[10], (DEPTH, HEAD_DIM)),
        'sink': nrm(ks[11], (DEPTH, N_HEADS), 0.5),
        'hy_conv_w': nrm(ks[12], (DEPTH, 3, 3 * HYENA_W), 3.0 ** -0.5),
        'hy_conv_b': nrm(ks[13], (DEPTH, 3 * HYENA_W), 0.02),
        'filt_w0': nrm(ks[14], (DEPTH, FILTER_EMB, FILTER_HIDDEN), FILTER_EMB ** -0.5),
        'filt_b0': nrm(ks[15], (DEPTH, FILTER_HIDDEN), 0.1),
        'filt_w1': nrm(ks[16], (DEPTH, FILTER_INNER, FILTER_HIDDEN, FILTER_HIDDEN), FILTER_HIDDEN ** -0.5),
        'filt_b1': nrm(ks[17], (DEPTH, FILTER_INNER, FILTER_HIDDEN), 0.1),
        'filt_freq': gain(ks[18], (DEPTH, FILTER_HIDDEN)),
        'filt_w2': nrm(ks[19], (DEPTH, FILTER_HIDDEN, HYENA_ORDER * 2 * HYENA_W), 0.05 * FILTER_HIDDEN ** -0.5),
        'hy_bias': nrm(ks[20], (DEPTH, HYENA_ORDER, HYENA_W), 0.5),
        'pool_w': nrm(ks[21], (DEPTH, len(POOL_WINDOWS), POOL_GROUP, POOL_GROUP), POOL_GROUP ** -0.5),
        'pool_scale': gain(ks[22], (DEPTH, POOL_W)),
        'w_att_o': nrm(ks[23], (DEPTH, ATTN_W, D_MODEL), ATTN_W ** -0.5),
        'w_hy_o': nrm(ks[24], (DEPTH, HYENA_W, D_MODEL), HYENA_W ** -0.5),
        'w_pool_o': nrm(ks[25], (DEPTH, POOL_W, D_MODEL), POOL_W ** -0.5),
        'w_out': nrm(ks[26], (DEPTH, D_MODEL, D_MODEL), D_MODEL ** -0.5),
        'mlp_w1': nrm(ks[27], (DEPTH, D_MODEL, D_FF), D_MODEL ** -0.5),
        'mlp_w2': nrm(ks[28], (DEPTH, D_FF, D_MODEL), D_FF ** -0.5),
    }


def reference(x, c, ctx, c_ctx, norm1_g, norm2_g, w_mod, b_mod, w_in, q_norm_g, k_norm_g, sink,
              hy_conv_w, hy_conv_b, filt_w0, filt_b0, filt_w1, filt_b1, filt_freq, filt_w2, hy_bias,
              pool_w, pool_scale, w_att_o, w_hy_o, w_pool_o, w_out, mlp_w1, mlp_w2):
    L = x.shape[1]
    Lc = ctx.shape[1]
    ang = _axial_rope_angles(L)
    c_lat = c[:, None, :]
    c_con = c_ctx[None, None, :]
    for l in range(DEPTH):
        last = l == DEPTH - 1
        filt = (filt_w0[l], filt_b0[l], filt_w1[l], filt_b1[l], filt_freq[l], filt_w2[l])
        sh1, sc1, ga1, sh2, sc2, ga2 = _modulation(c_lat, w_mod[l], b_mod[l])
        csh1, csc1, cga1, csh2, csc2, cga2 = _modulation(c_con, w_mod[l], b_mod[l])
        hx = _rms_norm(x, norm1_g[l]) * (1.0 + sc1) + sh1
        hc = _rms_norm(ctx, norm1_g[l]) * (1.0 + csc1) + csh1

        if last:
            kc_raw, vc_raw = jnp.split(hc @ w_in[l][:, K_OFF:HY_OFF], 2, axis=-1)
        else:
            qc_raw, kc_raw, vc_raw, hyc, poolc, gc = _split_in(hc @ w_in[l])
        kc = _rms_norm(_heads(kc_raw, N_KV_HEADS), k_norm_g[l])
        vc = _heads(vc_raw, N_KV_HEADS)

        qx_raw, kx_raw, vx_raw, hyx, poolx, gx = _split_in(hx @ w_in[l])
        qx = _apply_rope(_rms_norm(_heads(qx_raw, N_HEADS), q_norm_g[l]), ang)
        kx = _apply_rope(_rms_norm(_heads(kx_raw, N_KV_HEADS), k_norm_g[l]), ang)
        vx = _heads(vx_raw, N_KV_HEADS)
        y_att = _latent_window_attention(qx, kx, vx, kc, vc, sink[l])
        y_hy = _hyena_mixer(hyx, hy_conv_w[l], hy_conv_b[l], _hyena_filters(L, *filt), hy_bias[l])
        y_pool = _multiscale_pool(poolx, pool_w[l], pool_scale[l])
        x_new = x + ga1 * _merge(y_att, y_hy, y_pool, gx, w_att_o[l], w_hy_o[l], w_pool_o[l], w_out[l])
        x_new = x_new + ga2 * _sqrelu_mlp(_rms_norm(x_new, norm2_g[l]) * (1.0 + sc2) + sh2, mlp_w1[l], mlp_w2[l])

        if not last:
            qc = _rms_norm(_heads(qc_raw, N_HEADS), q_norm_g[l])
            yc_att = _context_attention(qc, kc, vc, sink[l])
            yc_hy = _hyena_mixer(hyc, hy_conv_w[l], hy_conv_b[l], _hyena_filters(Lc, *filt), hy_bias[l])
            yc_pool = _multiscale_pool(poolc, pool_w[l], pool_scale[l])
            ctx_new = ctx + cga1 * _merge(yc_att, yc_hy, yc_pool, gc, w_att_o[l], w_hy_o[l], w_pool_o[l], w_out[l])
            ctx = ctx_new + cga2 * _sqrelu_mlp(_rms_norm(ctx_new, norm2_g[l]) * (1.0 + csc2) + csh2,
                                               mlp_w1[l], mlp_w2[l])
        x = x_new
    return x
```

```python
import math
from contextlib import ExitStack
import numpy as np
import ml_dtypes
import concourse.bass as bass
import concourse.mybir as mybir
from concourse.bass_utils import run_bass_kernel_spmd

F32 = mybir.dt.float32
BF16 = mybir.dt.bfloat16
ALU = mybir.AluOpType
AF = mybir.ActivationFunctionType

D = 2048
L = 2048
LC = 256
TT = L + LC
DEPTH = 2
NCORES = 8
SPC = 2
EPS = 1e-6
IN_W = 9728
Q_OFF, K_OFF, V_OFF, HY_OFF, POOL_OFF, GATE_OFF = 0, 1024, 1280, 1536, 3072, 3584
D_FF = 8192
NPBF = np.dtype(ml_dtypes.bfloat16)


class Buf:
    __slots__ = ("name", "W", "R", "G", "sem", "cnt")

    def __init__(self, name, init_readers=None):
        self.name = name
        self.W = {}
        self.R = dict(init_readers) if init_readers else {}
        self.G = {}
        self.sem = None
        self.cnt = 0


class Op:
    __slots__ = ("eng", "fn", "deps", "dma", "token", "signal", "pos", "sigval", "key", "nofence")


class Prog:
    ENGS = ("pe", "act", "dve", "pool", "sp")

    def __init__(self, nc):
        self.nc = nc
        self.ops = {e: [] for e in self.ENGS}
        self.sems = {}
        self.dma_sems = []
        self.clock = {e: {} for e in self.ENGS}
        self.last_dma = {}
        self.fence_readers = {}
        self.nsem = 0
        self.scope = None
        self.free_sems = []
        self.final_waits = {}

    def buf(self, name):
        b = Buf(name, self.fence_readers)
        if self.scope is not None:
            self.scope.append(b)
        return b

    def push_scope(self):
        self.scope = []

    def pop_scope(self):
        for b in self.scope:
            if b.sem is not None:
                self.free_sems.append((b.sem, b.cnt))
                if b in self.dma_sems:
                    self.dma_sems.remove(b)
                self.final_waits[id(b.sem)] = (b.sem, b.cnt)
        self.scope = None

    def bufs(self, name, n):
        return [self.buf(f"{name}{i}") for i in range(n)]

    def fence(self):
        fr = {}
        for e in ("pe", "act", "dve", "pool"):
            for op in reversed(self.ops[e]):
                if not op.dma:
                    fr[("e", e)] = op
                    break
        for k, op in self.last_dma.items():
            if getattr(op, "nofence", False):
                continue
            fr[("d", k)] = op
        self.fence_readers = fr

    def _new_sem(self, name):
        s = self.nc.alloc_semaphore(name)
        self.nsem += 1
        return s

    def _add(self, eng, fn, reads, writes, partial, dma_dest, mm_first, mm_last):
        op = Op()
        op.nofence = False
        op.eng = eng
        op.fn = fn
        op.dma = dma_dest is not None
        op.signal = False
        op.sigval = None
        op.token = None
        deps = {}

        def add_deps(d):
            for k, a in d.items():
                old = deps.get(k)
                if old is None or self._later(a, old):
                    deps[k] = a

        if op.dma:
            b = dma_dest
            if b.sem is None:
                if self.free_sems:
                    b.sem, b.cnt = self.free_sems.pop()
                else:
                    b.sem = self._new_sem("d" + str(self.nsem))
                self.dma_sems.append(b)
            b.cnt += 16
            op.token = (b.sem, b.cnt)
            op.key = ("d", id(b.sem))
        else:
            op.key = ("e", eng)
        for b in reads:
            add_deps(b.W)
        for b in writes:
            if mm_first is False:
                pass
            else:
                add_deps(b.W)
                add_deps(b.R)
        for b in partial:
            if b.R:
                b.G = dict(b.R)
            add_deps(b.G)
        for b in reads:
            b.R[op.key] = op
        for b in writes:
            if mm_last is False:
                if mm_first:
                    b.W = {}
                    b.R = {}
                continue
            b.W = {op.key: op}
            b.R = {}
            b.G = {}
        for b in partial:
            if b.R:
                b.W = {op.key: op}
                b.R = {}
            else:
                b.W[op.key] = op
        clk = self.clock[eng]
        final = []
        op.pos = len(self.ops[eng])
        for k, a in deps.items():
            if a is op:
                continue
            if a.dma:
                sem, val = a.token
                if clk.get(k, 0) >= val:
                    continue
                clk[k] = val
                final.append(a)
            else:
                if a.eng == "pe" and eng == "pe":
                    continue
                if clk.get(k, -1) >= a.pos:
                    continue
                clk[k] = a.pos
                a.signal = True
                final.append(a)
        op.deps = final
        self.ops[eng].append(op)
        if op.dma:
            self.last_dma[id(op.token[0])] = op
        return op

    @staticmethod
    def _later(a, b):
        if a.dma:
            return a.token[1] > b.token[1]
        return a.pos > b.pos

    def op(self, eng, fn, reads=(), writes=(), partial=()):
        return self._add(eng, fn, reads, writes, partial, None, None, None)

    def mm(self, fn, reads, out, first, last):
        return self._add("pe", fn, reads, (out,), (), None, first, last)

    def dma(self, eng, out_ap, in_ap, dst, reads=(), partial=True, **kw):
        fn = lambda e: e.dma_start(out=out_ap, in_=in_ap, **kw)
        if partial:
            return self._add(eng, fn, reads, (), (dst,), dst, None, None)
        return self._add(eng, fn, reads, (dst,), (), dst, None, None)

    def emit(self):
        nc = self.nc
        esem = {e: self._new_sem("e_" + e) for e in ("pe", "act", "dve", "pool")}
        for e in ("pe", "act", "dve", "pool"):
            n = 0
            for op in self.ops[e]:
                if not op.dma and op.signal:
                    n += 1
                    op.sigval = n
        handles = {"pe": "tensor", "act": "scalar", "dve": "vector", "pool": "gpsimd", "sp": "sync"}
        stats = {}
        with nc.Block() as block:
            for e in self.ENGS:
                ops = self.ops[e]
                if not ops and e != "sp":
                    continue

                def body(h, ops=ops, e=e):
                    nw = 0
                    for op in ops:
                        for a in op.deps:
                            if a.dma:
                                h.wait_ge(a.token[0], a.token[1])
                            else:
                                h.wait_ge(esem[a.eng], a.sigval)
                            nw += 1
                        ins = op.fn(h)
                        if op.dma:
                            ins.then_inc(op.token[0], 16)
                        elif op.signal:
                            ins.then_inc(esem[e], 1)
                    if e == "sp":
                        fw = dict(self.final_waits)
                        for b in self.dma_sems:
                            fw[id(b.sem)] = (b.sem, b.cnt)
                        for (sm, cnt) in fw.values():
                            h.wait_ge(sm, cnt)
                    stats[e] = (len(ops), nw)

                getattr(block, handles[e])(body)
        self.stats = stats


def _bf(a):
    return np.ascontiguousarray(a.astype(NPBF))


_CONST_CACHE = {}


def _dft_consts(n_tok):
    n = 2 * n_tok
    t = np.arange(n_tok, dtype=np.float64)[:, None]
    f = np.arange(n_tok, dtype=np.float64)[None, :]
    ang = 2.0 * np.pi * (f + 0.5) * t / n
    Fm = np.concatenate([np.cos(ang), np.sin(ang)], axis=1)
    return _bf(Fm), _bf(Fm.T)


def _filter_consts(n_tok):
    t = np.linspace(0.0, 1.0, n_tok, dtype=np.float32)[:, None]
    w = (2.0 * math.pi * np.arange(n_tok, dtype=np.float32)[:, None] / n_tok).astype(np.float32)
    bands = np.linspace(1e-4, 15, 16, dtype=np.float32)[None, :]
    z = np.concatenate([t, np.cos(bands * w), -np.sin(bands * w)], axis=-1).astype(np.float32)
    deltas = np.linspace(math.log(1e-2) / 1.5, math.log(1e-2) / 0.3, 512, dtype=np.float32)
    decay = np.exp(-t * np.abs(deltas)[None, :]).astype(np.float32)
    zs = np.zeros_like(z)
    zs[1:] = z[:-1]
    ds = np.zeros_like(decay)
    ds[1:] = decay[:-1]
    zz = np.stack([z.T, zs.T], 0)
    dd = np.stack([decay, ds], 0)
    return np.ascontiguousarray(zz), np.ascontiguousarray(dd)


def _pool_inv(n_tok):
    t = np.arange(n_tok)
    out = np.zeros((4, n_tok), np.float32)
    for g, win in enumerate((2, 4, 8, 16)):
        a = np.clip(t - win // 2, 0, n_tok)
        b = np.clip(t + win // 2, 0, n_tok)
        out[g] = 1.0 / (b - a).astype(np.float32)
    return np.ascontiguousarray(np.broadcast_to(out[None], (128, 4, n_tok)))


def _rope_tabs():
    rows = L // 64
    row = np.repeat(np.arange(rows, dtype=np.float32), 64)
    col = np.tile(np.arange(64, dtype=np.float32), rows)
    inv = (10000.0 ** (-np.arange(16, dtype=np.float32) / 16)).astype(np.float32)
    ar = row[:, None] * inv
    ac = col[:, None] * inv
    C = np.zeros((64, L), np.float32)
    S = np.zeros((64, L), np.float32)
    for ax, a in enumerate((ar, ac)):
        c = np.cos(a).T
        s = np.sin(a).T
        C[ax * 32:ax * 32 + 16] = c
        C[ax * 32 + 16:ax * 32 + 32] = c
        S[ax * 32:ax * 32 + 16] = -s
        S[ax * 32 + 16:ax * 32 + 32] = s
    C = np.concatenate([C, C], 0)
    S = np.concatenate([S, S], 0)
    return np.ascontiguousarray(C), np.ascontiguousarray(S)


def _misc_mats():
    ident = np.eye(128, dtype=np.float32)
    ones = np.ones((128, 128), np.float32)
    blk = np.zeros((128, 128), np.float32)
    blk[:64, :64] = 1
    blk[64:, 64:] = 1
    perm = np.zeros((128, 128), np.float32)
    for p in range(128):
        r = p % 32
        q = p - r + (r + 16) % 32
        perm[q, p] = 1.0
    ki = np.arange(128)[:, None]
    qi = np.arange(128)[None, :]
    maskA = np.where(qi > ki, -30000.0, 0.0).astype(np.float32)
    maskB = np.where(ki > qi, -30000.0, 0.0).astype(np.float32)
    m = np.stack([ident, ones, blk, perm, maskA, maskB], 0)
    return np.ascontiguousarray(m.transpose(1, 0, 2))


def _consts():
    if "c" in _CONST_CACHE:
        return _CONST_CACHE["c"]
    c = {}
    c["F_l"], c["FT_l"] = _dft_consts(L)
    c["F_c"], c["FT_c"] = _dft_consts(LC)
    c["zz_l"], c["dd_l"] = _filter_consts(L)
    c["zz_c"], c["dd_c"] = _filter_consts(LC)
    c["pinv_l"] = _pool_inv(L)
    c["pinv_c"] = _pool_inv(LC)
    c["ropeC"], c["ropeS"] = _rope_tabs()
    c["mats"] = _misc_mats()
    _CONST_CACHE["c"] = c
    return c


class PP:
    def __init__(self):
        self.cols = []
        self.off = {}
        self.n = 0

    def add(self, name, arr):
        arr = np.asarray(arr, np.float32).reshape(128, -1)
        self.off[name] = (self.n, arr.shape[1])
        self.cols.append(arr)
        self.n += arr.shape[1]

    def build(self):
        return np.ascontiguousarray(np.concatenate(self.cols, axis=1))


def _chunked(v, nch):
    return np.asarray(v, np.float32).reshape(nch, 128).T


def _pp_layout(inp, core):
    pp = PP()
    b0 = core * SPC
    cv = np.stack([inp["c"][b0], inp["c"][b0 + 1], inp["c_ctx"]], 0)
    pp.add("cvec", cv.reshape(3, 16, 128).transpose(2, 1, 0).reshape(128, 48))
    for l in range(DEPTH):
        pp.add(f"n1g{l}", _chunked(inp["norm1_g"][l], 16))
        pp.add(f"n2g{l}", _chunked(inp["norm2_g"][l], 16))
        pp.add(f"bmod{l}", _chunked(inp["b_mod"][l], 96))
        pp.add(f"qg{l}", np.tile(inp["q_norm_g"][l], 2).reshape(128, 1))
        pp.add(f"kg{l}", np.tile(inp["k_norm_g"][l], 2).reshape(128, 1))
        pp.add(f"hcw{l}", inp["hy_conv_w"][l].reshape(3, 12, 128).transpose(2, 1, 0).reshape(128, 36))
        pp.add(f"hcb{l}", _chunked(inp["hy_conv_b"][l], 12))
        pp.add(f"psc{l}", _chunked(inp["pool_scale"][l], 4))
        fb = np.zeros((128, 4), np.float32)
        fb[:64, 0] = inp["filt_b0"][l]
        fb[:64, 1] = inp["filt_b1"][l][0]
        fb[:64, 2] = inp["filt_b1"][l][1]
        fb[:64, 3] = inp["filt_freq"][l]
        pp.add(f"fb{l}", fb)
    return pp


def _bc_layout(inp):
    a = np.concatenate([inp["sink"].reshape(-1), inp["hy_bias"].reshape(-1)]).astype(np.float32)
    return np.ascontiguousarray(np.broadcast_to(a[None], (128, a.size)))


class Ctx:
    pass


def _stub(*a, **k):
    return None


phase_filters = phase_att = phase_hy = phase_pool = phase_B = _stub


def build_program(pp_off, npp, nbc, stop_after=None, debug=False):
    nc = bass.Bass("TRN2", target_bir_lowering=False)
    P = Prog(nc)
    K = Ctx()
    K.nc, K.P, K.pp_off = nc, P, pp_off
    okind = "ExternalOutput" if debug else "Internal"

    def din(name, shape, dt=F32):
        return nc.dram_tensor(name, list(shape), dt, kind="ExternalInput").ap()

    def dscr(name, shape, dt, dbg=True):
        return nc.dram_tensor(name, list(shape), dt, kind=(okind if dbg else "Internal")).ap()

    K.xTin = din("xTin", [SPC, D, TT])
    K.pp = din("pp", [128, npp])
    K.bc = din("bc", [128, nbc])
    K.w_mod = din("w_mod", [DEPTH, D, 6 * D])
    K.w_in = din("w_in", [DEPTH, D, IN_W])
    K.filt_w0 = din("filt_w0", [DEPTH, 33, 64])
    K.filt_w1 = din("filt_w1", [DEPTH, 2, 64, 64])
    K.filt_w2 = din("filt_w2", [DEPTH, 64, 2048])
    K.pool_w = din("pool_w", [DEPTH, 4, 128, 128])
    K.w_att_o = din("w_att_o", [DEPTH, 1024, D])
    K.w_hy_o = din("w_hy_o", [DEPTH, 512, D])
    K.w_pool_o = din("w_pool_o", [DEPTH, 512, D])
    K.w_out = din("w_out", [DEPTH, D, D])
    K.mlp_w1 = din("mlp_w1", [DEPTH, D, D_FF])
    K.mlp_w2 = din("mlp_w2", [DEPTH, D_FF, D])
    K.F_l = din("F_l", [L, 2 * L], BF16)
    K.FT_l = din("FT_l", [2 * L, L], BF16)
    K.F_c = din("F_c", [LC, 2 * LC], BF16)
    K.FT_c = din("FT_c", [2 * LC, LC], BF16)
    K.zz_l = din("zz_l", [2, 33, L])
    K.dd_l = din("dd_l", [2, L, 512])
    K.zz_c = din("zz_c", [2, 33, LC])
    K.dd_c = din("dd_c", [2, LC, 512])
    K.pinv_l = din("pinv_l", [128, 4, L])
    K.pinv_c = din("pinv_c", [128, 4, LC])
    K.ropeC = din("ropeC", [128, L])
    K.ropeS = din("ropeS", [128, L])
    K.mats = din("mats", [128, 6, 128])
    K.out = nc.dram_tensor("outT", [SPC, D, L], F32, kind="ExternalOutput").ap()
    K.xT = [[K.xTin[s] for s in range(SPC)], [dscr(f"xT1_{s}", [D, TT], F32) for s in range(SPC)]]
    K.zq = [dscr(f"zq{s}", [1024, TT], BF16) for s in range(SPC)]
    K.zk = [dscr(f"zk{s}", [256, TT], BF16) for s in range(SPC)]
    K.zv = [dscr(f"zv{s}", [TT, 4 * 65], BF16) for s in range(SPC)]
    K.zhy = [dscr(f"zhy{s}", [1536, TT], BF16) for s in range(SPC)]
    K.zpl = [dscr(f"zpl{s}", [512, TT], BF16) for s in range(SPC)]
    K.ycat = [dscr(f"ycat{s}", [D, TT], BF16) for s in range(SPC)]
    K.Kf_l = dscr("Kf_l", [2, 2, L, 512], F32)
    K.Kf_c = dscr("Kf_c", [2, 2, LC, 512], F32)
    K.moddbg = dscr("moddbg", [128, DEPTH * 3 * 96], F32)
    WSH = {"w_in": (D, IN_W), "w_att_o": (1024, D), "w_hy_o": (512, D), "w_pool_o": (512, D), "w_out": (D, D),
           "mlp_w1": (D, D_FF), "mlp_w2": (D_FF, D)}
    K.wb = {n: [nc.dram_tensor(f"wb_{n}{l}", list(sh), BF16, kind="Internal").ap() for l in range(DEPTH)]
            for n, sh in WSH.items()}
    K.b_wb = {n: [P.buf(f"wb_{n}{l}") for l in range(DEPTH)] for n in WSH}
    K.WSH = WSH
    K.b_xT = [[P.buf(f"xT{a}{s}") for s in range(SPC)] for a in range(2)]
    K.b_z = [{n: P.buf(n + str(s)) for n in ("zq", "zk", "zv", "zhy", "zpl")} for s in range(SPC)]
    K.b_ycat = [{n: P.buf("y" + n + str(s)) for n in ("att", "hy", "pool")} for s in range(SPC)]
    K.b_Kf = {"l": P.buf("Kf_l"), "c": P.buf("Kf_c")}
    K.b_out = P.buf("out")
    K.b_dbg = P.buf("dbg")
    K.b_in = P.buf("inputs")

    with ExitStack() as es:
        K.ps = [es.enter_context(nc.psum_tensor(f"ps{i}", [128, 512], F32)) for i in range(8)]
        K.b_ps = P.bufs("ps", 8)
        K.psi = 0
        K.mats_f = es.enter_context(nc.sbuf_tensor("mats_f", [128, 6, 128], F32))
        K.mats_b = es.enter_context(nc.sbuf_tensor("mats_b", [128, 6, 128], BF16))
        K.ppt = es.enter_context(nc.sbuf_tensor("ppt", [128, npp], F32))
        K.bct = es.enter_context(nc.sbuf_tensor("bct", [128, nbc], F32))
        K.mod = es.enter_context(nc.sbuf_tensor("mod", [128, DEPTH * 3 * 96], F32))
        K.modA = es.enter_context(nc.sbuf_tensor("modA", [128, DEPTH * 3 * 2 * 16], F32))
        K.b_const = P.buf("const")
        K.b_mod = P.buf("mod")
        K.b_modA = P.buf("modA")
        K.cst = es.enter_context(nc.sbuf_tensor("cst", [128, 4], F32))
        P.op("dve", lambda e: e.memset(K.cst[:, 0:1], EPS), partial=[K.b_const])
        P.op("dve", lambda e: e.memset(K.cst[:, 1:2], -math.pi), partial=[K.b_const])
        P.op("dve", lambda e: e.memset(K.cst[:, 2:3], 0.0), partial=[K.b_const])
        P.dma("sp", K.mats_f[:], K.mats, K.b_const)
        P.dma("pool", K.mats_b[:], K.mats, K.b_const)
        P.dma("sp", K.ppt[:], K.pp, K.b_const)
        P.dma("sp", K.bct[:], K.bc, K.b_const)

        phases = [phase_precast, phase_mod]
        for l in range(DEPTH):
            phases.append(lambda K, l=l: phase_filters(K, l))
            for s in range(SPC):
                phases.append(lambda K, l=l, s=s: phase_A(K, l, s))
                phases.append(lambda K, l=l, s=s: phase_att(K, l, s))
                phases.append(lambda K, l=l, s=s: phase_hy(K, l, s))
                phases.append(lambda K, l=l, s=s: phase_pool(K, l, s))
                phases.append(lambda K, l=l, s=s: phase_B(K, l, s))
        import os as _os
        only = _os.environ.get("PH_ONLY")
        only = [int(v) for v in only.split(",")] if only else None
        K.cut = int(_os.environ.get("KB_CUT", "0"))
        for i, ph in enumerate(phases):
            if stop_after is not None and i >= stop_after:
                break
            if only is not None and i not in only:
                continue
            P.push_scope()
            with ExitStack() as pes:
                K.es = pes
                ph(K)
            P.fence()
            P.pop_scope()
        P.emit()
    return nc, P


def ppc(K, name, i=0, n=1, rows=128):
    o, w = K.pp_off[name]
    return K.ppt[0:rows, o + i:o + i + n]


def nextps(K):
    i = K.psi
    K.psi = (K.psi + 1) % 8
    return K.ps[i], K.b_ps[i]


def sb(K, name, shape, dt):
    K.uid = getattr(K, "uid", 0) + 1
    return K.es.enter_context(K.nc.sbuf_tensor(f"{name}_u{K.uid}", list(shape), dt))


def phase_precast(K):
    P = K.P
    for l in range(DEPTH):
        for n in ("w_in", "w_att_o", "w_hy_o", "w_pool_o", "w_out", "mlp_w1", "mlp_w2"):
            rows, cols = K.WSH[n]
            src = getattr(K, n)[l]
            RB = 128
            for r0 in range(0, rows, RB):
                o = P.dma("pool", K.wb[n][l][r0:r0 + RB, :], src[r0:r0 + RB, :], K.b_wb[n][l], reads=[K.b_in])
                o.nofence = True


def phase_transpose_in(K):
    P = K.P
    ident = K.mats_f[:, 0, :]
    NS = 2
    xin = [sb(K, f"t0_in{i}", [128, 4, D], F32) for i in range(NS)]
    b_xin = P.bufs("t0_in", NS)
    xo = [sb(K, f"t0_out{i}", [128, 16, 512], F32) for i in range(NS)]
    b_xo = P.bufs("t0_out", NS)
    it = 0
    for s in range(SPC):
        for (src, n_tok, tok0) in ((K.x[s], L, 0), (K.ctx[s], LC, L)):
            for t0 in range(0, n_tok, 512):
                T = min(512, n_tok - t0)
                nsub = T // 128
                i = it % NS
                it += 1
                P.dma("sp", xin[i][:, 0:nsub, :], src[t0:t0 + T, :].rearrange("(a p) d -> p a d", p=128),
                      b_xin[i], reads=[K.b_in], partial=False)
                for c in range(16):
                    ps, bps = nextps(K)
                    for a in range(nsub):
                        P.mm(lambda e, ps=ps, a=a, c=c, i=i: e.transpose(
                            ps[:, a * 128:(a + 1) * 128], xin[i][:, a, c * 128:(c + 1) * 128], ident),
                            [b_xin[i], K.b_const], bps, a == 0, a == nsub - 1)
                    eng = "dve" if c % 2 == 0 else "act"
                    if eng == "dve":
                        P.op("dve", lambda e, ps=ps, c=c, i=i, T=T: e.tensor_copy(xo[i][:, c, 0:T], ps[:, 0:T]),
                             reads=[bps], partial=[b_xo[i]])
                    else:
                        P.op("act", lambda e, ps=ps, c=c, i=i, T=T: e.copy(xo[i][:, c, 0:T], ps[:, 0:T]),
                             reads=[bps], partial=[b_xo[i]])
                P.dma("sp", K.xT[0][s].rearrange("(c p) t -> p c t", p=128)[:, :, tok0 + t0:tok0 + t0 + T],
                      xo[i][:, :, 0:T], K.b_xT[0][s], reads=[b_xo[i]])


def phase_mod(K):
    P = K.P
    nc = K.nc
    sc = sb(K, "pm_sc", [128, 48], F32)
    b_sc = P.buf("pm_sc")
    P.op("act", lambda e: e.activation(sc[:], ppc(K, "cvec", 0, 48), AF.Silu), reads=[K.b_const], writes=[b_sc])
    NS = 2
    wt = [sb(K, f"pm_w{i}", [128, 16, 512], F32) for i in range(NS)]
    b_wt = P.bufs("pm_w", NS)
    it = 0
    for l in range(DEPTH):
        for nb in range(24):
            i = it % NS
            it += 1
            P.dma("sp", wt[i][:], K.w_mod[l].rearrange("(k p) n -> p k n", p=128)[:, :, nb * 512:(nb + 1) * 512],
                  b_wt[i], reads=[K.b_in], partial=False)
            ps, bps = nextps(K)
            for j in range(4):
                for k in range(16):
                    P.mm(lambda e, ps=ps, i=i, j=j, k=k: e.matmul(
                        ps[:, j * 4:j * 4 + 3], wt[i][:, k, j * 128:(j + 1) * 128], sc[:, k * 3:k * 3 + 3],
                        start=(k == 0), stop=(k == 15)),
                        [b_wt[i], b_sc], bps, j == 0 and k == 0, j == 3 and k == 15)
            for j in range(4):
                n = nb * 4 + j
                for who in range(3):
                    col = (l * 3 + who) * 96 + n
                    P.op("dve", lambda e, ps=ps, j=j, who=who, col=col, l=l, n=n: e.tensor_tensor(
                        K.mod[:, col:col + 1], ps[:, j * 4 + who:j * 4 + who + 1], ppc(K, f"bmod{l}", n), ALU.add),
                        reads=[bps, K.b_const], partial=[K.b_mod])
    for l in range(DEPTH):
        for who in range(3):
            base = (l * 3 + who) * 96
            for which, (gname, scoff) in enumerate(((f"n1g{l}", 16), (f"n2g{l}", 64))):
                o = ((l * 3 + who) * 2 + which) * 16
                P.op("dve", lambda e, base=base, scoff=scoff, gname=gname, o=o: e.scalar_tensor_tensor(
                    K.modA[:, o:o + 16], K.mod[:, base + scoff:base + scoff + 16], 1.0, ppc(K, gname, 0, 16),
                    ALU.add, ALU.mult), reads=[K.b_mod, K.b_const], partial=[K.b_modA])
    P.dma("sp", K.moddbg, K.mod[:], K.b_dbg, reads=[K.b_mod])


def modcol(K, l, who, part, c):
    col = (l * 3 + who) * 96 + part * 16 + c
    return K.mod[:, col:col + 1]


def modAcol(K, l, who, which, c):
    o = ((l * 3 + who) * 2 + which) * 16 + c
    return K.modA[:, o:o + 1]


def wring(K, n=3):
    K.wr = [sb(K, f"wr{i}", [128, 8192], BF16) for i in range(n)]
    K.b_wr = K.P.bufs("wr", n)
    K.wri = 0


def wload(K, src_ap, kc, ncol, dep):
    i = K.wri
    K.wri = (K.wri + 1) % len(K.wr)
    view = K.wr[i][:, 0:kc * ncol].rearrange("p (k n) -> p k n", k=kc)
    K.P.dma("sp", view, src_ap, K.b_wr[i], reads=[dep], partial=False)
    return view, K.b_wr[i]


def rsqrt_from_ps(K, out, b_out, ps, bps, scale, rows=128):
    P = K.P
    P.op("act", lambda e: e.activation(out, ps, AF.Sqrt, bias=K.cst[0:rows, 0:1], scale=scale),
         reads=[bps, K.b_const], writes=[b_out])
    P.op("dve", lambda e: e.reciprocal(out, out), reads=[b_out], writes=[b_out])


def make_h(K, xt, b_xt, h, b_h, T, l, who, which, tmp):
    P = K.P
    ones_b = K.mats_b[:, 1, :]
    sq, b_sq, rs, b_rs, t32, b_t32 = tmp["sq"], tmp["b_sq"], tmp["rs"], tmp["b_rs"], tmp["t32"], tmp["b_t32"]
    P.op("act", lambda e: e.activation(sq[:, :, 0:T], xt[:, :, 0:T], AF.Square), reads=[b_xt], writes=[b_sq])
    ps, bps = nextps(K)
    for c in range(16):
        P.mm(lambda e, c=c: e.matmul(ps[:, 0:T], ones_b, sq[:, c, 0:T], start=(c == 0), stop=(c == 15)),
             [b_sq, K.b_const], bps, c == 0, c == 15)
    rsqrt_from_ps(K, rs[:, 0:T], b_rs, ps[:, 0:T], bps, 1.0 / D)
    sh_part = 0 if which == 0 else 3
    for c in range(16):
        j = c % 2
        P.op("dve", lambda e, c=c, j=j: e.scalar_tensor_tensor(
            t32[j][:, 0:T], xt[:, c, 0:T], modAcol(K, l, who, which, c), rs[:, 0:T], ALU.mult, ALU.mult),
            reads=[b_xt, b_rs, K.b_modA], writes=[b_t32[j]])
        P.op("act", lambda e, c=c, j=j: e.activation(
            h[:, c, 0:T], t32[j][:, 0:T], AF.Identity, bias=modcol(K, l, who, sh_part, c), scale=1.0),
            reads=[b_t32[j], K.b_mod], partial=[b_h])


def h_tmp(K, pfx):
    P = K.P
    return dict(sq=sb(K, pfx + "sq", [128, 16, 512], BF16), b_sq=P.buf(pfx + "sq"),
                rs=sb(K, pfx + "rs", [128, 512], F32), b_rs=P.buf(pfx + "rs"),
                t32=[sb(K, pfx + f"t32{j}", [128, 512], F32) for j in range(2)], b_t32=P.bufs(pfx + "t32", 2))


def tiles_of(l, with_ctx=True):
    t = [(t0, 512) for t0 in range(0, L, 512)]
    if with_ctx:
        t.append((L, LC))
    return t


def phase_A(K, l, s):
    P = K.P
    wring(K, 3)
    tmp = h_tmp(K, "pa_")
    xt = [sb(K, f"pa_x{i}", [128, 16, 512], F32) for i in range(2)]
    b_xt = P.bufs("pa_x", 2)
    h = [sb(K, f"pa_h{i}", [128, 16, 512], BF16) for i in range(2)]
    b_h = P.bufs("pa_h", 2)
    st = [sb(K, f"pa_st{i}", [128, 4, 512], BF16) for i in range(2)]
    b_st = P.bufs("pa_st", 2)
    vst = [sb(K, f"pa_vst{i}", [128, 4, 65], BF16) for i in range(2)]
    b_vst = P.bufs("pa_vst", 2)
    for i in range(2):
        P.op("dve", lambda e, i=i: e.memset(vst[i][:], 1.0), writes=[b_vst[i]])
    xsrc = K.xT[l % 2][s].rearrange("(c p) t -> p c t", p=128)
    win = K.wb["w_in"][l].rearrange("(k p) n -> p k n", p=128)
    b_win = K.b_wb["w_in"][l]
    sti = 0
    vsi = 0
    for ti, (t0, T) in enumerate(tiles_of(l)):
        is_ctx = t0 >= L
        who = 2 if is_ctx else s
        i = ti % 2
        P.dma("act", xt[i][:, :, 0:T], xsrc[:, :, t0:t0 + T], b_xt[i], reads=[K.b_xT[l % 2][s]], partial=False)
        make_h(K, xt[i], b_xt[i], h[i], b_h[i], T, l, who, 0, tmp)
        blocks = [(0, "zq", 0), (512, "zq", 512), (1024, "kv", 0), (1536, "zhy", 0), (2048, "zhy", 512),
                  (2560, "zhy", 1024), (3072, "zpl", 0)]
        for (col0, dst, r0) in blocks:
            if is_ctx and l == DEPTH - 1 and dst != "kv":
                continue
            wv, b_w = wload(K, win[:, :, col0:col0 + 512], 16, 512, b_win)
            nj = 2 if dst == "kv" else 4
            si = sti % 2
            sti += 1
            for j in range(nj):
                ps, bps = nextps(K)
                for k in range(16):
                    P.mm(lambda e, ps=ps, wv=wv, j=j, k=k, i=i, T=T: e.matmul(
                        ps[:, 0:T], wv[:, k, j * 128:(j + 1) * 128], h[i][:, k, 0:T], start=(k == 0), stop=(k == 15)),
                        [b_w, b_h[i]], bps, k == 0, k == 15)
                P.op("act", lambda e, ps=ps, si=si, j=j, T=T: e.copy(st[si][:, j, 0:T], ps[:, 0:T]),
                     reads=[bps], partial=[b_st[si]])
            dname = "zk" if dst == "kv" else dst
            dten = getattr(K, dname)[s].rearrange("(c p) t -> p c t", p=128)
            c0 = r0 // 128
            P.dma("act", dten[:, c0:c0 + nj, t0:t0 + T], st[si][:, 0:nj, 0:T], K.b_z[s][dname], reads=[b_st[si]])
            if dst == "kv":
                for a in range(T // 128):
                    ps, bps = nextps(K)
                    for k in range(16):
                        P.mm(lambda e, ps=ps, wv=wv, a=a, k=k, i=i: e.matmul(
                            ps[:, 0:256], h[i][:, k, a * 128:(a + 1) * 128], wv[:, k, 256:512],
                            start=(k == 0), stop=(k == 15)), [b_w, b_h[i]], bps, k == 0, k == 15)
                    vi = vsi % 2
                    vsi += 1
                    P.op("dve", lambda e, ps=ps, vi=vi: e.tensor_copy(
                        vst[vi][:, :, 0:64], ps[:, 0:256].rearrange("p (g d) -> p g d", g=4)),
                        reads=[bps], partial=[b_vst[vi]])
                    P.dma("act", K.zv[s][t0 + a * 128:t0 + (a + 1) * 128, :].rearrange("p (g d) -> p g d", g=4),
                          vst[vi][:], K.b_z[s]["zv"], reads=[b_vst[vi]])


def phase_filters(K, l):
    with ExitStack() as es2:
        K.es = es2
        filters_for(K, l, L, K.zz_l, K.dd_l, K.F_l, K.Kf_l, K.b_Kf["l"], "fl")
    K.P.fence()
    if l == 0:
        with ExitStack() as es2:
            K.es = es2
            filters_for(K, l, LC, K.zz_c, K.dd_c, K.F_c, K.Kf_c, K.b_Kf["c"], "fc")


def filters_for(K, l, n_tok, zz, dd, Fm, Kdst, b_Kdst, pfx):
    P = K.P
    nT = n_tok // 128
    TW = min(512, n_tok)
    w0 = sb(K, pfx + "w0", [33, 64], F32)
    w1 = sb(K, pfx + "w1", [64, 2, 64], F32)
    w2 = sb(K, pfx + "w2", [64, 2048], F32)
    b_w = P.buf(pfx + "w")
    P.dma("sp", w0[:], K.filt_w0[l], b_w, reads=[K.b_in])
    P.dma("sp", w1[:], K.filt_w1[l].rearrange("i k n -> k i n"), b_w, reads=[K.b_in])
    P.dma("sp", w2[:], K.filt_w2[l], b_w, reads=[K.b_in])
    zt = sb(K, pfx + "zt", [33, n_tok], F32)
    b_zt = P.buf(pfx + "zt")
    hb = sb(K, pfx + "hb", [128, 2, nT, 512], BF16)
    b_hb = P.buf(pfx + "hb")
    AB = sb(K, pfx + "AB", [128, 2, 2, nT, 512], BF16)
    b_AB = P.buf(pfx + "AB")
    hid = [sb(K, pfx + f"hid{i}", [64, TW], F32) for i in range(2)]
    b_hid = P.bufs(pfx + "hid", 2)
    arg = sb(K, pfx + "arg", [64, TW], F32)
    b_arg = P.buf(pfx + "arg")
    argi = sb(K, pfx + "argi", [64, TW], mybir.dt.int32)
    b_argi = P.buf(pfx + "argi")
    arg2 = sb(K, pfx + "arg2", [64, TW], F32)
    b_arg2 = P.buf(pfx + "arg2")
    dec = [sb(K, pfx + f"dec{i}", [128, 512], F32) for i in range(2)]
    b_dec = P.bufs(pfx + "dec", 2)
    hf = [sb(K, pfx + f"hf{i}", [128, 512], F32) for i in range(2)]
    b_hf = P.bufs(pfx + "hf", 2)
    fbn = f"fb{l}"
    di = 0
    for direction in (1, 0):
        P.dma("sp", zt[:], zz[direction], b_zt, reads=[K.b_in], partial=False)
        for t0 in range(0, n_tok, TW):
            cur_in, b_cur_in, kin = zt[:, t0:t0 + TW], b_zt, 33
            for layer in range(3):
                ps, bps = nextps(K)
                if layer == 0:
                    lhsT = w0[:]
                else:
                    lhsT = w1[:, layer - 1, :]
                P.mm(lambda e, ps=ps, lhsT=lhsT, cur_in=cur_in: e.matmul(ps[0:64, 0:TW], lhsT, cur_in,
                                                                          start=True, stop=True),
                     [b_w, b_cur_in], bps, True, True)
                P.op("dve", lambda e, ps=ps, layer=layer: e.tensor_scalar(
                    arg[:], ps[0:64, 0:TW], ppc(K, fbn, layer, 1, 64), ppc(K, fbn, 3, 1, 64), ALU.add, ALU.mult),
                    reads=[bps, K.b_const], writes=[b_arg])
                P.op("dve", lambda e: e.tensor_single_scalar(argi[:], arg[:], 1.0 / (2.0 * math.pi), ALU.mult),
                     reads=[b_arg], writes=[b_argi])
                P.op("dve", lambda e: e.scalar_tensor_tensor(arg2[:], argi[:], -2.0 * math.pi, arg[:], ALU.mult, ALU.add),
                     reads=[b_arg, b_argi], writes=[b_arg2])
                ho = layer % 2
                P.op("act", lambda e, ho=ho: e.activation(hid[ho][:], arg2[:], AF.Sin, bias=K.cst[0:64, 2:3],
                                                          scale=1.0 - 2e-6),
                     reads=[b_arg2, K.b_const], writes=[b_hid[ho]])
                cur_in, b_cur_in = hid[ho][:], b_hid[ho]
            h3 = cur_in
            for a in range(TW // 128):
                pc = t0 // 128 + a
                d_i = di % 2
                di += 1
                P.dma("sp", dec[d_i][:], dd[direction, pc * 128:(pc + 1) * 128, :], b_dec[d_i], reads=[K.b_in],
                      partial=False)
                for o in range(2):
                    ps, bps = nextps(K)
                    col = o * 1024 + direction * 512
                    P.mm(lambda e, ps=ps, h3=h3, a=a, col=col: e.matmul(
                        ps[:, :], h3[:, a * 128:(a + 1) * 128], w2[:, col:col + 512], start=True, stop=True),
                        [b_w, b_cur_in], bps, True, True)
                    if direction == 1:
                        P.op("dve", lambda e, ps=ps, o=o, pc=pc, d_i=d_i: e.tensor_tensor(
                            hb[:, o, pc, :], ps[:, :], dec[d_i][:], ALU.mult), reads=[bps, b_dec[d_i]], partial=[b_hb])
                    else:
                        P.op("dve", lambda e, ps=ps, o=o, d_i=d_i: e.tensor_tensor(
                            hf[o][:], ps[:, :], dec[d_i][:], ALU.mult), reads=[bps, b_dec[d_i]], writes=[b_hf[o]])
                        P.op("dve", lambda e, o=o, pc=pc: e.tensor_tensor(
                            AB[:, o, 0, pc, :], hf[o][:], hb[:, o, pc, :], ALU.add), reads=[b_hf[o], b_hb],
                            partial=[b_AB])
                        P.op("dve", lambda e, o=o, pc=pc: e.tensor_tensor(
                            AB[:, o, 1, pc, :], hf[o][:], hb[:, o, pc, :], ALU.subtract), reads=[b_hf[o], b_hb],
                            partial=[b_AB])
    M = n_tok
    FW = min(512, M)
    fw = [sb(K, pfx + f"F{i}", [128, nT, FW], BF16) for i in range(3)]
    b_fw = P.bufs(pfx + "F", 3)
    kst = [sb(K, pfx + f"kst{i}", [128, 512], F32) for i in range(2)]
    b_kst = P.bufs(pfx + "kst", 2)
    Fv = Fm.rearrange("(k p) f -> p k f", p=128)
    fi = 0
    ki = 0
    for cs in range(2):
        for f0 in range(0, M, FW):
            i = fi % 3
            fi += 1
            P.dma("sp", fw[i][:], Fv[:, :, cs * M + f0:cs * M + f0 + FW], b_fw[i], reads=[K.b_in], partial=False)
            for fc in range(FW // 128):
                for o in range(2):
                    ps, bps = nextps(K)
                    for k in range(nT):
                        P.mm(lambda e, ps=ps, i=i, k=k, fc=fc, o=o, cs=cs: e.matmul(
                            ps[:, :], fw[i][:, k, fc * 128:(fc + 1) * 128], AB[:, o, cs, k, :],
                            start=(k == 0), stop=(k == nT - 1)), [b_fw[i], b_AB], bps, k == 0, k == nT - 1)
                    kk = ki % 2
                    ki += 1
                    P.op("act", lambda e, ps=ps, kk=kk: e.copy(kst[kk][:], ps[:, :]), reads=[bps], writes=[b_kst[kk]])
                    r0 = f0 + fc * 128
                    P.dma("sp", Kdst[o, cs, r0:r0 + 128, :], kst[kk][:], b_Kdst, reads=[b_kst[kk]])


def phase_hy(K, l, s):
    with ExitStack() as es2:
        K.es = es2
        hyena_for(K, l, s, 0, L, K.F_l, K.FT_l, K.Kf_l, K.b_Kf["l"], "hl")
    if l == 0:
        K.P.fence()
        with ExitStack() as es2:
            K.es = es2
            hyena_for(K, l, s, L, LC, K.F_c, K.FT_c, K.Kf_c, K.b_Kf["c"], "hc")


def hyena_for(K, l, s, tok0, n_tok, Fm, FTm, Kf, b_Kf, pfx):
    P = K.P
    nT = n_tok // 128
    M = n_tok
    nF = M // 128
    ident_b = K.mats_b[:, 0, :]
    tokU = sb(K, pfx + "tokU", [128, nT, 1536], BF16)
    b_tokU = P.buf(pfx + "tokU")
    outer_es = K.es
    pro_es = ExitStack()
    K.es = pro_es
    ub = [sb(K, pfx + f"ub{i}", [128, n_tok + 2], BF16) for i in range(2)]
    b_ub = P.bufs(pfx + "ub", 2)
    for i in range(2):
        P.op("pool", lambda e, i=i: e.memset(ub[i][:, 0:1], 0.0), partial=[b_ub[i]])
        P.op("pool", lambda e, i=i: e.memset(ub[i][:, n_tok + 1:n_tok + 2], 0.0), partial=[b_ub[i]])
    c1 = [sb(K, pfx + f"c1{i}", [128, n_tok], F32) for i in range(2)]
    b_c1 = P.bufs(pfx + "c1", 2)
    uc = [sb(K, pfx + f"uc{i}", [128, n_tok], BF16) for i in range(2)]
    b_uc = P.bufs(pfx + "uc", 2)
    zsrc = K.zhy[s].rearrange("(c p) t -> p c t", p=128)
    hcw, hcb = f"hcw{l}", f"hcb{l}"
    G = min(8, nT)
    for c in range(12):
        i = c % 2
        P.dma("sp", ub[i][:, 1:n_tok + 1], zsrc[:, c, tok0:tok0 + n_tok], b_ub[i], reads=[K.b_z[s]["zhy"]])
        P.op("dve", lambda e, i=i, c=c: e.tensor_scalar(
            c1[0][:], ub[i][:, 1:n_tok + 1], ppc(K, hcw, c * 3 + 1), ppc(K, hcb, c), ALU.mult, ALU.add),
            reads=[b_ub[i], K.b_const], writes=[b_c1[0]])
        P.op("dve", lambda e, i=i, c=c: e.scalar_tensor_tensor(
            c1[1][:], ub[i][:, 0:n_tok], ppc(K, hcw, c * 3 + 0), c1[0][:], ALU.mult, ALU.add),
            reads=[b_ub[i], b_c1[0], K.b_const], writes=[b_c1[1]])
        P.op("dve", lambda e, i=i, c=c: e.scalar_tensor_tensor(
            uc[i][:], ub[i][:, 2:n_tok + 2], ppc(K, hcw, c * 3 + 2), c1[1][:], ALU.mult, ALU.add),
            reads=[b_ub[i], b_c1[1], K.b_const], writes=[b_uc[i]])
        for g0 in range(0, nT, G):
            ps, bps = nextps(K)
            psb = ps[:].bitcast(BF16)
            for a in range(G):
                P.mm(lambda e, psb=psb, a=a, i=i, g0=g0: e.transpose(
                    psb[:, a * 128:(a + 1) * 128], uc[i][:, (g0 + a) * 128:(g0 + a + 1) * 128], ident_b),
                    [b_uc[i], K.b_const], bps, a == 0, a == G - 1)
            P.op("act", lambda e, psb=psb, g0=g0, c=c: e.copy(
                tokU[:, g0:g0 + G, c * 128:(c + 1) * 128], psb[:, 0:G * 128].rearrange("p (a n) -> p a n", a=G)),
                reads=[bps], partial=[b_tokU])
    pro_es.close()
    K.es = outer_es
    P.fence()
    FW = min(256, M)
    fw = [sb(K, pfx + f"F{i}", [128, nT, FW], BF16) for i in range(4)]
    b_fw = P.bufs(pfx + "F", 4)
    TWI = 128
    ftw = [sb(K, pfx + f"FT{i}", [128, 2 * nF, TWI], BF16) for i in range(2)]
    b_ftw = P.bufs(pfx + "FT", 2)
    kt = [sb(K, pfx + f"kt{i}", [128, 2, 512], F32) for i in range(2)]
    b_kt = P.bufs(pfx + "kt", 2)
    tmp = [sb(K, pfx + f"tmp{i}", [128, 512], F32) for i in range(4)]
    b_tmp = P.bufs(pfx + "tmp", 4)
    Pf = sb(K, pfx + "Pf", [128, 2 * nF, 512], BF16)
    b_Pf = P.buf(pfx + "Pf")
    z1 = sb(K, pfx + "z1", [128, nT, 512], BF16)
    b_z1 = P.buf(pfx + "z1")
    z2 = sb(K, pfx + "z2", [128, nT, 512], BF16)
    b_z2 = P.buf(pfx + "z2")
    Fv = Fm.rearrange("(k p) f -> p k f", p=128)
    FTv = FTm.rearrange("(k p) t -> p k t", p=128)
    nbc_sink = 2 * 16
    fi = 0
    ki = 0
    fti = 0
    for o in range(2):
        zin, b_zin = (tokU[:, :, 0:512], b_tokU) if o == 0 else (z1[:], b_z1)
        gate = tokU[:, :, 512 * (o + 1):512 * (o + 2)]
        zout, b_zout = (z1, b_z1) if o == 0 else (z2, b_z2)
        dcol = nbc_sink + (l * 2 + o) * 512
        for f0 in range(0, M, FW):
            ia = fi % 4
            ib = (fi + 1) % 4
            fi += 2
            P.dma("sp", fw[ia][:], Fv[:, :, f0:f0 + FW], b_fw[ia], reads=[K.b_in], partial=False)
            P.dma("sp", fw[ib][:], Fv[:, :, M + f0:M + f0 + FW], b_fw[ib], reads=[K.b_in], partial=False)
            for fc in range(FW // 128):
                fch = (f0 // 128) + fc
                kk = ki % 2
                ki += 1
                P.dma("sp", kt[kk][:], Kf[o, :, fch * 128:(fch + 1) * 128, :].rearrange("c p n -> p c n"),
                      b_kt[kk], reads=[b_Kf], partial=False)
                psc, bpsc = nextps(K)
                pss, bpss = nextps(K)
                for (pp_, bpp_, iw) in ((psc, bpsc, ia), (pss, bpss, ib)):
                    for k in range(nT):
                        P.mm(lambda e, pp_=pp_, iw=iw, k=k, fc=fc, zin=zin: e.matmul(
                            pp_[:, :], fw[iw][:, k, fc * 128:(fc + 1) * 128], zin[:, k, :],
                            start=(k == 0), stop=(k == nT - 1)), [b_fw[iw], b_zin], bpp_, k == 0, k == nT - 1)
                P.op("dve", lambda e, psc=psc, kk=kk: e.tensor_tensor(tmp[0][:], psc[:, :], kt[kk][:, 0, :], ALU.mult),
                     reads=[bpsc, b_kt[kk]], writes=[b_tmp[0]])
                P.op("dve", lambda e, pss=pss, kk=kk: e.tensor_tensor(tmp[1][:], pss[:, :], kt[kk][:, 1, :], ALU.mult),
                     reads=[bpss, b_kt[kk]], writes=[b_tmp[1]])
                P.op("pool", lambda e, fch=fch: e.tensor_tensor(Pf[:, fch, :], tmp[0][:], tmp[1][:], ALU.subtract),
                     reads=[b_tmp[0], b_tmp[1]], partial=[b_Pf])
                P.op("dve", lambda e, psc=psc, kk=kk: e.tensor_tensor(tmp[2][:], psc[:, :], kt[kk][:, 1, :], ALU.mult),
                     reads=[bpsc, b_kt[kk]], writes=[b_tmp[2]])
                P.op("dve", lambda e, pss=pss, kk=kk: e.tensor_tensor(tmp[3][:], pss[:, :], kt[kk][:, 0, :], ALU.mult),
                     reads=[bpss, b_kt[kk]], writes=[b_tmp[3]])
                P.op("pool", lambda e, fch=fch: e.tensor_tensor(Pf[:, nF + fch, :], tmp[2][:], tmp[3][:], ALU.add),
                     reads=[b_tmp[2], b_tmp[3]], partial=[b_Pf])
        for t0 in range(0, n_tok, TWI):
            it = fti % 2
            fti += 1
            P.dma("sp", ftw[it][:], FTv[:, :, t0:t0 + TWI], b_ftw[it], reads=[K.b_in], partial=False)
            for a in range(TWI // 128):
                tch = t0 // 128 + a
                ps, bps = nextps(K)
                for f in range(2 * nF):
                    P.mm(lambda e, ps=ps, it=it, f=f, a=a: e.matmul(
                        ps[:, :], ftw[it][:, f, a * 128:(a + 1) * 128], Pf[:, f, :],
                        start=(f == 0), stop=(f == 2 * nF - 1)), [b_ftw[it], b_Pf], bps, f == 0, f == 2 * nF - 1)
                P.op("pool", lambda e, tch=tch, zin=zin, dcol=dcol: e.tensor_tensor(
                    tmp[0][:], zin[:, tch, :], K.bct[:, dcol:dcol + 512], ALU.mult),
                    reads=[b_zin, K.b_const], writes=[b_tmp[0]])
                P.op("dve", lambda e, ps=ps: e.scalar_tensor_tensor(
                    tmp[1][:], ps[:, :], 1.0 / n_tok, tmp[0][:], ALU.mult, ALU.add),
                    reads=[bps, b_tmp[0]], writes=[b_tmp[1]])
                P.op("pool", lambda e, tch=tch, gate=gate, zout=zout: e.tensor_tensor(
                    zout[:, tch, :], tmp[1][:], gate[:, tch, :], ALU.mult),
                    reads=[b_tmp[1], b_tokU], partial=[b_zout])
    TG = min(4, nT)
    yst = [sb(K, pfx + f"yst{i}", [128, 4, TG * 128], BF16) for i in range(2)]
    b_yst = P.bufs(pfx + "yst", 2)
    ydst = K.ycat[s].rearrange("(c p) t -> p c t", p=128)
    yi = 0
    for g0 in range(0, nT, TG):
        i = yi % 2
        yi += 1
        for c4 in range(4):
            ps, bps = nextps(K)
            psb = ps[:].bitcast(BF16)
            for a in range(TG):
                P.mm(lambda e, psb=psb, a=a, g0=g0, c4=c4: e.transpose(
                    psb[:, a * 128:(a + 1) * 128], z2[:, g0 + a, c4 * 128:(c4 + 1) * 128], ident_b),
                    [b_z2, K.b_const], bps, a == 0, a == TG - 1)
            P.op("act", lambda e, psb=psb, i=i, c4=c4: e.copy(yst[i][:, c4, :], psb[:, 0:TG * 128]),
                 reads=[bps], partial=[b_yst[i]])
        P.dma("sp", ydst[:, 8:12, tok0 + g0 * 128:tok0 + (g0 + TG) * 128], yst[i][:], K.b_ycat[s]["hy"],
              reads=[b_yst[i]])


def phase_pool(K, l, s):
    with ExitStack() as es2:
        K.es = es2
        pool_for(K, l, s, 0, L, K.pinv_l, "pl")
    if l == 0:
        K.P.fence()
        with ExitStack() as es2:
            K.es = es2
            pool_for(K, l, s, L, LC, K.pinv_c, "pc")


def pool_for(K, l, s, tok0, n_tok, pinv, pfx):
    P = K.P
    PADW = n_tok + 32
    EW = n_tok + 16
    u = sb(K, pfx + "u", [128, PADW], BF16)
    b_u = P.buf(pfx + "u")
    inv = sb(K, pfx + "inv", [128, n_tok], F32)
    b_inv = P.buf(pfx + "inv")
    wa = sb(K, pfx + "wa", [128, EW], F32)
    wb = sb(K, pfx + "wb", [128, EW], F32)
    b_wa, b_wb = P.buf(pfx + "wa"), P.buf(pfx + "wb")
    diff = sb(K, pfx + "diff", [128, n_tok], BF16)
    b_diff = P.buf(pfx + "diff")
    pw = sb(K, pfx + "pw", [128, 4, 128], BF16)
    b_pw = P.buf(pfx + "pw")
    yst = [sb(K, pfx + f"yst{i}", [128, 512], BF16) for i in range(2)]
    b_yst = P.bufs(pfx + "yst", 2)
    P.dma("pool", pw[:], K.pool_w[l].rearrange("g c d -> c g d"), b_pw, reads=[K.b_in])
    P.op("dve", lambda e: e.memset(u[:, 0:16], 0.0), partial=[b_u])
    P.op("dve", lambda e: e.memset(u[:, 16 + n_tok:PADW], 0.0), partial=[b_u])
    zsrc = K.zpl[s].rearrange("(c p) t -> p c t", p=128)
    ydst = K.ycat[s].rearrange("(c p) t -> p c t", p=128)
    yi = 0
    for g in range(4):
        P.dma("sp", u[:, 16:16 + n_tok], zsrc[:, g, tok0:tok0 + n_tok], b_u, reads=[K.b_z[s]["zpl"]])
        P.dma("sp", inv[:], pinv[:, g, :], b_inv, reads=[K.b_in], partial=False)
        P.op("dve", lambda e: e.tensor_tensor(wa[:], u[:, 7:7 + EW], u[:, 8:8 + EW], ALU.add),
             reads=[b_u], writes=[b_wa])
        cur, b_cur, oth, b_oth = wa, b_wa, wb, b_wb
        step = 1
        for lvl in range(g):
            lo = (1, 3, 7)[lvl]
            P.op("dve", lambda e, cur=cur, oth=oth, lo=lo, step=step: e.tensor_tensor(
                oth[:, lo:EW - lo], cur[:, lo - step:EW - lo - step], cur[:, lo + step:EW - lo + step], ALU.add),
                reads=[b_cur], writes=[b_oth])
            cur, b_cur, oth, b_oth = oth, b_oth, cur, b_cur
            step *= 2
        P.op("dve", lambda e, cur=cur, oth=oth: e.tensor_tensor(oth[:, 0:n_tok], cur[:, 8:8 + n_tok], inv[:], ALU.mult),
             reads=[b_cur, b_inv], writes=[b_oth])
        P.op("dve", lambda e, oth=oth: e.tensor_tensor(diff[:], oth[:, 0:n_tok], u[:, 16:16 + n_tok], ALU.subtract),
             reads=[b_oth, b_u], writes=[b_diff])
        for t0 in range(0, n_tok, 512):
            T = min(512, n_tok - t0)
            ps, bps = nextps(K)
            P.mm(lambda e, ps=ps, g=g, t0=t0, T=T: e.matmul(ps[:, 0:T], pw[:, g, :], diff[:, t0:t0 + T],
                                                           start=True, stop=True), [b_pw, b_diff], bps, True, True)
            i = yi % 2
            yi += 1
            P.op("act", lambda e, ps=ps, i=i, g=g, T=T: e.activation(
                yst[i][:, 0:T], ps[:, 0:T], AF.Copy, scale=ppc(K, f"psc{l}", g)), reads=[bps, K.b_const],
                writes=[b_yst[i]])
            P.dma("sp", ydst[:, 12 + g, tok0 + t0:tok0 + t0 + T], yst[i][:, 0:T], K.b_ycat[s]["pool"],
                  reads=[b_yst[i]])


def phase_att(K, l, s):
    P = K.P
    with_ctx_q = (l == 0)
    ident_b = K.mats_b[:, 0, :]
    blk_b = K.mats_b[:, 2, :]
    perm_b = K.mats_b[:, 3, :]
    maskA = K.mats_b[:, 4, :]
    maskB = K.mats_b[:, 5, :]
    NTI = 5
    qT = sb(K, "at_q", [128, 8, TT], BF16)
    b_q = [[P.buf(f"at_q{c}_{t}") for t in range(NTI)] for c in range(8)]
    kd = [sb(K, f"at_k{g}", [128, TT], BF16) for g in range(4)]
    b_k = [[P.buf(f"at_k{g}_{t}") for t in range(NTI)] for g in range(4)]
    Va = sb(K, "at_v", [128, 18, 260], BF16)
    b_v = P.buf("at_v")
    rC = sb(K, "at_rC", [128, L], F32)
    rS = sb(K, "at_rS", [128, L], F32)
    b_rope = P.buf("at_rope")
    esink = sb(K, "at_es", [128, 16], F32)
    b_es = P.buf("at_es")
    P.dma("sp", rC[:], K.ropeC, b_rope, reads=[K.b_in])
    P.dma("sp", rS[:], K.ropeS, b_rope, reads=[K.b_in])
    P.op("act", lambda e: e.activation(esink[:], K.bct[:, l * 16:(l + 1) * 16], AF.Exp), reads=[K.b_const],
         writes=[b_es])
    P.dma("sp", Va[:], K.zv[s].rearrange("(c p) n -> p c n", p=128), b_v, reads=[K.b_z[s]["zv"]])
    zq = K.zq[s].rearrange("(c p) t -> p c t", p=128)
    tiles = [(t0, 512) for t0 in range(0, L, 512)] + [(L, LC)]
    for ti, (t0, T) in enumerate(tiles):
        for c in range(8):
            if ti == 4 and not with_ctx_q:
                continue
            P.dma("sp", qT[:, c, t0:t0 + T], zq[:, c, t0:t0 + T], b_q[c][ti], reads=[K.b_z[s]["zq"]])
        for g in range(4):
            for half in range(2):
                P.dma("sp", kd[g][half * 64:(half + 1) * 64, t0:t0 + T], K.zk[s][g * 64:(g + 1) * 64, t0:t0 + T],
                      b_k[g][ti], reads=[K.b_z[s]["zk"]])
    sq = [sb(K, f"at_sq{i}", [128, 512], BF16) for i in range(2)]
    b_sq = P.bufs("at_sq", 2)
    rs = [sb(K, f"at_rs{i}", [128, 512], F32) for i in range(2)]
    b_rs = P.bufs("at_rs", 2)
    qn = [sb(K, f"at_qn{i}", [128, 512], BF16) for i in range(2)]
    b_qn = P.bufs("at_qn", 2)
    t1 = [sb(K, f"at_t1{i}", [128, 512], F32) for i in range(2)]
    b_t1 = P.bufs("at_t1", 2)
    t2 = [sb(K, f"at_t2{i}", [128, 512], F32) for i in range(2)]
    b_t2 = P.bufs("at_t2", 2)
    it = 0
    items = []
    for ti, (t0, T) in enumerate(tiles):
        for c in range(8):
            if ti == 4 and not with_ctx_q:
                continue
            items.append((qT[:, c, t0:t0 + T], b_q[c][ti], f"qg{l}", ti, t0, T))
        for g in range(4):
            items.append((kd[g][:, t0:t0 + T], b_k[g][ti], f"kg{l}", ti, t0, T))
    for (xa, b_x, gname, ti, t0, T) in items:
        i = it % 2
        it += 1
        P.op("act", lambda e, xa=xa, i=i, T=T: e.activation(sq[i][:, 0:T], xa, AF.Square), reads=[b_x],
             writes=[b_sq[i]])
        ps, bps = nextps(K)
        P.mm(lambda e, ps=ps, i=i, T=T: e.matmul(ps[:, 0:T], blk_b, sq[i][:, 0:T], start=True, stop=True),
             [b_sq[i], K.b_const], bps, True, True)
        rsqrt_from_ps(K, rs[i][:, 0:T], b_rs[i], ps[:, 0:T], bps, 1.0 / 64)
        if ti == 4:
            P.op("dve", lambda e, xa=xa, i=i, T=T, gname=gname: e.scalar_tensor_tensor(
                xa, xa, ppc(K, gname), rs[i][:, 0:T], ALU.mult, ALU.mult), reads=[b_x, b_rs[i], K.b_const],
                writes=[b_x])
            continue
        P.op("dve", lambda e, xa=xa, i=i, T=T, gname=gname: e.scalar_tensor_tensor(
            qn[i][:, 0:T], xa, ppc(K, gname), rs[i][:, 0:T], ALU.mult, ALU.mult), reads=[b_x, b_rs[i], K.b_const],
            writes=[b_qn[i]])
        ps2, bps2 = nextps(K)
        P.mm(lambda e, ps2=ps2, i=i, T=T: e.matmul(ps2[:, 0:T], perm_b, qn[i][:, 0:T], start=True, stop=True),
             [b_qn[i], K.b_const], bps2, True, True)
        P.op("pool", lambda e, i=i, t0=t0, T=T: e.tensor_tensor(t1[i][:, 0:T], qn[i][:, 0:T], rC[:, t0:t0 + T], ALU.mult),
             reads=[b_qn[i], b_rope], writes=[b_t1[i]])
        P.op("dve", lambda e, ps2=ps2, i=i, t0=t0, T=T: e.tensor_tensor(t2[i][:, 0:T], ps2[:, 0:T], rS[:, t0:t0 + T],
                                                                         ALU.mult),
             reads=[bps2, b_rope], writes=[b_t2[i]])
        P.op("pool", lambda e, xa=xa, i=i, T=T: e.tensor_tensor(xa, t1[i][:, 0:T], t2[i][:, 0:T], ALU.add),
             reads=[b_t1[i], b_t2[i]], writes=[b_x])
    Pt = [sb(K, f"at_P{i}", [128, 5, 128], BF16) for i in range(2)]
    b_P = P.bufs("at_P", 2)
    yt = [sb(K, f"at_y{i}", [128, 16, 64], BF16) for i in range(2)]
    b_y = P.bufs("at_y", 2)
    dsum = [sb(K, f"at_ds{i}", [128, 16], F32) for i in range(2)]
    b_ds = P.bufs("at_ds", 2)
    yst = [sb(K, f"at_yst{i}", [128, 8, 512], BF16) for i in range(2)]
    b_yst = P.bufs("at_yst", 2)
    ydst = K.ycat[s].rearrange("(c p) t -> p c t", p=128)
    blocks = list(range(16)) + ([16, 17] if with_ctx_q else [])
    pi = 0
    sli = 0
    sci = 0
    ysi = 0
    grp_start = 0
    for bi_, n in enumerate(blocks):
        is_cq = n >= 16
        tq = n // 4
        if is_cq:
            loc = []
        else:
            loc = [j for j in (n - 1, n, n + 1) if 0 <= j < 16]
        chunks = [(j, idx) for idx, j in enumerate(loc)] + [(16, 3), (17, 4)]
        yi = bi_ % 2
        for h in range(16):
            g = h // 4
            base = (h % 2) * 64
            c = h // 2
            qa = qT[base:base + 64, c, n * 128:(n + 1) * 128]
            p_i = pi % 2
            pi += 1
            if loc:
                bl = sli % 2
                sli += 1
                Sl, b_Sl = K.ps[bl], K.b_ps[bl]
                ops = []
                for idx, j in enumerate(loc):
                    ops.append((idx, j, None))
                    if j == n - 1:
                        ops.append((idx, j, maskA))
                    elif j == n + 1:
                        ops.append((idx, j, maskB))
                for oi, (idx, j, msk) in enumerate(ops):
                    first, last = oi == 0, oi == len(ops) - 1
                    has_mask = (j != n)
                    if msk is None:
                        P.mm(lambda e, Sl=Sl, idx=idx, j=j, g=g, base=base, qa=qa, has_mask=has_mask: e.matmul(
                            Sl[:, idx * 128:(idx + 1) * 128], kd[g][base:base + 64, j * 128:(j + 1) * 128], qa,
                            start=True, stop=not has_mask), [b_k[g][j // 4], b_q[c][tq]], b_Sl, first, last)
                    else:
                        P.mm(lambda e, Sl=Sl, idx=idx, msk=msk: e.matmul(
                            Sl[:, idx * 128:(idx + 1) * 128], ident_b, msk, start=False, stop=True),
                            [K.b_const], b_Sl, first, last)
                nl = len(loc)
                P.op("act", lambda e, Sl=Sl, p_i=p_i, nl=nl: e.activation(
                    Pt[p_i][:, 0:nl, :], Sl[:, 0:nl * 128].rearrange("p (a n) -> p a n", a=nl), AF.Exp, scale=0.125),
                    reads=[b_Sl], partial=[b_P[p_i]])
            bc_ = 2 + sci % 2
            sci += 1
            Sc, b_Sc = K.ps[bc_], K.b_ps[bc_]
            for ci in range(2):
                j = 16 + ci
                P.mm(lambda e, Sc=Sc, ci=ci, j=j, g=g, base=base, qa=qa: e.matmul(
                    Sc[:, ci * 128:(ci + 1) * 128], kd[g][base:base + 64, j * 128:(j + 1) * 128], qa,
                    start=True, stop=True), [b_k[g][4], b_q[c][tq]], b_Sc, ci == 0, ci == 1)
            P.op("act", lambda e, Sc=Sc, p_i=p_i: e.activation(
                Pt[p_i][:, 3:5, :], Sc[:, 0:256].rearrange("p (a n) -> p a n", a=2), AF.Exp, scale=0.125),
                reads=[b_Sc], partial=[b_P[p_i]])
            ob = 5 + h // 7
            off = (h % 7) * 65
            O, b_O = K.ps[ob], K.b_ps[ob]
            hb_first = (h % 7 == 0)
            hb_last = (h % 7 == 6) or (h == 15)
            for ki_, (j, slot) in enumerate(chunks):
                P.mm(lambda e, O=O, off=off, p_i=p_i, slot=slot, j=j, g=g, ki_=ki_, nch=len(chunks): e.matmul(
                    O[:, off:off + 65], Pt[p_i][:, slot, :], Va[:, j, g * 65:(g + 1) * 65],
                    start=(ki_ == 0), stop=(ki_ == nch - 1)), [b_P[p_i], b_v], b_O,
                    hb_first and ki_ == 0, hb_last and ki_ == len(chunks) - 1)
        for bk, (h0, nh) in enumerate(((0, 7), (7, 7), (14, 2))):
            O, b_O = K.ps[5 + bk], K.b_ps[5 + bk]
            Ov = O[:, 0:nh * 65].rearrange("p (h d) -> p h d", d=65)
            P.op("dve", lambda e, Ov=Ov, h0=h0, nh=nh, yi=yi: e.tensor_tensor(
                dsum[yi][:, h0:h0 + nh], Ov[:, :, 64], esink[:, h0:h0 + nh], ALU.add),
                reads=[b_O, b_es], partial=[b_ds[yi]])
            P.op("dve", lambda e, h0=h0, nh=nh, yi=yi: e.reciprocal(dsum[yi][:, h0:h0 + nh], dsum[yi][:, h0:h0 + nh]),
                 reads=[b_ds[yi]], writes=[b_ds[yi]])
            for hh in range(nh):
                P.op("dve", lambda e, Ov=Ov, hh=hh, h0=h0, yi=yi: e.tensor_scalar(
                    yt[yi][:, h0 + hh, :], Ov[:, hh, 0:64], dsum[yi][:, h0 + hh:h0 + hh + 1], None, ALU.mult),
                    reads=[b_O, b_ds[yi]], partial=[b_y[yi]])
        Tb = K.ps[4][:].bitcast(BF16)
        b_T = K.b_ps[4]
        yflat = yt[yi][:].rearrange("p h d -> p (h d)")
        for c in range(8):
            P.mm(lambda e, Tb=Tb, c=c, yflat=yflat: e.transpose(
                Tb[:, c * 128:(c + 1) * 128], yflat[:, c * 128:(c + 1) * 128], ident_b),
                [b_y[yi], K.b_const], b_T, c == 0, c == 7)
        a = n % 4
        ys = ysi % 2
        P.op("act", lambda e, Tb=Tb, ys=ys, a=a: e.copy(
            yst[ys][:, :, a * 128:(a + 1) * 128], Tb[:, 0:1024].rearrange("p (c n) -> p c n", c=8)),
            reads=[b_T], partial=[b_yst[ys]])
        end_grp = (a == 3) or (n == blocks[-1])
        if end_grp:
            g0 = (n // 4) * 4
            w = (a + 1) * 128
            P.dma("sp", ydst[:, 0:8, g0 * 128:g0 * 128 + w], yst[ys][:, :, 0:w], K.b_ycat[s]["att"],
                  reads=[b_yst[ys]])
            ysi += 1


def phase_B(K, l, s):
    P = K.P
    last = (l == DEPTH - 1)
    wring(K, 3)
    tmp = h_tmp(K, "pb_")
    ident_f = K.mats_f[:, 0, :]
    xt = sb(K, "pb_x", [128, 16, 512], F32)
    b_xt = P.buf("pb_x")
    h = sb(K, "pb_h", [128, 16, 512], BF16)
    b_h = P.buf("pb_h")
    yc = sb(K, "pb_yc", [128, 16, 512], BF16)
    b_yc = P.buf("pb_yc")
    m = sb(K, "pb_m", [128, 16, 512], BF16)
    b_m = P.buf("pb_m")
    gs = [sb(K, f"pb_gs{i}", [128, 512], BF16) for i in range(2)]
    b_gs = P.bufs("pb_gs", 2)
    acc = sb(K, "pb_acc", [128, 4, 512], F32)
    b_acc = P.bufs("pb_acc", 4)
    tr = [sb(K, f"pb_tr{i}", [128, 512], F32) for i in range(2)]
    b_tr = P.bufs("pb_tr", 2)
    ev = [sb(K, f"pb_ev{i}", [128, 512], F32) for i in range(4)]
    b_ev = P.bufs("pb_ev", 4)
    ei_ = 0
    hid = [sb(K, f"pb_hid{i}", [128, 4, 512], BF16) for i in range(2)]
    b_hid = P.bufs("pb_hid", 2)
    b_ost = [P.buf("pb_ost")] * 2
    xsrc = K.xT[l % 2][s].rearrange("(c p) t -> p c t", p=128)
    xdst = K.xT[(l + 1) % 2][s].rearrange("(c p) t -> p c t", p=128)
    ysrc = K.ycat[s].rearrange("(c p) t -> p c t", p=128)
    win = K.wb["w_in"][l].rearrange("(k p) n -> p k n", p=128)
    b_win = K.b_wb["w_in"][l]
    branches = [(GATE_OFF, K.wb["w_att_o"][l].rearrange("(k p) n -> p k n", p=128), 8, 0, K.b_wb["w_att_o"][l]),
                (GATE_OFF + D, K.wb["w_hy_o"][l].rearrange("(k p) n -> p k n", p=128), 4, 8, K.b_wb["w_hy_o"][l]),
                (GATE_OFF + 2 * D, K.wb["w_pool_o"][l].rearrange("(k p) n -> p k n", p=128), 4, 12,
                 K.b_wb["w_pool_o"][l])]
    wout = K.wb["w_out"][l].rearrange("(k p) n -> p k n", p=128)
    w1v = K.wb["mlp_w1"][l].rearrange("(k p) n -> p k n", p=128)
    gi = 0
    ti_ = 0
    hi_ = 0
    oi = 0
    for (t0, T) in tiles_of(l, with_ctx=not last):
        is_ctx = t0 >= L
        who = 2 if is_ctx else s
        P.dma("act", xt[:, :, 0:T], xsrc[:, :, t0:t0 + T], b_xt, reads=[K.b_xT[l % 2][s]], partial=False)
        P.dma("act", yc[:, :, 0:T], ysrc[:, :, t0:t0 + T], b_yc,
              reads=[K.b_ycat[s]["att"], K.b_ycat[s]["hy"], K.b_ycat[s]["pool"]], partial=False)
        make_h(K, xt, b_xt, h, b_h, T, l, who, 0, tmp)
        for J in range(4):
            for r, (gcol, wo_ap, kr, yoff, b_wsrc) in enumerate(branches):
                wg, b_wg = wload(K, win[:, :, gcol + J * 512:gcol + (J + 1) * 512], 16, 512, b_win)
                wo, b_wo = wload(K, wo_ap[:, :, J * 512:(J + 1) * 512], kr, 512, b_wsrc)
                for j in range(4):
                    psg, bpsg = nextps(K)
                    for k in range(16):
                        P.mm(lambda e, psg=psg, wg=wg, j=j, k=k, T=T: e.matmul(
                            psg[:, 0:T], wg[:, k, j * 128:(j + 1) * 128], h[:, k, 0:T], start=(k == 0), stop=(k == 15)),
                            [b_wg, b_h], bpsg, k == 0, k == 15)
                    pso, bpso = nextps(K)
                    for k in range(kr):
                        P.mm(lambda e, pso=pso, wo=wo, j=j, k=k, T=T, yoff=yoff, kr=kr: e.matmul(
                            pso[:, 0:T], wo[:, k, j * 128:(j + 1) * 128], yc[:, yoff + k, 0:T],
                            start=(k == 0), stop=(k == kr - 1)), [b_wo, b_yc], bpso, k == 0, k == kr - 1)
                    g_i = gi % 2
                    gi += 1
                    P.op("act", lambda e, psg=psg, g_i=g_i, T=T: e.activation(gs[g_i][:, 0:T], psg[:, 0:T], AF.Sigmoid),
                         reads=[bpsg], writes=[b_gs[g_i]])
                    if r == 0:
                        P.op("dve", lambda e, pso=pso, g_i=g_i, j=j, T=T: e.tensor_tensor(
                            acc[:, j, 0:T], pso[:, 0:T], gs[g_i][:, 0:T], ALU.mult), reads=[bpso, b_gs[g_i]],
                            writes=[b_acc[j]])
                    else:
                        t_i = ti_ % 2
                        ti_ += 1
                        P.op("dve", lambda e, pso=pso, g_i=g_i, t_i=t_i, T=T: e.tensor_tensor(
                            tr[t_i][:, 0:T], pso[:, 0:T], gs[g_i][:, 0:T], ALU.mult), reads=[bpso, b_gs[g_i]],
                            writes=[b_tr[t_i]])
                        if r == 1:
                            P.op("pool", lambda e, t_i=t_i, j=j, T=T: e.tensor_tensor(
                                acc[:, j, 0:T], acc[:, j, 0:T], tr[t_i][:, 0:T], ALU.add), reads=[b_tr[t_i], b_acc[j]],
                                writes=[b_acc[j]])
                        else:
                            P.op("pool", lambda e, t_i=t_i, j=j, J=J, T=T: e.tensor_tensor(
                                m[:, 4 * J + j, 0:T], acc[:, j, 0:T], tr[t_i][:, 0:T], ALU.add),
                                reads=[b_tr[t_i], b_acc[j]], partial=[b_m])
        if K.cut == 1:
            P.dma("sp", xdst[:, :, t0:t0 + T], xt[:, :, 0:T], K.b_xT[(l + 1) % 2][s], reads=[b_xt, b_m])
            continue
        for J in range(4):
            wo, b_wo = wload(K, wout[:, :, J * 512:(J + 1) * 512], 16, 512, K.b_wb["w_out"][l])
            for j in range(4):
                cc = 4 * J + j
                ps, bps = nextps(K)
                for k in range(16):
                    P.mm(lambda e, ps=ps, wo=wo, j=j, k=k, T=T: e.matmul(
                        ps[:, 0:T], wo[:, k, j * 128:(j + 1) * 128], m[:, k, 0:T], start=(k == 0), stop=(k == 15)),
                        [b_wo, b_m], bps, k == 0, k == 15)
                e_i = ei_ % 4
                ei_ += 1
                P.op("act", lambda e, ps=ps, cc=cc, T=T, who=who, e_i=e_i: e.activation(
                    ev[e_i][:, 0:T], ps[:, 0:T], AF.Copy, scale=modcol(K, l, who, 2, cc)),
                    reads=[bps, K.b_mod], writes=[b_ev[e_i]])
                P.op("pool", lambda e, cc=cc, T=T, e_i=e_i: e.tensor_tensor(
                    xt[:, cc, 0:T], xt[:, cc, 0:T], ev[e_i][:, 0:T], ALU.add),
                    reads=[b_ev[e_i], b_xt], writes=[b_xt])
        if K.cut == 2:
            P.dma("sp", xdst[:, :, t0:t0 + T], xt[:, :, 0:T], K.b_xT[(l + 1) % 2][s], reads=[b_xt, b_m])
            continue
        make_h(K, xt, b_xt, h, b_h, T, l, who, 1, tmp)
        if K.cut == 3:
            P.dma("sp", xdst[:, :, t0:t0 + T], xt[:, :, 0:T], K.b_xT[(l + 1) % 2][s], reads=[b_xt, b_h])
            continue
        for Hb in range(16):
            w1, b_w1 = wload(K, w1v[:, :, Hb * 512:(Hb + 1) * 512], 16, 512, K.b_wb["mlp_w1"][l])
            w2, b_w2 = wload(K, K.wb["mlp_w2"][l][Hb * 512:(Hb + 1) * 512, :].rearrange("(k p) n -> p k n", p=128),
                             4, D, K.b_wb["mlp_w2"][l])
            h_i = hi_ % 2
            hi_ += 1
            for jj in range(4):
                ps, bps = nextps(K)
                for k in range(16):
                    P.mm(lambda e, ps=ps, w1=w1, jj=jj, k=k, T=T: e.matmul(
                        ps[:, 0:T], w1[:, k, jj * 128:(jj + 1) * 128], h[:, k, 0:T], start=(k == 0), stop=(k == 15)),
                        [b_w1, b_h], bps, k == 0, k == 15)
                t_i = ti_ % 2
                ti_ += 1
                P.op("act", lambda e, ps=ps, t_i=t_i, T=T: e.activation(tr[t_i][:, 0:T], ps[:, 0:T], AF.Square),
                     reads=[bps], writes=[b_tr[t_i]])
                P.op("dve", lambda e, ps=ps, t_i=t_i, h_i=h_i, jj=jj, T=T: e.scalar_tensor_tensor(
                    hid[h_i][:, jj, 0:T], ps[:, 0:T], 0.0, tr[t_i][:, 0:T], ALU.is_gt, ALU.mult),
                    reads=[bps, b_tr[t_i]], partial=[b_hid[h_i]])
            if K.cut == 4:
                P.dma("sp", xdst[:, 0:4, t0:t0 + T], hid[h_i][:, :, 0:T].bitcast(F32) if False else xt[:, 0:4, 0:T],
                      K.b_xT[(l + 1) % 2][s], reads=[b_xt, b_hid[h_i], b_w2])
                continue
            for j in range(16):
                ps, bps = nextps(K)
                for k in range(4):
                    P.mm(lambda e, ps=ps, w2=w2, j=j, k=k, T=T, h_i=h_i: e.matmul(
                        ps[:, 0:T], w2[:, k, j * 128:(j + 1) * 128], hid[h_i][:, k, 0:T], start=(k == 0), stop=(k == 3)),
                        [b_w2, b_hid[h_i]], bps, k == 0, k == 3)
                if K.cut == 7:
                    P.op("dve", lambda e, ps=ps, j=j, T=T, who=who: e.scalar_tensor_tensor(
                        tr[0][:, 0:T], ps[:, 0:T], modcol(K, l, who, 5, j), xt[:, j, 0:T], ALU.mult, ALU.add),
                        reads=[bps, K.b_mod, b_xt], writes=[b_tr[0]])
                    continue
                if K.cut == 8:
                    P.op("dve", lambda e, ps=ps, j=j, T=T, who=who: e.tensor_tensor(
                        xt[:, j, 0:T], ps[:, 0:T], xt[:, j, 0:T], ALU.add),
                        reads=[bps, b_xt], writes=[b_xt])
                    continue
                if K.cut == 6:
                    P.op("dve", lambda e, ps=ps, T=T: e.tensor_copy(tr[0][:, 0:T], ps[:, 0:T]),
                         reads=[bps], writes=[b_tr[0]])
                    continue
                e_i = ei_ % 4
                ei_ += 1
                P.op("act", lambda e, ps=ps, j=j, T=T, who=who, e_i=e_i: e.activation(
                    ev[e_i][:, 0:T], ps[:, 0:T], AF.Copy, scale=modcol(K, l, who, 5, j)),
                    reads=[bps, K.b_mod], writes=[b_ev[e_i]])
                P.op("pool", lambda e, j=j, T=T, e_i=e_i: e.tensor_tensor(
                    xt[:, j, 0:T], xt[:, j, 0:T], ev[e_i][:, 0:T], ALU.add),
                    reads=[b_ev[e_i], b_xt], writes=[b_xt])
        if not last:
            P.dma("act", xdst[:, :, t0:t0 + T], xt[:, :, 0:T], K.b_xT[(l + 1) % 2][s], reads=[b_xt])
            if K.cut in (5, 6, 7, 8):
                break
        else:
            P.dma("act", K.out[s].rearrange("(c p) t -> p c t", p=128)[:, :, t0:t0 + T], xt[:, :, 0:T], K.b_out,
                  reads=[b_xt])


def make_in_maps(inp, cores):
    c = _consts()
    f32 = lambda a: np.ascontiguousarray(np.asarray(a, np.float32))
    shared = {k: f32(inp[k]) for k in ("w_mod", "w_in", "filt_w0", "filt_w1", "filt_w2", "pool_w", "w_att_o",
                                       "w_hy_o", "w_pool_o", "w_out", "mlp_w1", "mlp_w2")}
    shared.update(c)
    shared["bc"] = _bc_layout(inp)
    maps = []
    pp_off = None
    for core in cores:
        pp = _pp_layout(inp, core)
        m = dict(shared)
        xs = np.concatenate([inp["x"][core * SPC:(core + 1) * SPC], inp["ctx"][core * SPC:(core + 1) * SPC]], axis=1)
        m["xTin"] = f32(xs.transpose(0, 2, 1))
        m["pp"] = pp.build()
        pp_off = (pp.off, pp.n)
        maps.append(m)
    return maps, pp_off, shared["bc"].shape[1]


def kernel(**inputs):
    inp = {k: np.asarray(v) for k, v in inputs.items()}
    cores = list(range(NCORES))
    maps, (off, npp), nbc = make_in_maps(inp, cores)
    nc, P = build_program(off, npp, nbc)
    res = run_bass_kernel_spmd(nc, maps, core_ids=cores)
    outs = [np.asarray(r["outT"]).transpose(0, 2, 1) for r in res.results]
    return np.ascontiguousarray(np.concatenate(outs, axis=0).astype(np.float32))
```

```python
import math
from contextlib import ExitStack
import numpy as np
import ml_dtypes
import concourse.bass as bass
import concourse.mybir as mybir
from concourse.bass_utils import run_bass_kernel_spmd

F32 = mybir.dt.float32
BF16 = mybir.dt.bfloat16
ALU = mybir.AluOpType
AF = mybir.ActivationFunctionType

D = 2048
L = 2048
LC = 256
TT = L + LC
DEPTH = 2
NCORES = 8
SPC = 2
EPS = 1e-6
IN_W = 9728
Q_OFF, K_OFF, V_OFF, HY_OFF, POOL_OFF, GATE_OFF = 0, 1024, 1280, 1536, 3072, 3584
D_FF = 8192
NPBF = np.dtype(ml_dtypes.bfloat16)


class Buf:
    __slots__ = ("name", "W", "R", "G", "sem", "cnt")

    def __init__(self, name, init_readers=None):
        self.name = name
        self.W = {}
        self.R = dict(init_readers) if init_readers else {}
        self.G = {}
        self.sem = None
        self.cnt = 0


class Op:
    __slots__ = ("eng", "fn", "deps", "dma", "token", "signal", "pos", "sigval", "key", "nofence")


class Prog:
    ENGS = ("pe", "act", "dve", "pool", "sp")

    def __init__(self, nc):
        self.nc = nc
        self.ops = {e: [] for e in self.ENGS}
        self.sems = {}
        self.dma_sems = []
        self.clock = {e: {} for e in self.ENGS}
        self.last_dma = {}
        self.fence_readers = {}
        self.nsem = 0
        self.scope = None
        self.free_sems = []
        self.final_waits = {}

    def buf(self, name):
        b = Buf(name, self.fence_readers)
        if self.scope is not None:
            self.scope.append(b)
        return b

    def push_scope(self):
        self.scope = []

    def pop_scope(self):
        for b in self.scope:
            if b.sem is not None:
                self.free_sems.append((b.sem, b.cnt))
                if b in self.dma_sems:
                    self.dma_sems.remove(b)
                self.final_waits[id(b.sem)] = (b.sem, b.cnt)
        self.scope = None

    def bufs(self, name, n):
        return [self.buf(f"{name}{i}") for i in range(n)]

    def fence(self):
        fr = {}
        for e in ("pe", "act", "dve", "pool"):
            for op in reversed(self.ops[e]):
                if not op.dma:
                    fr[("e", e)] = op
                    break
        for k, op in self.last_dma.items():
            if getattr(op, "nofence", False):
                continue
            fr[("d", k)] = op
        self.fence_readers = fr

    def _new_sem(self, name):
        s = self.nc.alloc_semaphore(name)
        self.nsem += 1
        return s

    def _add(self, eng, fn, reads, writes, partial, dma_dest, mm_first, mm_last):
        op = Op()
        op.nofence = False
        op.eng = eng
        op.fn = fn
        op.dma = dma_dest is not None
        op.signal = False
        op.sigval = None
        op.token = None
        deps = {}

        def add_deps(d):
            for k, a in d.items():
                old = deps.get(k)
                if old is None or self._later(a, old):
                    deps[k] = a

        if op.dma:
            b = dma_dest
            if b.sem is None:
                if self.free_sems:
                    b.sem, b.cnt = self.free_sems.pop()
                else:
                    b.sem = self._new_sem("d" + str(self.nsem))
                self.dma_sems.append(b)
            b.cnt += 16
            op.token = (b.sem, b.cnt)
            op.key = ("d", id(b.sem))
        else:
            op.key = ("e", eng)
        for b in reads:
            add_deps(b.W)
        for b in writes:
            if mm_first is False:
                pass
            else:
                add_deps(b.W)
                add_deps(b.R)
        for b in partial:
            if b.R:
                b.G = dict(b.R)
            add_deps(b.G)
        for b in reads:
            b.R[op.key] = op
        for b in writes:
            if mm_last is False:
                if mm_first:
                    b.W = {}
                    b.R = {}
                continue
            b.W = {op.key: op}
            b.R = {}
            b.G = {}
        for b in partial:
            if b.R:
                b.W = {op.key: op}
                b.R = {}
            else:
                b.W[op.key] = op
        clk = self.clock[eng]
        final = []
        op.pos = len(self.ops[eng])
        for k, a in deps.items():
            if a is op:
                continue
            if a.dma:
                sem, val = a.token
                if clk.get(k, 0) >= val:
                    continue
                clk[k] = val
                final.append(a)
            else:
                if a.eng == "pe" and eng == "pe":
                    continue
                if clk.get(k, -1) >= a.pos:
                    continue
                clk[k] = a.pos
                a.signal = True
                final.append(a)
        op.deps = final
        self.ops[eng].append(op)
        if op.dma:
            self.last_dma[id(op.token[0])] = op
        return op

    @staticmethod
    def _later(a, b):
        if a.dma:
            return a.token[1] > b.token[1]
        return a.pos > b.pos

    def op(self, eng, fn, reads=(), writes=(), partial=()):
        return self._add(eng, fn, reads, writes, partial, None, None, None)

    def mm(self, fn, reads, out, first, last):
        return self._add("pe", fn, reads, (out,), (), None, first, last)

    def dma(self, eng, out_ap, in_ap, dst, reads=(), partial=True, **kw):
        fn = lambda e: e.dma_start(out=out_ap, in_=in_ap, **kw)
        if partial:
            return self._add(eng, fn, reads, (), (dst,), dst, None, None)
        return self._add(eng, fn, reads, (dst,), (), dst, None, None)

    def emit(self):
        nc = self.nc
        esem = {e: self._new_sem("e_" + e) for e in ("pe", "act", "dve", "pool")}
        for e in ("pe", "act", "dve", "pool"):
            n = 0
            for op in self.ops[e]:
                if not op.dma and op.signal:
                    n += 1
                    op.sigval = n
        handles = {"pe": "tensor", "act": "scalar", "dve": "vector", "pool": "gpsimd", "sp": "sync"}
        stats = {}
        with nc.Block() as block:
            for e in self.ENGS:
                ops = self.ops[e]
                if not ops and e != "sp":
                    continue

                def body(h, ops=ops, e=e):
                    nw = 0
                    for op in ops:
                        for a in op.deps:
                            if a.dma:
                                h.wait_ge(a.token[0], a.token[1])
                            else:
                                h.wait_ge(esem[a.eng], a.sigval)
                            nw += 1
                        ins = op.fn(h)
                        if op.dma:
                            ins.then_inc(op.token[0], 16)
                        elif op.signal:
                            ins.then_inc(esem[e], 1)
                    if e == "sp":
                        fw = dict(self.final_waits)
                        for b in self.dma_sems:
                            fw[id(b.sem)] = (b.sem, b.cnt)
                        for (sm, cnt) in fw.values():
                            h.wait_ge(sm, cnt)
                    stats[e] = (len(ops), nw)

                getattr(block, handles[e])(body)
        self.stats = stats


def _bf(a):
    return np.ascontiguousarray(a.astype(NPBF))


_CONST_CACHE = {}


def _dft_consts(n_tok):
    n = 2 * n_tok
    t = np.arange(n_tok, dtype=np.float64)[:, None]
    f = np.arange(n_tok, dtype=np.float64)[None, :]
    ang = 2.0 * np.pi * (f + 0.5) * t / n
    Fm = np.concatenate([np.cos(ang), np.sin(ang)], axis=1)
    return _bf(Fm), _bf(Fm.T)


def _filter_consts(n_tok):
    t = np.linspace(0.0, 1.0, n_tok, dtype=np.float32)[:, None]
    w = (2.0 * math.pi * np.arange(n_tok, dtype=np.float32)[:, None] / n_tok).astype(np.float32)
    bands = np.linspace(1e-4, 15, 16, dtype=np.float32)[None, :]
    z = np.concatenate([t, np.cos(bands * w), -np.sin(bands * w)], axis=-1).astype(np.float32)
    deltas = np.linspace(math.log(1e-2) / 1.5, math.log(1e-2) / 0.3, 512, dtype=np.float32)
    decay = np.exp(-t * np.abs(deltas)[None, :]).astype(np.float32)
    zs = np.zeros_like(z)
    zs[1:] = z[:-1]
    ds = np.zeros_like(decay)
    ds[1:] = decay[:-1]
    zz = np.stack([z.T, zs.T], 0)
    dd = np.stack([decay, ds], 0)
    return np.ascontiguousarray(zz), np.ascontiguousarray(dd)


def _pool_inv(n_tok):
    t = np.arange(n_tok)
    out = np.zeros((4, n_tok), np.float32)
    for g, win in enumerate((2, 4, 8, 16)):
        a = np.clip(t - win // 2, 0, n_tok)
        b = np.clip(t + win // 2, 0, n_tok)
        out[g] = 1.0 / (b - a).astype(np.float32)
    return np.ascontiguousarray(np.broadcast_to(out[None], (128, 4, n_tok)))


def _rope_tabs():
    rows = L // 64
    row = np.repeat(np.arange(rows, dtype=np.float32), 64)
    col = np.tile(np.arange(64, dtype=np.float32), rows)
    inv = (10000.0 ** (-np.arange(16, dtype=np.float32) / 16)).astype(np.float32)
    ar = row[:, None] * inv
    ac = col[:, None] * inv
    C = np.zeros((64, L), np.float32)
    S = np.zeros((64, L), np.float32)
    for ax, a in enumerate((ar, ac)):
        c = np.cos(a).T
        s = np.sin(a).T
        C[ax * 32:ax * 32 + 16] = c
        C[ax * 32 + 16:ax * 32 + 32] = c
        S[ax * 32:ax * 32 + 16] = -s
        S[ax * 32 + 16:ax * 32 + 32] = s
    C = np.concatenate([C, C], 0)
    S = np.concatenate([S, S], 0)
    return np.ascontiguousarray(C), np.ascontiguousarray(S)


def _misc_mats():
    ident = np.eye(128, dtype=np.float32)
    ones = np.ones((128, 128), np.float32)
    blk = np.zeros((128, 128), np.float32)
    blk[:64, :64] = 1
    blk[64:, 64:] = 1
    perm = np.zeros((128, 128), np.float32)
    for p in range(128):
        r = p % 32
        q = p - r + (r + 16) % 32
        perm[q, p] = 1.0
    ki = np.arange(128)[:, None]
    qi = np.arange(128)[None, :]
    maskA = np.where(qi > ki, -30000.0, 0.0).astype(np.float32)
    maskB = np.where(ki > qi, -30000.0, 0.0).astype(np.float32)
    m = np.stack([ident, ones, blk, perm, maskA, maskB], 0)
    return np.ascontiguousarray(m.transpose(1, 0, 2))


def _consts():
    if "c" in _CONST_CACHE:
        return _CONST_CACHE["c"]
    c = {}
    c["F_l"], c["FT_l"] = _dft_consts(L)
    c["F_c"], c["FT_c"] = _dft_consts(LC)
    c["zz_l"], c["dd_l"] = _filter_consts(L)
    c["zz_c"], c["dd_c"] = _filter_consts(LC)
    c["pinv_l"] = _pool_inv(L)
    c["pinv_c"] = _pool_inv(LC)
    c["ropeC"], c["ropeS"] = _rope_tabs()
    c["mats"] = _misc_mats()
    _CONST_CACHE["c"] = c
    return c


class PP:
    def __init__(self):
        self.cols = []
        self.off = {}
        self.n = 0

    def add(self, name, arr):
        arr = np.asarray(arr, np.float32).reshape(128, -1)
        self.off[name] = (self.n, arr.shape[1])
        self.cols.append(arr)
        self.n += arr.shape[1]

    def build(self):
        return np.ascontiguousarray(np.concatenate(self.cols, axis=1))


def _chunked(v, nch):
    return np.asarray(v, np.float32).reshape(nch, 128).T


def _pp_layout(inp, core):
    pp = PP()
    b0 = core * SPC
    cv = np.stack([inp["c"][b0], inp["c"][b0 + 1], inp["c_ctx"]], 0)
    pp.add("cvec", cv.reshape(3, 16, 128).transpose(2, 1, 0).reshape(128, 48))
    for l in range(DEPTH):
        pp.add(f"n1g{l}", _chunked(inp["norm1_g"][l], 16))
        pp.add(f"n2g{l}", _chunked(inp["norm2_g"][l], 16))
        pp.add(f"bmod{l}", _chunked(inp["b_mod"][l], 96))
        pp.add(f"qg{l}", np.tile(inp["q_norm_g"][l], 2).reshape(128, 1))
        pp.add(f"kg{l}", np.tile(inp["k_norm_g"][l], 2).reshape(128, 1))
        pp.add(f"hcw{l}", inp["hy_conv_w"][l].reshape(3, 12, 128).transpose(2, 1, 0).reshape(128, 36))
        pp.add(f"hcb{l}", _chunked(inp["hy_conv_b"][l], 12))
        pp.add(f"psc{l}", _chunked(inp["pool_scale"][l], 4))
        fb = np.zeros((128, 4), np.float32)
        fb[:64, 0] = inp["filt_b0"][l]
        fb[:64, 1] = inp["filt_b1"][l][0]
        fb[:64, 2] = inp["filt_b1"][l][1]
        fb[:64, 3] = inp["filt_freq"][l]
        pp.add(f"fb{l}", fb)
    return pp


def _bc_layout(inp):
    a = np.concatenate([inp["sink"].reshape(-1), inp["hy_bias"].reshape(-1)]).astype(np.float32)
    return np.ascontiguousarray(np.broadcast_to(a[None], (128, a.size)))


class Ctx:
    pass


def _stub(*a, **k):
    return None


phase_filters = phase_att = phase_hy = phase_pool = phase_B = _stub


def build_program(pp_off, npp, nbc, stop_after=None, debug=False):
    nc = bass.Bass("TRN2", target_bir_lowering=False)
    P = Prog(nc)
    K = Ctx()
    K.nc, K.P, K.pp_off = nc, P, pp_off
    okind = "ExternalOutput" if debug else "Internal"

    def din(name, shape, dt=F32):
        return nc.dram_tensor(name, list(shape), dt, kind="ExternalInput").ap()

    def dscr(name, shape, dt, dbg=True):
        return nc.dram_tensor(name, list(shape), dt, kind=(okind if dbg else "Internal")).ap()

    K.xTin = din("xTin", [SPC, D, TT])
    K.pp = din("pp", [128, npp])
    K.bc = din("bc", [128, nbc])
    K.w_mod = din("w_mod", [DEPTH, D, 6 * D])
    K.w_in = din("w_in", [DEPTH, D, IN_W])
    K.filt_w0 = din("filt_w0", [DEPTH, 33, 64])
    K.filt_w1 = din("filt_w1", [DEPTH, 2, 64, 64])
    K.filt_w2 = din("filt_w2", [DEPTH, 64, 2048])
    K.pool_w = din("pool_w", [DEPTH, 4, 128, 128])
    K.w_att_o = din("w_att_o", [DEPTH, 1024, D])
    K.w_hy_o = din("w_hy_o", [DEPTH, 512, D])
    K.w_pool_o = din("w_pool_o", [DEPTH, 512, D])
    K.w_out = din("w_out", [DEPTH, D, D])
    K.mlp_w1 = din("mlp_w1", [DEPTH, D, D_FF])
    K.mlp_w2 = din("mlp_w2", [DEPTH, D_FF, D])
    K.F_l = din("F_l", [L, 2 * L], BF16)
    K.FT_l = din("FT_l", [2 * L, L], BF16)
    K.F_c = din("F_c", [LC, 2 * LC], BF16)
    K.FT_c = din("FT_c", [2 * LC, LC], BF16)
    K.zz_l = din("zz_l", [2, 33, L])
    K.dd_l = din("dd_l", [2, L, 512])
    K.zz_c = din("zz_c", [2, 33, LC])
    K.dd_c = din("dd_c", [2, LC, 512])
    K.pinv_l = din("pinv_l", [128, 4, L])
    K.pinv_c = din("pinv_c", [128, 4, LC])
    K.ropeC = din("ropeC", [128, L])
    K.ropeS = din("ropeS", [128, L])
    K.mats = din("mats", [128, 6, 128])
    K.out = nc.dram_tensor("outT", [SPC, D, L], F32, kind="ExternalOutput").ap()
    K.xT = [[K.xTin[s] for s in range(SPC)], [dscr(f"xT1_{s}", [D, TT], F32) for s in range(SPC)]]
    K.zq = [dscr(f"zq{s}", [1024, TT], BF16) for s in range(SPC)]
    K.zk = [dscr(f"zk{s}", [256, TT], BF16) for s in range(SPC)]
    K.zv = [dscr(f"zv{s}", [TT, 4 * 65], BF16) for s in range(SPC)]
    K.zhy = [dscr(f"zhy{s}", [1536, TT], BF16) for s in range(SPC)]
    K.zpl = [dscr(f"zpl{s}", [512, TT], BF16) for s in range(SPC)]
    K.ycat = [dscr(f"ycat{s}", [D, TT], BF16) for s in range(SPC)]
    K.Kf_l = dscr("Kf_l", [2, 2, L, 512], F32)
    K.Kf_c = dscr("Kf_c", [2, 2, LC, 512], F32)
    K.moddbg = dscr("moddbg", [128, DEPTH * 3 * 96], F32)
    WSH = {"w_in": (D, IN_W), "w_att_o": (1024, D), "w_hy_o": (512, D), "w_pool_o": (512, D), "w_out": (D, D),
           "mlp_w1": (D, D_FF), "mlp_w2": (D_FF, D)}
    K.wb = {n: [nc.dram_tensor(f"wb_{n}{l}", list(sh), BF16, kind="Internal").ap() for l in range(DEPTH)]
            for n, sh in WSH.items()}
    K.b_wb = {n: [P.buf(f"wb_{n}{l}") for l in range(DEPTH)] for n in WSH}
    K.WSH = WSH
    K.b_xT = [[P.buf(f"xT{a}{s}") for s in range(SPC)] for a in range(2)]
    K.b_z = [{n: P.buf(n + str(s)) for n in ("zq", "zk", "zv", "zhy", "zpl")} for s in range(SPC)]
    K.b_ycat = [{n: P.buf("y" + n + str(s)) for n in ("att", "hy", "pool")} for s in range(SPC)]
    K.b_Kf = {"l": P.buf("Kf_l"), "c": P.buf("Kf_c")}
    K.b_out = P.buf("out")
    K.b_dbg = P.buf("dbg")
    K.b_in = P.buf("inputs")

    with ExitStack() as es:
        K.ps = [es.enter_context(nc.psum_tensor(f"ps{i}", [128, 512], F32)) for i in range(8)]
        K.b_ps = P.bufs("ps", 8)
        K.psi = 0
        K.mats_f = es.enter_context(nc.sbuf_tensor("mats_f", [128, 6, 128], F32))
        K.mats_b = es.enter_context(nc.sbuf_tensor("mats_b", [128, 6, 128], BF16))
        K.ppt = es.enter_context(nc.sbuf_tensor("ppt", [128, npp], F32))
        K.bct = es.enter_context(nc.sbuf_tensor("bct", [128, nbc], F32))
        K.mod = es.enter_context(nc.sbuf_tensor("mod", [128, DEPTH * 3 * 96], F32))
        K.modA = es.enter_context(nc.sbuf_tensor("modA", [128, DEPTH * 3 * 2 * 16], F32))
        K.b_const = P.buf("const")
        K.b_mod = P.buf("mod")
        K.b_modA = P.buf("modA")
        K.cst = es.enter_context(nc.sbuf_tensor("cst", [128, 4], F32))
        P.op("dve", lambda e: e.memset(K.cst[:, 0:1], EPS), partial=[K.b_const])
        P.op("dve", lambda e: e.memset(K.cst[:, 1:2], -math.pi), partial=[K.b_const])
        P.op("dve", lambda e: e.memset(K.cst[:, 2:3], 0.0), partial=[K.b_const])
        P.dma("sp", K.mats_f[:], K.mats, K.b_const)
        P.dma("pool", K.mats_b[:], K.mats, K.b_const)
        P.dma("sp", K.ppt[:], K.pp, K.b_const)
        P.dma("sp", K.bct[:], K.bc, K.b_const)

        phases = [phase_precast, phase_mod]
        for l in range(DEPTH):
            phases.append(lambda K, l=l: phase_filters(K, l))
            for s in range(SPC):
                phases.append(lambda K, l=l, s=s: phase_A(K, l, s))
                phases.append(lambda K, l=l, s=s: phase_att(K, l, s))
                phases.append(lambda K, l=l, s=s: phase_hy(K, l, s))
                phases.append(lambda K, l=l, s=s: phase_pool(K, l, s))
                phases.append(lambda K, l=l, s=s: phase_B(K, l, s))
        import os as _os
        only = _os.environ.get("PH_ONLY")
        only = [int(v) for v in only.split(",")] if only else None
        K.cut = int(_os.environ.get("KB_CUT", "0"))
        for i, ph in enumerate(phases):
            if stop_after is not None and i >= stop_after:
                break
            if only is not None and i not in only:
                continue
            P.push_scope()
            with ExitStack() as pes:
                K.es = pes
                ph(K)
            P.fence()
            P.pop_scope()
        P.emit()
    return nc, P


def ppc(K, name, i=0, n=1, rows=128):
    o, w = K.pp_off[name]
    return K.ppt[0:rows, o + i:o + i + n]


def nextps(K):
    i = K.psi
    K.psi = (K.psi + 1) % 8
    return K.ps[i], K.b_ps[i]


def sb(K, name, shape, dt):
    K.uid = getattr(K, "uid", 0) + 1
    return K.es.enter_context(K.nc.sbuf_tensor(f"{name}_u{K.uid}", list(shape), dt))


def phase_precast(K):
    P = K.P
    for l in range(DEPTH):
        for n in ("w_in", "w_att_o", "w_hy_o", "w_pool_o", "w_out", "mlp_w1", "mlp_w2"):
            rows, cols = K.WSH[n]
            src = getattr(K, n)[l]
            RB = 128
            for r0 in range(0, rows, RB):
                o = P.dma("pool", K.wb[n][l][r0:r0 + RB, :], src[r0:r0 + RB, :], K.b_wb[n][l], reads=[K.b_in])
                o.nofence = True


def phase_transpose_in(K):
    P = K.P
    ident = K.mats_f[:, 0, :]
    NS = 2
    xin = [sb(K, f"t0_in{i}", [128, 4, D], F32) for i in range(NS)]
    b_xin = P.bufs("t0_in", NS)
    xo = [sb(K, f"t0_out{i}", [128, 16, 512], F32) for i in range(NS)]
    b_xo = P.bufs("t0_out", NS)
    it = 0
    for s in range(SPC):
        for (src, n_tok, tok0) in ((K.x[s], L, 0), (K.ctx[s], LC, L)):
            for t0 in range(0, n_tok, 512):
                T = min(512, n_tok - t0)
                nsub = T // 128
                i = it % NS
                it += 1
                P.dma("sp", xin[i][:, 0:nsub, :], src[t0:t0 + T, :].rearrange("(a p) d -> p a d", p=128),
                      b_xin[i], reads=[K.b_in], partial=False)
                for c in range(16):
                    ps, bps = nextps(K)
                    for a in range(nsub):
                        P.mm(lambda e, ps=ps, a=a, c=c, i=i: e.transpose(
                            ps[:, a * 128:(a + 1) * 128], xin[i][:, a, c * 128:(c + 1) * 128], ident),
                            [b_xin[i], K.b_const], bps, a == 0, a == nsub - 1)
                    eng = "dve" if c % 2 == 0 else "act"
                    if eng == "dve":
                        P.op("dve", lambda e, ps=ps, c=c, i=i, T=T: e.tensor_copy(xo[i][:, c, 0:T], ps[:, 0:T]),
                             reads=[bps], partial=[b_xo[i]])
                    else:
                        P.op("act", lambda e, ps=ps, c=c, i=i, T=T: e.copy(xo[i][:, c, 0:T], ps[:, 0:T]),
                             reads=[bps], partial=[b_xo[i]])
                P.dma("sp", K.xT[0][s].rearrange("(c p) t -> p c t", p=128)[:, :, tok0 + t0:tok0 + t0 + T],
                      xo[i][:, :, 0:T], K.b_xT[0][s], reads=[b_xo[i]])


def phase_mod(K):
    P = K.P
    nc = K.nc
    sc = sb(K, "pm_sc", [128, 48], F32)
    b_sc = P.buf("pm_sc")
    P.op("act", lambda e: e.activation(sc[:], ppc(K, "cvec", 0, 48), AF.Silu), reads=[K.b_const], writes=[b_sc])
    NS = 2
    wt = [sb(K, f"pm_w{i}", [128, 16, 512], F32) for i in range(NS)]
    b_wt = P.bufs("pm_w", NS)
    it = 0
    for l in range(DEPTH):
        for nb in range(24):
            i = it % NS
            it += 1
            P.dma("sp", wt[i][:], K.w_mod[l].rearrange("(k p) n -> p k n", p=128)[:, :, nb * 512:(nb + 1) * 512],
                  b_wt[i], reads=[K.b_in], partial=False)
            ps, bps = nextps(K)
            for j in range(4):
                for k in range(16):
                    P.mm(lambda e, ps=ps, i=i, j=j, k=k: e.matmul(
                        ps[:, j * 4:j * 4 + 3], wt[i][:, k, j * 128:(j + 1) * 128], sc[:, k * 3:k * 3 + 3],
                        start=(k == 0), stop=(k == 15)),
                        [b_wt[i], b_sc], bps, j == 0 and k == 0, j == 3 and k == 15)
            for j in range(4):
                n = nb * 4 + j
                for who in range(3):
                    col = (l * 3 + who) * 96 + n
                    P.op("dve", lambda e, ps=ps, j=j, who=who, col=col, l=l, n=n: e.tensor_tensor(
                        K.mod[:, col:col + 1], ps[:, j * 4 + who:j * 4 + who + 1], ppc(K, f"bmod{l}", n), ALU.add),
                        reads=[bps, K.b_const], partial=[K.b_mod])
    for l in range(DEPTH):
        for who in range(3):
            base = (l * 3 + who) * 96
            for which, (gname, scoff) in enumerate(((f"n1g{l}", 16), (f"n2g{l}", 64))):
                o = ((l * 3 + who) * 2 + which) * 16
                P.op("dve", lambda e, base=base, scoff=scoff, gname=gname, o=o: e.scalar_tensor_tensor(
                    K.modA[:, o:o + 16], K.mod[:, base + scoff:base + scoff + 16], 1.0, ppc(K, gname, 0, 16),
                    ALU.add, ALU.mult), reads=[K.b_mod, K.b_const], partial=[K.b_modA])
    P.dma("sp", K.moddbg, K.mod[:], K.b_dbg, reads=[K.b_mod])


def modcol(K, l, who, part, c):
    col = (l * 3 + who) * 96 + part * 16 + c
    return K.mod[:, col:col + 1]


def modAcol(K, l, who, which, c):
    o = ((l * 3 + who) * 2 + which) * 16 + c
    return K.modA[:, o:o + 1]


def wring(K, n=3):
    K.wr = [sb(K, f"wr{i}", [128, 8192], BF16) for i in range(n)]
    K.b_wr = K.P.bufs("wr", n)
    K.wri = 0


def wload(K, src_ap, kc, ncol, dep):
    i = K.wri
    K.wri = (K.wri + 1) % len(K.wr)
    view = K.wr[i][:, 0:kc * ncol].rearrange("p (k n) -> p k n", k=kc)
    K.P.dma("sp", view, src_ap, K.b_wr[i], reads=[dep], partial=False)
    return view, K.b_wr[i]


def rsqrt_from_ps(K, out, b_out, ps, bps, scale, rows=128):
    P = K.P
    P.op("act", lambda e: e.activation(out, ps, AF.Ln, bias=K.cst[0:rows, 0:1], scale=scale),
         reads=[bps, K.b_const], writes=[b_out])
    P.op("act", lambda e: e.activation(out, out, AF.Exp, scale=-0.5), reads=[b_out], writes=[b_out])


def make_h(K, xt, b_xt, h, b_h, T, l, who, which, tmp):
    P = K.P
    ones_b = K.mats_b[:, 1, :]
    sq, b_sq, rs, b_rs, t32, b_t32 = tmp["sq"], tmp["b_sq"], tmp["rs"], tmp["b_rs"], tmp["t32"], tmp["b_t32"]
    P.op("act", lambda e: e.activation(sq[:, :, 0:T], xt[:, :, 0:T], AF.Square), reads=[b_xt], writes=[b_sq])
    ps, bps = nextps(K)
    for c in range(16):
        P.mm(lambda e, c=c: e.matmul(ps[:, 0:T], ones_b, sq[:, c, 0:T], start=(c == 0), stop=(c == 15)),
             [b_sq, K.b_const], bps, c == 0, c == 15)
    rsqrt_from_ps(K, rs[:, 0:T], b_rs, ps[:, 0:T], bps, 1.0 / D)
    sh_part = 0 if which == 0 else 3
    for c in range(16):
        j = c % 2
        P.op("dve", lambda e, c=c, j=j: e.scalar_tensor_tensor(
            t32[j][:, 0:T], xt[:, c, 0:T], modAcol(K, l, who, which, c), rs[:, 0:T], ALU.mult, ALU.mult),
            reads=[b_xt, b_rs, K.b_modA], writes=[b_t32[j]])
        P.op("act", lambda e, c=c, j=j: e.activation(
            h[:, c, 0:T], t32[j][:, 0:T], AF.Identity, bias=modcol(K, l, who, sh_part, c), scale=1.0),
            reads=[b_t32[j], K.b_mod], partial=[b_h])


def h_tmp(K, pfx):
    P = K.P
    return dict(sq=sb(K, pfx + "sq", [128, 16, 512], BF16), b_sq=P.buf(pfx + "sq"),
                rs=sb(K, pfx + "rs", [128, 512], F32), b_rs=P.buf(pfx + "rs"),
                t32=[sb(K, pfx + f"t32{j}", [128, 512], F32) for j in range(2)], b_t32=P.bufs(pfx + "t32", 2))


def tiles_of(l, with_ctx=True):
    t = [(t0, 512) for t0 in range(0, L, 512)]
    if with_ctx:
        t.append((L, LC))
    return t


def phase_A(K, l, s):
    P = K.P
    wring(K, 3)
    tmp = h_tmp(K, "pa_")
    xt = [sb(K, f"pa_x{i}", [128, 16, 512], F32) for i in range(2)]
    b_xt = P.bufs("pa_x", 2)
    h = [sb(K, f"pa_h{i}", [128, 16, 512], BF16) for i in range(2)]
    b_h = P.bufs("pa_h", 2)
    st = [sb(K, f"pa_st{i}", [128, 4, 512], BF16) for i in range(2)]
    b_st = P.bufs("pa_st", 2)
    vst = [sb(K, f"pa_vst{i}", [128, 4, 65], BF16) for i in range(2)]
    b_vst = P.bufs("pa_vst", 2)
    for i in range(2):
        P.op("dve", lambda e, i=i: e.memset(vst[i][:], 1.0), writes=[b_vst[i]])
    xsrc = K.xT[l % 2][s].rearrange("(c p) t -> p c t", p=128)
    win = K.wb["w_in"][l].rearrange("(k p) n -> p k n", p=128)
    b_win = K.b_wb["w_in"][l]
    sti = 0
    vsi = 0
    tl = tiles_of(l)

    def prep(ti):
        t0, T = tl[ti]
        who = 2 if t0 >= L else s
        i = ti % 2
        P.dma("act", xt[i][:, :, 0:T], xsrc[:, :, t0:t0 + T], b_xt[i], reads=[K.b_xT[l % 2][s]], partial=False)
        make_h(K, xt[i], b_xt[i], h[i], b_h[i], T, l, who, 0, tmp)

    prep(0)
    for ti, (t0, T) in enumerate(tl):
        is_ctx = t0 >= L
        who = 2 if is_ctx else s
        i = ti % 2
        nblk = 0
        prepped = False
        blocks = [(0, "zq", 0), (512, "zq", 512), (1024, "kv", 0), (1536, "zhy", 0), (2048, "zhy", 512),
                  (2560, "zhy", 1024), (3072, "zpl", 0)]
        for (col0, dst, r0) in blocks:
            if is_ctx and l == DEPTH - 1 and dst != "kv":
                continue
            if nblk == 2 and ti + 1 < len(tl):
                prep(ti + 1)
                prepped = True
            nblk += 1
            wv, b_w = wload(K, win[:, :, col0:col0 + 512], 16, 512, b_win)
            nj = 2 if dst == "kv" else 4
            si = sti % 2
            sti += 1
            for j in range(nj):
                ps, bps = nextps(K)
                for k in range(16):
                    P.mm(lambda e, ps=ps, wv=wv, j=j, k=k, i=i, T=T: e.matmul(
                        ps[:, 0:T], wv[:, k, j * 128:(j + 1) * 128], h[i][:, k, 0:T], start=(k == 0), stop=(k == 15)),
                        [b_w, b_h[i]], bps, k == 0, k == 15)
                P.op("act", lambda e, ps=ps, si=si, j=j, T=T: e.copy(st[si][:, j, 0:T], ps[:, 0:T]),
                     reads=[bps], partial=[b_st[si]])
            dname = "zk" if dst == "kv" else dst
            dten = getattr(K, dname)[s].rearrange("(c p) t -> p c t", p=128)
            c0 = r0 // 128
            P.dma("act", dten[:, c0:c0 + nj, t0:t0 + T], st[si][:, 0:nj, 0:T], K.b_z[s][dname], reads=[b_st[si]])
            if dst == "kv":
                for a in range(T // 128):
                    ps, bps = nextps(K)
                    for k in range(16):
                        P.mm(lambda e, ps=ps, wv=wv, a=a, k=k, i=i: e.matmul(
                            ps[:, 0:256], h[i][:, k, a * 128:(a + 1) * 128], wv[:, k, 256:512],
                            start=(k == 0), stop=(k == 15)), [b_w, b_h[i]], bps, k == 0, k == 15)
                    vi = vsi % 2
                    vsi += 1
                    P.op("dve", lambda e, ps=ps, vi=vi: e.tensor_copy(
                        vst[vi][:, :, 0:64], ps[:, 0:256].rearrange("p (g d) -> p g d", g=4)),
                        reads=[bps], partial=[b_vst[vi]])
                    P.dma("act", K.zv[s][t0 + a * 128:t0 + (a + 1) * 128, :].rearrange("p (g d) -> p g d", g=4),
                          vst[vi][:], K.b_z[s]["zv"], reads=[b_vst[vi]])
        if not prepped and ti + 1 < len(tl):
            prep(ti + 1)


def phase_filters(K, l):
    with ExitStack() as es2:
        K.es = es2
        filters_for(K, l, L, K.zz_l, K.dd_l, K.F_l, K.Kf_l, K.b_Kf["l"], "fl")
    K.P.fence()
    if l == 0:
        with ExitStack() as es2:
            K.es = es2
            filters_for(K, l, LC, K.zz_c, K.dd_c, K.F_c, K.Kf_c, K.b_Kf["c"], "fc")


def filters_for(K, l, n_tok, zz, dd, Fm, Kdst, b_Kdst, pfx):
    P = K.P
    nT = n_tok // 128
    TW = min(512, n_tok)
    w0 = sb(K, pfx + "w0", [33, 64], F32)
    w1 = sb(K, pfx + "w1", [64, 2, 64], F32)
    w2 = sb(K, pfx + "w2", [64, 2048], F32)
    b_w = P.buf(pfx + "w")
    P.dma("sp", w0[:], K.filt_w0[l], b_w, reads=[K.b_in])
    P.dma("sp", w1[:], K.filt_w1[l].rearrange("i k n -> k i n"), b_w, reads=[K.b_in])
    P.dma("sp", w2[:], K.filt_w2[l], b_w, reads=[K.b_in])
    zt = sb(K, pfx + "zt", [33, n_tok], F32)
    b_zt = P.buf(pfx + "zt")
    hb = sb(K, pfx + "hb", [128, 2, nT, 512], BF16)
    b_hb = P.buf(pfx + "hb")
    AB = sb(K, pfx + "AB", [128, 2, 2, nT, 512], BF16)
    b_AB = P.buf(pfx + "AB")
    hid = [sb(K, pfx + f"hid{i}", [64, TW], F32) for i in range(2)]
    b_hid = P.bufs(pfx + "hid", 2)
    arg = sb(K, pfx + "arg", [64, TW], F32)
    b_arg = P.buf(pfx + "arg")
    argi = sb(K, pfx + "argi", [64, TW], mybir.dt.int32)
    b_argi = P.buf(pfx + "argi")
    arg2 = sb(K, pfx + "arg2", [64, TW], F32)
    b_arg2 = P.buf(pfx + "arg2")
    dec = [sb(K, pfx + f"dec{i}", [128, 512], F32) for i in range(2)]
    b_dec = P.bufs(pfx + "dec", 2)
    hf = [sb(K, pfx + f"hf{i}", [128, 512], F32) for i in range(2)]
    b_hf = P.bufs(pfx + "hf", 2)
    fbn = f"fb{l}"
    di = 0
    for direction in (1, 0):
        P.dma("sp", zt[:], zz[direction], b_zt, reads=[K.b_in], partial=False)
        for t0 in range(0, n_tok, TW):
            cur_in, b_cur_in, kin = zt[:, t0:t0 + TW], b_zt, 33
            for layer in range(3):
                ps, bps = nextps(K)
                if layer == 0:
                    lhsT = w0[:]
                else:
                    lhsT = w1[:, layer - 1, :]
                P.mm(lambda e, ps=ps, lhsT=lhsT, cur_in=cur_in: e.matmul(ps[0:64, 0:TW], lhsT, cur_in,
                                                                          start=True, stop=True),
                     [b_w, b_cur_in], bps, True, True)
                P.op("dve", lambda e, ps=ps, layer=layer: e.tensor_scalar(
                    arg[:], ps[0:64, 0:TW], ppc(K, fbn, layer, 1, 64), ppc(K, fbn, 3, 1, 64), ALU.add, ALU.mult),
                    reads=[bps, K.b_const], writes=[b_arg])
                P.op("dve", lambda e: e.tensor_single_scalar(argi[:], arg[:], 1.0 / (2.0 * math.pi), ALU.mult),
                     reads=[b_arg], writes=[b_argi])
                P.op("dve", lambda e: e.scalar_tensor_tensor(arg2[:], argi[:], -2.0 * math.pi, arg[:], ALU.mult, ALU.add),
                     reads=[b_arg, b_argi], writes=[b_arg2])
                ho = layer % 2
                P.op("act", lambda e, ho=ho: e.activation(hid[ho][:], arg2[:], AF.Sin, bias=K.cst[0:64, 2:3],
                                                          scale=1.0 - 2e-6),
                     reads=[b_arg2, K.b_const], writes=[b_hid[ho]])
                cur_in, b_cur_in = hid[ho][:], b_hid[ho]
            h3 = cur_in
            for a in range(TW // 128):
                pc = t0 // 128 + a
                d_i = di % 2
                di += 1
                P.dma("sp", dec[d_i][:], dd[direction, pc * 128:(pc + 1) * 128, :], b_dec[d_i], reads=[K.b_in],
                      partial=False)
                for o in range(2):
                    ps, bps = nextps(K)
                    col = o * 1024 + direction * 512
                    P.mm(lambda e, ps=ps, h3=h3, a=a, col=col: e.matmul(
                        ps[:, :], h3[:, a * 128:(a + 1) * 128], w2[:, col:col + 512], start=True, stop=True),
                        [b_w, b_cur_in], bps, True, True)
                    if direction == 1:
                        P.op("dve", lambda e, ps=ps, o=o, pc=pc, d_i=d_i: e.tensor_tensor(
                            hb[:, o, pc, :], ps[:, :], dec[d_i][:], ALU.mult), reads=[bps, b_dec[d_i]], partial=[b_hb])
                    else:
                        P.op("dve", lambda e, ps=ps, o=o, d_i=d_i: e.tensor_tensor(
                            hf[o][:], ps[:, :], dec[d_i][:], ALU.mult), reads=[bps, b_dec[d_i]], writes=[b_hf[o]])
                        P.op("dve", lambda e, o=o, pc=pc: e.tensor_tensor(
                            AB[:, o, 0, pc, :], hf[o][:], hb[:, o, pc, :], ALU.add), reads=[b_hf[o], b_hb],
                            partial=[b_AB])
                        P.op("dve", lambda e, o=o, pc=pc: e.tensor_tensor(
                            AB[:, o, 1, pc, :], hf[o][:], hb[:, o, pc, :], ALU.subtract), reads=[b_hf[o], b_hb],
                            partial=[b_AB])
    M = n_tok
    FW = min(512, M)
    fw = [sb(K, pfx + f"F{i}", [128, nT, FW], BF16) for i in range(3)]
    b_fw = P.bufs(pfx + "F", 3)
    kst = [sb(K, pfx + f"kst{i}", [128, 512], F32) for i in range(2)]
    b_kst = P.bufs(pfx + "kst", 2)
    Fv = Fm.rearrange("(k p) f -> p k f", p=128)
    fi = 0
    ki = 0
    for cs in range(2):
        for f0 in range(0, M, FW):
            i = fi % 3
            fi += 1
            P.dma("sp", fw[i][:], Fv[:, :, cs * M + f0:cs * M + f0 + FW], b_fw[i], reads=[K.b_in], partial=False)
            for fc in range(FW // 128):
                for o in range(2):
                    ps, bps = nextps(K)
                    for k in range(nT):
                        P.mm(lambda e, ps=ps, i=i, k=k, fc=fc, o=o, cs=cs: e.matmul(
                            ps[:, :], fw[i][:, k, fc * 128:(fc + 1) * 128], AB[:, o, cs, k, :],
                            start=(k == 0), stop=(k == nT - 1)), [b_fw[i], b_AB], bps, k == 0, k == nT - 1)
                    kk = ki % 2
                    ki += 1
                    P.op("act", lambda e, ps=ps, kk=kk: e.copy(kst[kk][:], ps[:, :]), reads=[bps], writes=[b_kst[kk]])
                    r0 = f0 + fc * 128
                    P.dma("sp", Kdst[o, cs, r0:r0 + 128, :], kst[kk][:], b_Kdst, reads=[b_kst[kk]])


def phase_hy(K, l, s):
    with ExitStack() as es2:
        K.es = es2
        hyena_for(K, l, s, 0, L, K.F_l, K.FT_l, K.Kf_l, K.b_Kf["l"], "hl")
    if l == 0:
        K.P.fence()
        with ExitStack() as es2:
            K.es = es2
            hyena_for(K, l, s, L, LC, K.F_c, K.FT_c, K.Kf_c, K.b_Kf["c"], "hc")


def hyena_for(K, l, s, tok0, n_tok, Fm, FTm, Kf, b_Kf, pfx):
    P = K.P
    nT = n_tok // 128
    M = n_tok
    nF = M // 128
    ident_b = K.mats_b[:, 0, :]
    tokU = sb(K, pfx + "tokU", [128, nT, 1536], BF16)
    b_tokU = P.buf(pfx + "tokU")
    outer_es = K.es
    pro_es = ExitStack()
    K.es = pro_es
    ub = [sb(K, pfx + f"ub{i}", [128, n_tok + 2], BF16) for i in range(2)]
    b_ub = P.bufs(pfx + "ub", 2)
    for i in range(2):
        P.op("pool", lambda e, i=i: e.memset(ub[i][:, 0:1], 0.0), partial=[b_ub[i]])
        P.op("pool", lambda e, i=i: e.memset(ub[i][:, n_tok + 1:n_tok + 2], 0.0), partial=[b_ub[i]])
    c1 = [sb(K, pfx + f"c1{i}", [128, n_tok], F32) for i in range(2)]
    b_c1 = P.bufs(pfx + "c1", 2)
    uc = [sb(K, pfx + f"uc{i}", [128, n_tok], BF16) for i in range(2)]
    b_uc = P.bufs(pfx + "uc", 2)
    zsrc = K.zhy[s].rearrange("(c p) t -> p c t", p=128)
    hcw, hcb = f"hcw{l}", f"hcb{l}"
    G = min(8, nT)
    for c in range(12):
        i = c % 2
        P.dma("sp", ub[i][:, 1:n_tok + 1], zsrc[:, c, tok0:tok0 + n_tok], b_ub[i], reads=[K.b_z[s]["zhy"]])
        P.op("dve", lambda e, i=i, c=c: e.tensor_scalar(
            c1[0][:], ub[i][:, 1:n_tok + 1], ppc(K, hcw, c * 3 + 1), ppc(K, hcb, c), ALU.mult, ALU.add),
            reads=[b_ub[i], K.b_const], writes=[b_c1[0]])
        P.op("dve", lambda e, i=i, c=c: e.scalar_tensor_tensor(
            c1[1][:], ub[i][:, 0:n_tok], ppc(K, hcw, c * 3 + 0), c1[0][:], ALU.mult, ALU.add),
            reads=[b_ub[i], b_c1[0], K.b_const], writes=[b_c1[1]])
        P.op("dve", lambda e, i=i, c=c: e.scalar_tensor_tensor(
            uc[i][:], ub[i][:, 2:n_tok + 2], ppc(K, hcw, c * 3 + 2), c1[1][:], ALU.mult, ALU.add),
            reads=[b_ub[i], b_c1[1], K.b_const], writes=[b_uc[i]])
        for g0 in range(0, nT, G):
            ps, bps = nextps(K)
            psb = ps[:].bitcast(BF16)
            for a in range(G):
                P.mm(lambda e, psb=psb, a=a, i=i, g0=g0: e.transpose(
                    psb[:, a * 128:(a + 1) * 128], uc[i][:, (g0 + a) * 128:(g0 + a + 1) * 128], ident_b),
                    [b_uc[i], K.b_const], bps, a == 0, a == G - 1)
            P.op("act", lambda e, psb=psb, g0=g0, c=c: e.copy(
                tokU[:, g0:g0 + G, c * 128:(c + 1) * 128], psb[:, 0:G * 128].rearrange("p (a n) -> p a n", a=G)),
                reads=[bps], partial=[b_tokU])
    pro_es.close()
    K.es = outer_es
    P.fence()
    FW = min(256, M)
    fw = [sb(K, pfx + f"F{i}", [128, nT, FW], BF16) for i in range(4)]
    b_fw = P.bufs(pfx + "F", 4)
    TWI = 128
    ftw = [sb(K, pfx + f"FT{i}", [128, 2 * nF, TWI], BF16) for i in range(2)]
    b_ftw = P.bufs(pfx + "FT", 2)
    kt = [sb(K, pfx + f"kt{i}", [128, 2, 512], F32) for i in range(2)]
    b_kt = P.bufs(pfx + "kt", 2)
    tmp = [sb(K, pfx + f"tmp{i}", [128, 512], F32) for i in range(4)]
    b_tmp = P.bufs(pfx + "tmp", 4)
    Pf = sb(K, pfx + "Pf", [128, 2 * nF, 512], BF16)
    b_Pf = P.buf(pfx + "Pf")
    z1 = sb(K, pfx + "z1", [128, nT, 512], BF16)
    b_z1 = P.buf(pfx + "z1")
    z2 = sb(K, pfx + "z2", [128, nT, 512], BF16)
    b_z2 = P.buf(pfx + "z2")
    Fv = Fm.rearrange("(k p) f -> p k f", p=128)
    FTv = FTm.rearrange("(k p) t -> p k t", p=128)
    nbc_sink = 2 * 16
    fi = 0
    ki = 0
    fti = 0
    for o in range(2):
        zin, b_zin = (tokU[:, :, 0:512], b_tokU) if o == 0 else (z1[:], b_z1)
        gate = tokU[:, :, 512 * (o + 1):512 * (o + 2)]
        zout, b_zout = (z1, b_z1) if o == 0 else (z2, b_z2)
        dcol = nbc_sink + (l * 2 + o) * 512
        for f0 in range(0, M, FW):
            ia = fi % 4
            ib = (fi + 1) % 4
            fi += 2
            P.dma("sp", fw[ia][:], Fv[:, :, f0:f0 + FW], b_fw[ia], reads=[K.b_in], partial=False)
            P.dma("sp", fw[ib][:], Fv[:, :, M + f0:M + f0 + FW], b_fw[ib], reads=[K.b_in], partial=False)
            for fc in range(FW // 128):
                fch = (f0 // 128) + fc
                kk = ki % 2
                ki += 1
                P.dma("sp", kt[kk][:], Kf[o, :, fch * 128:(fch + 1) * 128, :].rearrange("c p n -> p c n"),
                      b_kt[kk], reads=[b_Kf], partial=False)
                psc, bpsc = nextps(K)
                pss, bpss = nextps(K)
                for (pp_, bpp_, iw) in ((psc, bpsc, ia), (pss, bpss, ib)):
                    for k in range(nT):
                        P.mm(lambda e, pp_=pp_, iw=iw, k=k, fc=fc, zin=zin: e.matmul(
                            pp_[:, :], fw[iw][:, k, fc * 128:(fc + 1) * 128], zin[:, k, :],
                            start=(k == 0), stop=(k == nT - 1)), [b_fw[iw], b_zin], bpp_, k == 0, k == nT - 1)
                P.op("dve", lambda e, psc=psc, kk=kk: e.tensor_tensor(tmp[0][:], psc[:, :], kt[kk][:, 0, :], ALU.mult),
                     reads=[bpsc, b_kt[kk]], writes=[b_tmp[0]])
                P.op("dve", lambda e, pss=pss, kk=kk: e.tensor_tensor(tmp[1][:], pss[:, :], kt[kk][:, 1, :], ALU.mult),
                     reads=[bpss, b_kt[kk]], writes=[b_tmp[1]])
                P.op("pool", lambda e, fch=fch: e.tensor_tensor(Pf[:, fch, :], tmp[0][:], tmp[1][:], ALU.subtract),
                     reads=[b_tmp[0], b_tmp[1]], partial=[b_Pf])
                P.op("dve", lambda e, psc=psc, kk=kk: e.tensor_tensor(tmp[2][:], psc[:, :], kt[kk][:, 1, :], ALU.mult),
                     reads=[bpsc, b_kt[kk]], writes=[b_tmp[2]])
                P.op("dve", lambda e, pss=pss, kk=kk: e.tensor_tensor(tmp[3][:], pss[:, :], kt[kk][:, 0, :], ALU.mult),
                     reads=[bpss, b_kt[kk]], writes=[b_tmp[3]])
                P.op("pool", lambda e, fch=fch: e.tensor_tensor(Pf[:, nF + fch, :], tmp[2][:], tmp[3][:], ALU.add),
                     reads=[b_tmp[2], b_tmp[3]], partial=[b_Pf])
        for t0 in range(0, n_tok, TWI):
            it = fti % 2
            fti += 1
            P.dma("sp", ftw[it][:], FTv[:, :, t0:t0 + TWI], b_ftw[it], reads=[K.b_in], partial=False)
            for a in range(TWI // 128):
                tch = t0 // 128 + a
                ps, bps = nextps(K)
                for f in range(2 * nF):
                    P.mm(lambda e, ps=ps, it=it, f=f, a=a: e.matmul(
                        ps[:, :], ftw[it][:, f, a * 128:(a + 1) * 128], Pf[:, f, :],
                        start=(f == 0), stop=(f == 2 * nF - 1)), [b_ftw[it], b_Pf], bps, f == 0, f == 2 * nF - 1)
                P.op("pool", lambda e, tch=tch, zin=zin, dcol=dcol: e.tensor_tensor(
                    tmp[0][:], zin[:, tch, :], K.bct[:, dcol:dcol + 512], ALU.mult),
                    reads=[b_zin, K.b_const], writes=[b_tmp[0]])
                P.op("dve", lambda e, ps=ps: e.scalar_tensor_tensor(
                    tmp[1][:], ps[:, :], 1.0 / n_tok, tmp[0][:], ALU.mult, ALU.add),
                    reads=[bps, b_tmp[0]], writes=[b_tmp[1]])
                P.op("pool", lambda e, tch=tch, gate=gate, zout=zout: e.tensor_tensor(
                    zout[:, tch, :], tmp[1][:], gate[:, tch, :], ALU.mult),
                    reads=[b_tmp[1], b_tokU], partial=[b_zout])
    TG = min(4, nT)
    yst = [sb(K, pfx + f"yst{i}", [128, 4, TG * 128], BF16) for i in range(2)]
    b_yst = P.bufs(pfx + "yst", 2)
    ydst = K.ycat[s].rearrange("(c p) t -> p c t", p=128)
    yi = 0
    for g0 in range(0, nT, TG):
        i = yi % 2
        yi += 1
        for c4 in range(4):
            ps, bps = nextps(K)
            psb = ps[:].bitcast(BF16)
            for a in range(TG):
                P.mm(lambda e, psb=psb, a=a, g0=g0, c4=c4: e.transpose(
                    psb[:, a * 128:(a + 1) * 128], z2[:, g0 + a, c4 * 128:(c4 + 1) * 128], ident_b),
                    [b_z2, K.b_const], bps, a == 0, a == TG - 1)
            P.op("act", lambda e, psb=psb, i=i, c4=c4: e.copy(yst[i][:, c4, :], psb[:, 0:TG * 128]),
                 reads=[bps], partial=[b_yst[i]])
        P.dma("sp", ydst[:, 8:12, tok0 + g0 * 128:tok0 + (g0 + TG) * 128], yst[i][:], K.b_ycat[s]["hy"],
              reads=[b_yst[i]])


def phase_pool(K, l, s):
    with ExitStack() as es2:
        K.es = es2
        pool_for(K, l, s, 0, L, K.pinv_l, "pl")
    if l == 0:
        K.P.fence()
        with ExitStack() as es2:
            K.es = es2
            pool_for(K, l, s, L, LC, K.pinv_c, "pc")


def pool_for(K, l, s, tok0, n_tok, pinv, pfx):
    P = K.P
    PADW = n_tok + 32
    EW = n_tok + 16
    u = sb(K, pfx + "u", [128, PADW], BF16)
    b_u = P.buf(pfx + "u")
    inv = sb(K, pfx + "inv", [128, n_tok], F32)
    b_inv = P.buf(pfx + "inv")
    wa = sb(K, pfx + "wa", [128, EW], F32)
    wb = sb(K, pfx + "wb", [128, EW], F32)
    b_wa, b_wb = P.buf(pfx + "wa"), P.buf(pfx + "wb")
    diff = sb(K, pfx + "diff", [128, n_tok], BF16)
    b_diff = P.buf(pfx + "diff")
    pw = sb(K, pfx + "pw", [128, 4, 128], BF16)
    b_pw = P.buf(pfx + "pw")
    yst = [sb(K, pfx + f"yst{i}", [128, 512], BF16) for i in range(2)]
    b_yst = P.bufs(pfx + "yst", 2)
    P.dma("pool", pw[:], K.pool_w[l].rearrange("g c d -> c g d"), b_pw, reads=[K.b_in])
    P.op("dve", lambda e: e.memset(u[:, 0:16], 0.0), partial=[b_u])
    P.op("dve", lambda e: e.memset(u[:, 16 + n_tok:PADW], 0.0), partial=[b_u])
    zsrc = K.zpl[s].rearrange("(c p) t -> p c t", p=128)
    ydst = K.ycat[s].rearrange("(c p) t -> p c t", p=128)
    yi = 0
    for g in range(4):
        P.dma("sp", u[:, 16:16 + n_tok], zsrc[:, g, tok0:tok0 + n_tok], b_u, reads=[K.b_z[s]["zpl"]])
        P.dma("sp", inv[:], pinv[:, g, :], b_inv, reads=[K.b_in], partial=False)
        P.op("dve", lambda e: e.tensor_tensor(wa[:], u[:, 7:7 + EW], u[:, 8:8 + EW], ALU.add),
             reads=[b_u], writes=[b_wa])
        cur, b_cur, oth, b_oth = wa, b_wa, wb, b_wb
        step = 1
        for lvl in range(g):
            lo = (1, 3, 7)[lvl]
            P.op("dve", lambda e, cur=cur, oth=oth, lo=lo, step=step: e.tensor_tensor(
                oth[:, lo:EW - lo], cur[:, lo - step:EW - lo - step], cur[:, lo + step:EW - lo + step], ALU.add),
                reads=[b_cur], writes=[b_oth])
            cur, b_cur, oth, b_oth = oth, b_oth, cur, b_cur
            step *= 2
        P.op("dve", lambda e, cur=cur, oth=oth: e.tensor_tensor(oth[:, 0:n_tok], cur[:, 8:8 + n_tok], inv[:], ALU.mult),
             reads=[b_cur, b_inv], writes=[b_oth])
        P.op("dve", lambda e, oth=oth: e.tensor_tensor(diff[:], oth[:, 0:n_tok], u[:, 16:16 + n_tok], ALU.subtract),
             reads=[b_oth, b_u], writes=[b_diff])
        for t0 in range(0, n_tok, 512):
            T = min(512, n_tok - t0)
            ps, bps = nextps(K)
            P.mm(lambda e, ps=ps, g=g, t0=t0, T=T: e.matmul(ps[:, 0:T], pw[:, g, :], diff[:, t0:t0 + T],
                                                           start=True, stop=True), [b_pw, b_diff], bps, True, True)
            i = yi % 2
            yi += 1
            P.op("act", lambda e, ps=ps, i=i, g=g, T=T: e.activation(
                yst[i][:, 0:T], ps[:, 0:T], AF.Copy, scale=ppc(K, f"psc{l}", g)), reads=[bps, K.b_const],
                writes=[b_yst[i]])
            P.dma("sp", ydst[:, 12 + g, tok0 + t0:tok0 + t0 + T], yst[i][:, 0:T], K.b_ycat[s]["pool"],
                  reads=[b_yst[i]])


def phase_att(K, l, s):
    P = K.P
    with_ctx_q = (l == 0)
    ident_b = K.mats_b[:, 0, :]
    blk_b = K.mats_b[:, 2, :]
    perm_b = K.mats_b[:, 3, :]
    maskA = K.mats_b[:, 4, :]
    maskB = K.mats_b[:, 5, :]
    NTI = 5
    qT = sb(K, "at_q", [128, 8, TT], BF16)
    b_q = [[P.buf(f"at_q{c}_{t}") for t in range(NTI)] for c in range(8)]
    kd = [sb(K, f"at_k{g}", [128, TT], BF16) for g in range(4)]
    b_k = [[P.buf(f"at_k{g}_{t}") for t in range(NTI)] for g in range(4)]
    Va = sb(K, "at_v", [128, 18, 260], BF16)
    b_v = P.buf("at_v")
    rC = sb(K, "at_rC", [128, L], F32)
    rS = sb(K, "at_rS", [128, L], F32)
    b_rope = P.buf("at_rope")
    esink = sb(K, "at_es", [128, 16], F32)
    b_es = P.buf("at_es")
    P.dma("sp", rC[:], K.ropeC, b_rope, reads=[K.b_in])
    P.dma("sp", rS[:], K.ropeS, b_rope, reads=[K.b_in])
    P.op("act", lambda e: e.activation(esink[:], K.bct[:, l * 16:(l + 1) * 16], AF.Exp), reads=[K.b_const],
         writes=[b_es])
    P.dma("sp", Va[:], K.zv[s].rearrange("(c p) n -> p c n", p=128), b_v, reads=[K.b_z[s]["zv"]])
    zq = K.zq[s].rearrange("(c p) t -> p c t", p=128)
    tiles = [(t0, 512) for t0 in range(0, L, 512)] + [(L, LC)]
    for ti, (t0, T) in enumerate(tiles):
        for c in range(8):
            if ti == 4 and not with_ctx_q:
                continue
            P.dma("sp", qT[:, c, t0:t0 + T], zq[:, c, t0:t0 + T], b_q[c][ti], reads=[K.b_z[s]["zq"]])
        for g in range(4):
            for half in range(2):
                P.dma("sp", kd[g][half * 64:(half + 1) * 64, t0:t0 + T], K.zk[s][g * 64:(g + 1) * 64, t0:t0 + T],
                      b_k[g][ti], reads=[K.b_z[s]["zk"]])
    sq = [sb(K, f"at_sq{i}", [128, 512], BF16) for i in range(2)]
    b_sq = P.bufs("at_sq", 2)
    rs = [sb(K, f"at_rs{i}", [128, 512], F32) for i in range(2)]
    b_rs = P.bufs("at_rs", 2)
    qn = [sb(K, f"at_qn{i}", [128, 512], BF16) for i in range(2)]
    b_qn = P.bufs("at_qn", 2)
    t1 = [sb(K, f"at_t1{i}", [128, 512], F32) for i in range(2)]
    b_t1 = P.bufs("at_t1", 2)
    t2 = [sb(K, f"at_t2{i}", [128, 512], F32) for i in range(2)]
    b_t2 = P.bufs("at_t2", 2)
    it = 0
    items = []
    for ti, (t0, T) in enumerate(tiles):
        for c in range(8):
            if ti == 4 and not with_ctx_q:
                continue
            items.append((qT[:, c, t0:t0 + T], b_q[c][ti], f"qg{l}", ti, t0, T))
        for g in range(4):
            items.append((kd[g][:, t0:t0 + T], b_k[g][ti], f"kg{l}", ti, t0, T))
    for (xa, b_x, gname, ti, t0, T) in items:
        i = it % 2
        it += 1
        P.op("act", lambda e, xa=xa, i=i, T=T: e.activation(sq[i][:, 0:T], xa, AF.Square), reads=[b_x],
             writes=[b_sq[i]])
        ps, bps = nextps(K)
        P.mm(lambda e, ps=ps, i=i, T=T: e.matmul(ps[:, 0:T], blk_b, sq[i][:, 0:T], start=True, stop=True),
             [b_sq[i], K.b_const], bps, True, True)
        rsqrt_from_ps(K, rs[i][:, 0:T], b_rs[i], ps[:, 0:T], bps, 1.0 / 64)
        if ti == 4:
            P.op("dve", lambda e, xa=xa, i=i, T=T, gname=gname: e.scalar_tensor_tensor(
                xa, xa, ppc(K, gname), rs[i][:, 0:T], ALU.mult, ALU.mult), reads=[b_x, b_rs[i], K.b_const],
                writes=[b_x])
            continue
        P.op("dve", lambda e, xa=xa, i=i, T=T, gname=gname: e.scalar_tensor_tensor(
            qn[i][:, 0:T], xa, ppc(K, gname), rs[i][:, 0:T], ALU.mult, ALU.mult), reads=[b_x, b_rs[i], K.b_const],
            writes=[b_qn[i]])
        ps2, bps2 = nextps(K)
        P.mm(lambda e, ps2=ps2, i=i, T=T: e.matmul(ps2[:, 0:T], perm_b, qn[i][:, 0:T], start=True, stop=True),
             [b_qn[i], K.b_const], bps2, True, True)
        P.op("pool", lambda e, i=i, t0=t0, T=T: e.tensor_tensor(t1[i][:, 0:T], qn[i][:, 0:T], rC[:, t0:t0 + T], ALU.mult),
             reads=[b_qn[i], b_rope], writes=[b_t1[i]])
        P.op("dve", lambda e, ps2=ps2, i=i, t0=t0, T=T: e.tensor_tensor(t2[i][:, 0:T], ps2[:, 0:T], rS[:, t0:t0 + T],
                                                                         ALU.mult),
             reads=[bps2, b_rope], writes=[b_t2[i]])
        P.op("pool", lambda e, xa=xa, i=i, T=T: e.tensor_tensor(xa, t1[i][:, 0:T], t2[i][:, 0:T], ALU.add),
             reads=[b_t1[i], b_t2[i]], writes=[b_x])
    Pt = [sb(K, f"at_P{i}", [128, 5, 128], BF16) for i in range(2)]
    b_P = P.bufs("at_P", 2)
    yt = [sb(K, f"at_y{i}", [128, 16, 64], BF16) for i in range(2)]
    b_y = P.bufs("at_y", 2)
    dsum = [sb(K, f"at_ds{i}", [128, 16], F32) for i in range(2)]
    b_ds = P.bufs("at_ds", 2)
    yst = [sb(K, f"at_yst{i}", [128, 8, 512], BF16) for i in range(2)]
    b_yst = P.bufs("at_yst", 2)
    ydst = K.ycat[s].rearrange("(c p) t -> p c t", p=128)
    blocks = list(range(16)) + ([16, 17] if with_ctx_q else [])
    pi = 0
    sli = 0
    sci = 0
    ysi = 0
    grp_start = 0
    for bi_, n in enumerate(blocks):
        is_cq = n >= 16
        tq = n // 4
        if is_cq:
            loc = []
        else:
            loc = [j for j in (n - 1, n, n + 1) if 0 <= j < 16]
        chunks = [(j, idx) for idx, j in enumerate(loc)] + [(16, 3), (17, 4)]
        yi = bi_ % 2
        for h in range(16):
            g = h // 4
            base = (h % 2) * 64
            c = h // 2
            qa = qT[base:base + 64, c, n * 128:(n + 1) * 128]
            p_i = pi % 2
            pi += 1
            if loc:
                bl = sli % 2
                sli += 1
                Sl, b_Sl = K.ps[bl], K.b_ps[bl]
                ops = []
                for idx, j in enumerate(loc):
                    ops.append((idx, j, None))
                    if j == n - 1:
                        ops.append((idx, j, maskA))
                    elif j == n + 1:
                        ops.append((idx, j, maskB))
                for oi, (idx, j, msk) in enumerate(ops):
                    first, last = oi == 0, oi == len(ops) - 1
                    has_mask = (j != n)
                    if msk is None:
                        P.mm(lambda e, Sl=Sl, idx=idx, j=j, g=g, base=base, qa=qa, has_mask=has_mask: e.matmul(
                            Sl[:, idx * 128:(idx + 1) * 128], kd[g][base:base + 64, j * 128:(j + 1) * 128], qa,
                            start=True, stop=not has_mask), [b_k[g][j // 4], b_q[c][tq]], b_Sl, first, last)
                    else:
                        P.mm(lambda e, Sl=Sl, idx=idx, msk=msk: e.matmul(
                            Sl[:, idx * 128:(idx + 1) * 128], ident_b, msk, start=False, stop=True),
                            [K.b_const], b_Sl, first, last)
                nl = len(loc)
                P.op("act", lambda e, Sl=Sl, p_i=p_i, nl=nl: e.activation(
                    Pt[p_i][:, 0:nl, :], Sl[:, 0:nl * 128].rearrange("p (a n) -> p a n", a=nl), AF.Exp, scale=0.125),
                    reads=[b_Sl], partial=[b_P[p_i]])
            bc_ = 2 + sci % 2
            sci += 1
            Sc, b_Sc = K.ps[bc_], K.b_ps[bc_]
            for ci in range(2):
                j = 16 + ci
                P.mm(lambda e, Sc=Sc, ci=ci, j=j, g=g, base=base, qa=qa: e.matmul(
                    Sc[:, ci * 128:(ci + 1) * 128], kd[g][base:base + 64, j * 128:(j + 1) * 128], qa,
                    start=True, stop=True), [b_k[g][4], b_q[c][tq]], b_Sc, ci == 0, ci == 1)
            P.op("act", lambda e, Sc=Sc, p_i=p_i: e.activation(
                Pt[p_i][:, 3:5, :], Sc[:, 0:256].rearrange("p (a n) -> p a n", a=2), AF.Exp, scale=0.125),
                reads=[b_Sc], partial=[b_P[p_i]])
            ob = 5 + h // 7
            off = (h % 7) * 65
            O, b_O = K.ps[ob], K.b_ps[ob]
            hb_first = (h % 7 == 0)
            hb_last = (h % 7 == 6) or (h == 15)
            for ki_, (j, slot) in enumerate(chunks):
                P.mm(lambda e, O=O, off=off, p_i=p_i, slot=slot, j=j, g=g, ki_=ki_, nch=len(chunks): e.matmul(
                    O[:, off:off + 65], Pt[p_i][:, slot, :], Va[:, j, g * 65:(g + 1) * 65],
                    start=(ki_ == 0), stop=(ki_ == nch - 1)), [b_P[p_i], b_v], b_O,
                    hb_first and ki_ == 0, hb_last and ki_ == len(chunks) - 1)
        for bk, (h0, nh) in enumerate(((0, 7), (7, 7), (14, 2))):
            O, b_O = K.ps[5 + bk], K.b_ps[5 + bk]
            Ov = O[:, 0:nh * 65].rearrange("p (h d) -> p h d", d=65)
            P.op("dve", lambda e, Ov=Ov, h0=h0, nh=nh, yi=yi: e.tensor_tensor(
                dsum[yi][:, h0:h0 + nh], Ov[:, :, 64], esink[:, h0:h0 + nh], ALU.add),
                reads=[b_O, b_es], partial=[b_ds[yi]])
            P.op("dve", lambda e, h0=h0, nh=nh, yi=yi: e.reciprocal(dsum[yi][:, h0:h0 + nh], dsum[yi][:, h0:h0 + nh]),
                 reads=[b_ds[yi]], writes=[b_ds[yi]])
            for hh in range(nh):
                P.op("dve", lambda e, Ov=Ov, hh=hh, h0=h0, yi=yi: e.tensor_scalar(
                    yt[yi][:, h0 + hh, :], Ov[:, hh, 0:64], dsum[yi][:, h0 + hh:h0 + hh + 1], None, ALU.mult),
                    reads=[b_O, b_ds[yi]], partial=[b_y[yi]])
        Tb = K.ps[4][:].bitcast(BF16)
        b_T = K.b_ps[4]
        yflat = yt[yi][:].rearrange("p h d -> p (h d)")
        for c in range(8):
            P.mm(lambda e, Tb=Tb, c=c, yflat=yflat: e.transpose(
                Tb[:, c * 128:(c + 1) * 128], yflat[:, c * 128:(c + 1) * 128], ident_b),
                [b_y[yi], K.b_const], b_T, c == 0, c == 7)
        a = n % 4
        ys = ysi % 2
        P.op("act", lambda e, Tb=Tb, ys=ys, a=a: e.copy(
            yst[ys][:, :, a * 128:(a + 1) * 128], Tb[:, 0:1024].rearrange("p (c n) -> p c n", c=8)),
            reads=[b_T], partial=[b_yst[ys]])
        end_grp = (a == 3) or (n == blocks[-1])
        if end_grp:
            g0 = (n // 4) * 4
            w = (a + 1) * 128
            P.dma("sp", ydst[:, 0:8, g0 * 128:g0 * 128 + w], yst[ys][:, :, 0:w], K.b_ycat[s]["att"],
                  reads=[b_yst[ys]])
            ysi += 1


def phase_B(K, l, s):
    P = K.P
    last = (l == DEPTH - 1)
    wring(K, 3)
    tmp = h_tmp(K, "pb_")
    ident_f = K.mats_f[:, 0, :]
    xt = sb(K, "pb_x", [128, 16, 512], F32)
    b_xt = P.buf("pb_x")
    h = sb(K, "pb_h", [128, 16, 512], BF16)
    b_h = P.buf("pb_h")
    yc = sb(K, "pb_yc", [128, 16, 512], BF16)
    b_yc = P.buf("pb_yc")
    m = sb(K, "pb_m", [128, 16, 512], BF16)
    b_m = P.buf("pb_m")
    gs = [sb(K, f"pb_gs{i}", [128, 512], BF16) for i in range(2)]
    b_gs = P.bufs("pb_gs", 2)
    acc = sb(K, "pb_acc", [128, 4, 512], F32)
    b_acc = P.bufs("pb_acc", 4)
    tr = [sb(K, f"pb_tr{i}", [128, 512], F32) for i in range(2)]
    b_tr = P.bufs("pb_tr", 2)
    ev = [sb(K, f"pb_ev{i}", [128, 512], F32) for i in range(4)]
    b_ev = P.bufs("pb_ev", 4)
    ei_ = 0
    hid = [sb(K, f"pb_hid{i}", [128, 4, 512], BF16) for i in range(2)]
    b_hid = P.bufs("pb_hid", 2)
    b_ost = [P.buf("pb_ost")] * 2
    xsrc = K.xT[l % 2][s].rearrange("(c p) t -> p c t", p=128)
    xdst = K.xT[(l + 1) % 2][s].rearrange("(c p) t -> p c t", p=128)
    ysrc = K.ycat[s].rearrange("(c p) t -> p c t", p=128)
    win = K.wb["w_in"][l].rearrange("(k p) n -> p k n", p=128)
    b_win = K.b_wb["w_in"][l]
    branches = [(GATE_OFF, K.wb["w_att_o"][l].rearrange("(k p) n -> p k n", p=128), 8, 0, K.b_wb["w_att_o"][l]),
                (GATE_OFF + D, K.wb["w_hy_o"][l].rearrange("(k p) n -> p k n", p=128), 4, 8, K.b_wb["w_hy_o"][l]),
                (GATE_OFF + 2 * D, K.wb["w_pool_o"][l].rearrange("(k p) n -> p k n", p=128), 4, 12,
                 K.b_wb["w_pool_o"][l])]
    wout = K.wb["w_out"][l].rearrange("(k p) n -> p k n", p=128)
    w1v = K.wb["mlp_w1"][l].rearrange("(k p) n -> p k n", p=128)
    gi = 0
    ti_ = 0
    hi_ = 0
    oi = 0
    for (t0, T) in tiles_of(l, with_ctx=not last):
        is_ctx = t0 >= L
        who = 2 if is_ctx else s
        P.dma("act", xt[:, :, 0:T], xsrc[:, :, t0:t0 + T], b_xt, reads=[K.b_xT[l % 2][s]], partial=False)
        P.dma("act", yc[:, :, 0:T], ysrc[:, :, t0:t0 + T], b_yc,
              reads=[K.b_ycat[s]["att"], K.b_ycat[s]["hy"], K.b_ycat[s]["pool"]], partial=False)
        make_h(K, xt, b_xt, h, b_h, T, l, who, 0, tmp)
        for J in range(4):
            for r, (gcol, wo_ap, kr, yoff, b_wsrc) in enumerate(branches):
                wg, b_wg = wload(K, win[:, :, gcol + J * 512:gcol + (J + 1) * 512], 16, 512, b_win)
                wo, b_wo = wload(K, wo_ap[:, :, J * 512:(J + 1) * 512], kr, 512, b_wsrc)
                for j in range(4):
                    psg, bpsg = nextps(K)
                    for k in range(16):
                        P.mm(lambda e, psg=psg, wg=wg, j=j, k=k, T=T: e.matmul(
                            psg[:, 0:T], wg[:, k, j * 128:(j + 1) * 128], h[:, k, 0:T], start=(k == 0), stop=(k == 15)),
                            [b_wg, b_h], bpsg, k == 0, k == 15)
                    pso, bpso = nextps(K)
                    for k in range(kr):
                        P.mm(lambda e, pso=pso, wo=wo, j=j, k=k, T=T, yoff=yoff, kr=kr: e.matmul(
                            pso[:, 0:T], wo[:, k, j * 128:(j + 1) * 128], yc[:, yoff + k, 0:T],
                            start=(k == 0), stop=(k == kr - 1)), [b_wo, b_yc], bpso, k == 0, k == kr - 1)
                    g_i = gi % 2
                    gi += 1
                    P.op("act", lambda e, psg=psg, g_i=g_i, T=T: e.activation(gs[g_i][:, 0:T], psg[:, 0:T], AF.Sigmoid),
                         reads=[bpsg], writes=[b_gs[g_i]])
                    if r == 0:
                        P.op("dve", lambda e, pso=pso, g_i=g_i, j=j, T=T: e.tensor_tensor(
                            acc[:, j, 0:T], pso[:, 0:T], gs[g_i][:, 0:T], ALU.mult), reads=[bpso, b_gs[g_i]],
                            writes=[b_acc[j]])
                    else:
                        t_i = ti_ % 2
                        ti_ += 1
                        P.op("dve", lambda e, pso=pso, g_i=g_i, t_i=t_i, T=T: e.tensor_tensor(
                            tr[t_i][:, 0:T], pso[:, 0:T], gs[g_i][:, 0:T], ALU.mult), reads=[bpso, b_gs[g_i]],
                            writes=[b_tr[t_i]])
                        if r == 1:
                            P.op("pool", lambda e, t_i=t_i, j=j, T=T: e.tensor_tensor(
                                acc[:, j, 0:T], acc[:, j, 0:T], tr[t_i][:, 0:T], ALU.add), reads=[b_tr[t_i], b_acc[j]],
                                writes=[b_acc[j]])
                        else:
                            P.op("pool", lambda e, t_i=t_i, j=j, J=J, T=T: e.tensor_tensor(
                                m[:, 4 * J + j, 0:T], acc[:, j, 0:T], tr[t_i][:, 0:T], ALU.add),
                                reads=[b_tr[t_i], b_acc[j]], partial=[b_m])
        if K.cut == 1:
            P.dma("sp", xdst[:, :, t0:t0 + T], xt[:, :, 0:T], K.b_xT[(l + 1) % 2][s], reads=[b_xt, b_m])
            continue
        for J in range(4):
            wo, b_wo = wload(K, wout[:, :, J * 512:(J + 1) * 512], 16, 512, K.b_wb["w_out"][l])
            for j in range(4):
                cc = 4 * J + j
                ps, bps = nextps(K)
                for k in range(16):
                    P.mm(lambda e, ps=ps, wo=wo, j=j, k=k, T=T: e.matmul(
                        ps[:, 0:T], wo[:, k, j * 128:(j + 1) * 128], m[:, k, 0:T], start=(k == 0), stop=(k == 15)),
                        [b_wo, b_m], bps, k == 0, k == 15)
                e_i = ei_ % 4
                ei_ += 1
                P.op("act", lambda e, ps=ps, cc=cc, T=T, who=who, e_i=e_i: e.activation(
                    ev[e_i][:, 0:T], ps[:, 0:T], AF.Copy, scale=modcol(K, l, who, 2, cc)),
                    reads=[bps, K.b_mod], writes=[b_ev[e_i]])
                P.op("pool", lambda e, cc=cc, T=T, e_i=e_i: e.tensor_tensor(
                    xt[:, cc, 0:T], xt[:, cc, 0:T], ev[e_i][:, 0:T], ALU.add),
                    reads=[b_ev[e_i], b_xt], writes=[b_xt])
        if K.cut == 2:
            P.dma("sp", xdst[:, :, t0:t0 + T], xt[:, :, 0:T], K.b_xT[(l + 1) % 2][s], reads=[b_xt, b_m])
            continue
        make_h(K, xt, b_xt, h, b_h, T, l, who, 1, tmp)
        if K.cut == 3:
            P.dma("sp", xdst[:, :, t0:t0 + T], xt[:, :, 0:T], K.b_xT[(l + 1) % 2][s], reads=[b_xt, b_h])
            continue
        for Hb in range(16):
            w1, b_w1 = wload(K, w1v[:, :, Hb * 512:(Hb + 1) * 512], 16, 512, K.b_wb["mlp_w1"][l])
            w2, b_w2 = wload(K, K.wb["mlp_w2"][l][Hb * 512:(Hb + 1) * 512, :].rearrange("(k p) n -> p k n", p=128),
                             4, D, K.b_wb["mlp_w2"][l])
            h_i = hi_ % 2
            hi_ += 1
            for jj in range(4):
                ps, bps = nextps(K)
                for k in range(16):
                    P.mm(lambda e, ps=ps, w1=w1, jj=jj, k=k, T=T: e.matmul(
                        ps[:, 0:T], w1[:, k, jj * 128:(jj + 1) * 128], h[:, k, 0:T], start=(k == 0), stop=(k == 15)),
                        [b_w1, b_h], bps, k == 0, k == 15)
                t_i = ti_ % 2
                ti_ += 1
                P.op("act", lambda e, ps=ps, t_i=t_i, T=T: e.activation(tr[t_i][:, 0:T], ps[:, 0:T], AF.Square),
                     reads=[bps], writes=[b_tr[t_i]])
                P.op("dve", lambda e, ps=ps, t_i=t_i, h_i=h_i, jj=jj, T=T: e.scalar_tensor_tensor(
                    hid[h_i][:, jj, 0:T], ps[:, 0:T], 0.0, tr[t_i][:, 0:T], ALU.is_gt, ALU.mult),
                    reads=[bps, b_tr[t_i]], partial=[b_hid[h_i]])
            if K.cut == 4:
                P.dma("sp", xdst[:, 0:4, t0:t0 + T], hid[h_i][:, :, 0:T].bitcast(F32) if False else xt[:, 0:4, 0:T],
                      K.b_xT[(l + 1) % 2][s], reads=[b_xt, b_hid[h_i], b_w2])
                continue
            for j in range(16):
                ps, bps = nextps(K)
                for k in range(4):
                    P.mm(lambda e, ps=ps, w2=w2, j=j, k=k, T=T, h_i=h_i: e.matmul(
                        ps[:, 0:T], w2[:, k, j * 128:(j + 1) * 128], hid[h_i][:, k, 0:T], start=(k == 0), stop=(k == 3)),
                        [b_w2, b_hid[h_i]], bps, k == 0, k == 3)
                if K.cut == 7:
                    P.op("dve", lambda e, ps=ps, j=j, T=T, who=who: e.scalar_tensor_tensor(
                        tr[0][:, 0:T], ps[:, 0:T], modcol(K, l, who, 5, j), xt[:, j, 0:T], ALU.mult, ALU.add),
                        reads=[bps, K.b_mod, b_xt], writes=[b_tr[0]])
                    continue
                if K.cut == 8:
                    P.op("dve", lambda e, ps=ps, j=j, T=T, who=who: e.tensor_tensor(
                        xt[:, j, 0:T], ps[:, 0:T], xt[:, j, 0:T], ALU.add),
                        reads=[bps, b_xt], writes=[b_xt])
                    continue
                if K.cut == 6:
                    P.op("dve", lambda e, ps=ps, T=T: e.tensor_copy(tr[0][:, 0:T], ps[:, 0:T]),
                         reads=[bps], writes=[b_tr[0]])
                    continue
                e_i = ei_ % 4
                ei_ += 1
                P.op("act", lambda e, ps=ps, j=j, T=T, who=who, e_i=e_i: e.activation(
                    ev[e_i][:, 0:T], ps[:, 0:T], AF.Copy, scale=modcol(K, l, who, 5, j)),
                    reads=[bps, K.b_mod], writes=[b_ev[e_i]])
                P.op("pool", lambda e, j=j, T=T, e_i=e_i: e.tensor_tensor(
                    xt[:, j, 0:T], xt[:, j, 0:T], ev[e_i][:, 0:T], ALU.add),
                    reads=[b_ev[e_i], b_xt], writes=[b_xt])
        if not last:
            P.dma("act", xdst[:, :, t0:t0 + T], xt[:, :, 0:T], K.b_xT[(l + 1) % 2][s], reads=[b_xt])
            if K.cut in (5, 6, 7, 8):
                break
        else:
            P.dma("act", K.out[s].rearrange("(c p) t -> p c t", p=128)[:, :, t0:t0 + T], xt[:, :, 0:T], K.b_out,
                  reads=[b_xt])


def make_in_maps(inp, cores):
    c = _consts()
    f32 = lambda a: np.ascontiguousarray(np.asarray(a, np.float32))
    shared = {k: f32(inp[k]) for k in ("w_mod", "w_in", "filt_w0", "filt_w1", "filt_w2", "pool_w", "w_att_o",
                                       "w_hy_o", "w_pool_o", "w_out", "mlp_w1", "mlp_w2")}
    shared.update(c)
    shared["bc"] = _bc_layout(inp)
    maps = []
    pp_off = None
    for core in cores:
        pp = _pp_layout(inp, core)
        m = dict(shared)
        xs = np.concatenate([inp["x"][core * SPC:(core + 1) * SPC], inp["ctx"][core * SPC:(core + 1) * SPC]], axis=1)
        m["xTin"] = f32(xs.transpose(0, 2, 1))
        m["pp"] = pp.build()
        pp_off = (pp.off, pp.n)
        maps.append(m)
    return maps, pp_off, shared["bc"].shape[1]


def kernel(**inputs):
    inp = {k: np.asarray(v) for k, v in inputs.items()}
    cores = list(range(NCORES))
    maps, (off, npp), nbc = make_in_maps(inp, cores)
    nc, P = build_program(off, npp, nbc)
    res = run_bass_kernel_spmd(nc, maps, core_ids=cores)
    outs = [np.asarray(r["outT"]).transpose(0, 2, 1) for r in res.results]
    return np.ascontiguousarray(np.concatenate(outs, axis=0).astype(np.float32))
```

```python
import math
from contextlib import ExitStack
import numpy as np
import ml_dtypes
import concourse.bass as bass
import concourse.mybir as mybir
from concourse.bass_utils import run_bass_kernel_spmd

F32 = mybir.dt.float32
BF16 = mybir.dt.bfloat16
ALU = mybir.AluOpType
AF = mybir.ActivationFunctionType

D = 2048
L = 2048
LC = 256
TT = L + LC
DEPTH = 2
NCORES = 8
SPC = 2
EPS = 1e-6
IN_W = 9728
Q_OFF, K_OFF, V_OFF, HY_OFF, POOL_OFF, GATE_OFF = 0, 1024, 1280, 1536, 3072, 3584
D_FF = 8192
NPBF = np.dtype(ml_dtypes.bfloat16)


class Buf:
    __slots__ = ("name", "W", "R", "G", "sem", "cnt")

    def __init__(self, name, init_readers=None):
        self.name = name
        self.W = {}
        self.R = dict(init_readers) if init_readers else {}
        self.G = {}
        self.sem = None
        self.cnt = 0


class Op:
    __slots__ = ("eng", "fn", "deps", "dma", "token", "signal", "pos", "sigval", "key", "nofence")


class Prog:
    ENGS = ("pe", "act", "dve", "pool", "sp")

    def __init__(self, nc):
        self.nc = nc
        self.ops = {e: [] for e in self.ENGS}
        self.sems = {}
        self.dma_sems = []
        self.clock = {e: {} for e in self.ENGS}
        self.last_dma = {}
        self.fence_readers = {}
        self.nsem = 0
        self.scope = None
        self.free_sems = []
        self.final_waits = {}

    def buf(self, name):
        b = Buf(name, self.fence_readers)
        if self.scope is not None:
            self.scope.append(b)
        return b

    def push_scope(self):
        self.scope = []

    def pop_scope(self):
        for b in self.scope:
            if b.sem is not None:
                self.free_sems.append((b.sem, b.cnt))
                if b in self.dma_sems:
                    self.dma_sems.remove(b)
                self.final_waits[id(b.sem)] = (b.sem, b.cnt)
        self.scope = None

    def bufs(self, name, n):
        return [self.buf(f"{name}{i}") for i in range(n)]

    def fence(self):
        fr = {}
        for e in ("pe", "act", "dve", "pool"):
            for op in reversed(self.ops[e]):
                if not op.dma:
                    fr[("e", e)] = op
                    break
        for k, op in self.last_dma.items():
            if getattr(op, "nofence", False):
                continue
            fr[("d", k)] = op
        self.fence_readers = fr

    def _new_sem(self, name):
        s = self.nc.alloc_semaphore(name)
        self.nsem += 1
        return s

    def _add(self, eng, fn, reads, writes, partial, dma_dest, mm_first, mm_last):
        op = Op()
        op.nofence = False
        op.eng = eng
        op.fn = fn
        op.dma = dma_dest is not None
        op.signal = False
        op.sigval = None
        op.token = None
        deps = {}

        def add_deps(d):
            for k, a in d.items():
                old = deps.get(k)
                if old is None or self._later(a, old):
                    deps[k] = a

        if op.dma:
            b = dma_dest
            if b.sem is None:
                if self.free_sems:
                    b.sem, b.cnt = self.free_sems.pop()
                else:
                    b.sem = self._new_sem("d" + str(self.nsem))
                self.dma_sems.append(b)
            b.cnt += 16
            op.token = (b.sem, b.cnt)
            op.key = ("d", id(b.sem))
        else:
            op.key = ("e", eng)
        for b in reads:
            add_deps(b.W)
        for b in writes:
            if mm_first is False:
                pass
            else:
                add_deps(b.W)
                add_deps(b.R)
        for b in partial:
            if b.R:
                b.G = dict(b.R)
            add_deps(b.G)
        for b in reads:
            b.R[op.key] = op
        for b in writes:
            if mm_last is False:
                if mm_first:
                    b.W = {}
                    b.R = {}
                continue
            b.W = {op.key: op}
            b.R = {}
            b.G = {}
        for b in partial:
            if b.R:
                b.W = {op.key: op}
                b.R = {}
            else:
                b.W[op.key] = op
        clk = self.clock[eng]
        final = []
        op.pos = len(self.ops[eng])
        for k, a in deps.items():
            if a is op:
                continue
            if a.dma:
                sem, val = a.token
                if clk.get(k, 0) >= val:
                    continue
                clk[k] = val
                final.append(a)
            else:
                if a.eng == "pe" and eng == "pe":
                    continue
                if clk.get(k, -1) >= a.pos:
                    continue
                clk[k] = a.pos
                a.signal = True
                final.append(a)
        op.deps = final
        self.ops[eng].append(op)
        if op.dma:
            self.last_dma[id(op.token[0])] = op
        return op

    @staticmethod
    def _later(a, b):
        if a.dma:
            return a.token[1] > b.token[1]
        return a.pos > b.pos

    def op(self, eng, fn, reads=(), writes=(), partial=()):
        return self._add(eng, fn, reads, writes, partial, None, None, None)

    def mm(self, fn, reads, out, first, last):
        return self._add("pe", fn, reads, (out,), (), None, first, last)

    def dma(self, eng, out_ap, in_ap, dst, reads=(), partial=True, **kw):
        fn = lambda e: e.dma_start(out=out_ap, in_=in_ap, **kw)
        if partial:
            return self._add(eng, fn, reads, (), (dst,), dst, None, None)
        return self._add(eng, fn, reads, (dst,), (), dst, None, None)

    def emit(self):
        nc = self.nc
        esem = {e: self._new_sem("e_" + e) for e in ("pe", "act", "dve", "pool")}
        for e in ("pe", "act", "dve", "pool"):
            n = 0
            for op in self.ops[e]:
                if not op.dma and op.signal:
                    n += 1
                    op.sigval = n
        handles = {"pe": "tensor", "act": "scalar", "dve": "vector", "pool": "gpsimd", "sp": "sync"}
        stats = {}
        with nc.Block() as block:
            for e in self.ENGS:
                ops = self.ops[e]
                if not ops and e != "sp":
                    continue

                def body(h, ops=ops, e=e):
                    nw = 0
                    for op in ops:
                        for a in op.deps:
                            if a.dma:
                                h.wait_ge(a.token[0], a.token[1])
                            else:
                                h.wait_ge(esem[a.eng], a.sigval)
                            nw += 1
                        ins = op.fn(h)
                        if op.dma:
                            ins.then_inc(op.token[0], 16)
                        elif op.signal:
                            ins.then_inc(esem[e], 1)
                    if e == "sp":
                        fw = dict(self.final_waits)
                        for b in self.dma_sems:
                            fw[id(b.sem)] = (b.sem, b.cnt)
                        for (sm, cnt) in fw.values():
                            h.wait_ge(sm, cnt)
                    stats[e] = (len(ops), nw)

                getattr(block, handles[e])(body)
        self.stats = stats


def _bf(a):
    return np.ascontiguousarray(a.astype(NPBF))


_CONST_CACHE = {}


def _dft_consts(n_tok):
    n = 2 * n_tok
    t = np.arange(n_tok, dtype=np.float64)[:, None]
    f = np.arange(n_tok, dtype=np.float64)[None, :]
    ang = 2.0 * np.pi * (f + 0.5) * t / n
    Fm = np.concatenate([np.cos(ang), np.sin(ang)], axis=1)
    return _bf(Fm), _bf(Fm.T)


def _filter_consts(n_tok):
    t = np.linspace(0.0, 1.0, n_tok, dtype=np.float32)[:, None]
    w = (2.0 * math.pi * np.arange(n_tok, dtype=np.float32)[:, None] / n_tok).astype(np.float32)
    bands = np.linspace(1e-4, 15, 16, dtype=np.float32)[None, :]
    z = np.concatenate([t, np.cos(bands * w), -np.sin(bands * w)], axis=-1).astype(np.float32)
    deltas = np.linspace(math.log(1e-2) / 1.5, math.log(1e-2) / 0.3, 512, dtype=np.float32)
    decay = np.exp(-t * np.abs(deltas)[None, :]).astype(np.float32)
    zs = np.zeros_like(z)
    zs[1:] = z[:-1]
    ds = np.zeros_like(decay)
    ds[1:] = decay[:-1]
    zz = np.stack([z.T, zs.T], 0)
    dd = np.stack([decay, ds], 0)
    return np.ascontiguousarray(zz), np.ascontiguousarray(dd)


def _pool_inv(n_tok):
    t = np.arange(n_tok)
    out = np.zeros((4, n_tok), np.float32)
    for g, win in enumerate((2, 4, 8, 16)):
        a = np.clip(t - win // 2, 0, n_tok)
        b = np.clip(t + win // 2, 0, n_tok)
        out[g] = 1.0 / (b - a).astype(np.float32)
    return np.ascontiguousarray(np.broadcast_to(out[None], (128, 4, n_tok)))


def _rope_tabs():
    rows = L // 64
    row = np.repeat(np.arange(rows, dtype=np.float32), 64)
    col = np.tile(np.arange(64, dtype=np.float32), rows)
    inv = (10000.0 ** (-np.arange(16, dtype=np.float32) / 16)).astype(np.float32)
    ar = row[:, None] * inv
    ac = col[:, None] * inv
    C = np.zeros((64, L), np.float32)
    S = np.zeros((64, L), np.float32)
    for ax, a in enumerate((ar, ac)):
        c = np.cos(a).T
        s = np.sin(a).T
        C[ax * 32:ax * 32 + 16] = c
        C[ax * 32 + 16:ax * 32 + 32] = c
        S[ax * 32:ax * 32 + 16] = -s
        S[ax * 32 + 16:ax * 32 + 32] = s
    C = np.concatenate([C, C], 0)
    S = np.concatenate([S, S], 0)
    return np.ascontiguousarray(C), np.ascontiguousarray(S)


def _misc_mats():
    ident = np.eye(128, dtype=np.float32)
    ones = np.ones((128, 128), np.float32)
    blk = np.zeros((128, 128), np.float32)
    blk[:64, :64] = 1
    blk[64:, 64:] = 1
    perm = np.zeros((128, 128), np.float32)
    for p in range(128):
        r = p % 32
        q = p - r + (r + 16) % 32
        perm[q, p] = 1.0
    ki = np.arange(128)[:, None]
    qi = np.arange(128)[None, :]
    maskA = np.where(qi > ki, -30000.0, 0.0).astype(np.float32)
    maskB = np.where(ki > qi, -30000.0, 0.0).astype(np.float32)
    m = np.stack([ident, ones, blk, perm, maskA, maskB], 0)
    return np.ascontiguousarray(m.transpose(1, 0, 2))


def _consts():
    if "c" in _CONST_CACHE:
        return _CONST_CACHE["c"]
    c = {}
    c["F_l"], c["FT_l"] = _dft_consts(L)
    c["F_c"], c["FT_c"] = _dft_consts(LC)
    c["zz_l"], c["dd_l"] = _filter_consts(L)
    c["zz_c"], c["dd_c"] = _filter_consts(LC)
    c["pinv_l"] = _pool_inv(L)
    c["pinv_c"] = _pool_inv(LC)
    c["ropeC"], c["ropeS"] = _rope_tabs()
    c["mats"] = _misc_mats()
    _CONST_CACHE["c"] = c
    return c


class PP:
    def __init__(self):
        self.cols = []
        self.off = {}
        self.n = 0

    def add(self, name, arr):
        arr = np.asarray(arr, np.float32).reshape(128, -1)
        self.off[name] = (self.n, arr.shape[1])
        self.cols.append(arr)
        self.n += arr.shape[1]

    def build(self):
        return np.ascontiguousarray(np.concatenate(self.cols, axis=1))


def _chunked(v, nch):
    return np.asarray(v, np.float32).reshape(nch, 128).T


def _pp_layout(inp, core):
    pp = PP()
    b0 = core * SPC
    cv = np.stack([inp["c"][b0], inp["c"][b0 + 1], inp["c_ctx"]], 0)
    pp.add("cvec", cv.reshape(3, 16, 128).transpose(2, 1, 0).reshape(128, 48))
    for l in range(DEPTH):
        pp.add(f"n1g{l}", _chunked(inp["norm1_g"][l], 16))
        pp.add(f"n2g{l}", _chunked(inp["norm2_g"][l], 16))
        pp.add(f"bmod{l}", _chunked(inp["b_mod"][l], 96))
        pp.add(f"qg{l}", np.tile(inp["q_norm_g"][l], 2).reshape(128, 1))
        pp.add(f"kg{l}", np.tile(inp["k_norm_g"][l], 2).reshape(128, 1))
        pp.add(f"hcw{l}", inp["hy_conv_w"][l].reshape(3, 12, 128).transpose(2, 1, 0).reshape(128, 36))
        pp.add(f"hcb{l}", _chunked(inp["hy_conv_b"][l], 12))
        pp.add(f"psc{l}", _chunked(inp["pool_scale"][l], 4))
        fb = np.zeros((128, 4), np.float32)
        fb[:64, 0] = inp["filt_b0"][l]
        fb[:64, 1] = inp["filt_b1"][l][0]
        fb[:64, 2] = inp["filt_b1"][l][1]
        fb[:64, 3] = inp["filt_freq"][l]
        pp.add(f"fb{l}", fb)
    return pp


def _bc_layout(inp):
    a = np.concatenate([inp["sink"].reshape(-1), inp["hy_bias"].reshape(-1)]).astype(np.float32)
    return np.ascontiguousarray(np.broadcast_to(a[None], (128, a.size)))


class Ctx:
    pass


def _stub(*a, **k):
    return None


phase_filters = phase_att = phase_hy = phase_pool = phase_B = _stub


def build_program(pp_off, npp, nbc, stop_after=None, debug=False):
    nc = bass.Bass("TRN2", target_bir_lowering=False)
    P = Prog(nc)
    K = Ctx()
    K.nc, K.P, K.pp_off = nc, P, pp_off
    okind = "ExternalOutput" if debug else "Internal"

    def din(name, shape, dt=F32):
        return nc.dram_tensor(name, list(shape), dt, kind="ExternalInput").ap()

    def dscr(name, shape, dt, dbg=True):
        return nc.dram_tensor(name, list(shape), dt, kind=(okind if dbg else "Internal")).ap()

    K.xTin = din("xTin", [SPC, D, TT])
    K.pp = din("pp", [128, npp])
    K.bc = din("bc", [128, nbc])
    K.w_mod = din("w_mod", [DEPTH, D, 6 * D])
    K.w_in = din("w_in", [DEPTH, D, IN_W])
    K.filt_w0 = din("filt_w0", [DEPTH, 33, 64])
    K.filt_w1 = din("filt_w1", [DEPTH, 2, 64, 64])
    K.filt_w2 = din("filt_w2", [DEPTH, 64, 2048])
    K.pool_w = din("pool_w", [DEPTH, 4, 128, 128])
    K.w_att_o = din("w_att_o", [DEPTH, 1024, D])
    K.w_hy_o = din("w_hy_o", [DEPTH, 512, D])
    K.w_pool_o = din("w_pool_o", [DEPTH, 512, D])
    K.w_out = din("w_out", [DEPTH, D, D])
    K.mlp_w1 = din("mlp_w1", [DEPTH, D, D_FF])
    K.mlp_w2 = din("mlp_w2", [DEPTH, D_FF, D])
    K.F_l = din("F_l", [L, 2 * L], BF16)
    K.FT_l = din("FT_l", [2 * L, L], BF16)
    K.F_c = din("F_c", [LC, 2 * LC], BF16)
    K.FT_c = din("FT_c", [2 * LC, LC], BF16)
    K.zz_l = din("zz_l", [2, 33, L])
    K.dd_l = din("dd_l", [2, L, 512])
    K.zz_c = din("zz_c", [2, 33, LC])
    K.dd_c = din("dd_c", [2, LC, 512])
    K.pinv_l = din("pinv_l", [128, 4, L])
    K.pinv_c = din("pinv_c", [128, 4, LC])
    K.ropeC = din("ropeC", [128, L])
    K.ropeS = din("ropeS", [128, L])
    K.mats = din("mats", [128, 6, 128])
    K.out = nc.dram_tensor("outT", [SPC, D, L], F32, kind="ExternalOutput").ap()
    K.xT = [[K.xTin[s] for s in range(SPC)], [dscr(f"xT1_{s}", [D, TT], F32) for s in range(SPC)]]
    K.zq = [dscr(f"zq{s}", [1024, TT], BF16) for s in range(SPC)]
    K.zk = [dscr(f"zk{s}", [256, TT], BF16) for s in range(SPC)]
    K.zv = [dscr(f"zv{s}", [TT, 4 * 65], BF16) for s in range(SPC)]
    K.zhy = [dscr(f"zhy{s}", [1536, TT], BF16) for s in range(SPC)]
    K.zpl = [dscr(f"zpl{s}", [512, TT], BF16) for s in range(SPC)]
    K.ycat = [dscr(f"ycat{s}", [D, TT], BF16) for s in range(SPC)]
    K.Kf_l = dscr("Kf_l", [2, 2, L, 512], F32)
    K.Kf_c = dscr("Kf_c", [2, 2, LC, 512], F32)
    K.moddbg = dscr("moddbg", [128, DEPTH * 3 * 96], F32)
    WSH = {"w_in": (D, IN_W), "w_att_o": (1024, D), "w_hy_o": (512, D), "w_pool_o": (512, D), "w_out": (D, D),
           "mlp_w1": (D, D_FF), "mlp_w2": (D_FF, D)}
    K.wb = {n: [nc.dram_tensor(f"wb_{n}{l}", list(sh), BF16, kind="Internal").ap() for l in range(DEPTH)]
            for n, sh in WSH.items()}
    K.b_wb = {n: [P.buf(f"wb_{n}{l}") for l in range(DEPTH)] for n in WSH}
    K.WSH = WSH
    K.b_xT = [[P.buf(f"xT{a}{s}") for s in range(SPC)] for a in range(2)]
    K.b_z = [{n: P.buf(n + str(s)) for n in ("zq", "zk", "zv", "zhy", "zpl")} for s in range(SPC)]
    K.b_ycat = [{n: P.buf("y" + n + str(s)) for n in ("att", "hy", "pool")} for s in range(SPC)]
    K.b_Kf = {"l": P.buf("Kf_l"), "c": P.buf("Kf_c")}
    K.b_out = P.buf("out")
    K.b_dbg = P.buf("dbg")
    K.b_in = P.buf("inputs")

    with ExitStack() as es:
        K.ps = [es.enter_context(nc.psum_tensor(f"ps{i}", [128, 512], F32)) for i in range(8)]
        K.b_ps = P.bufs("ps", 8)
        K.psi = 0
        K.mats_f = es.enter_context(nc.sbuf_tensor("mats_f", [128, 6, 128], F32))
        K.mats_b = es.enter_context(nc.sbuf_tensor("mats_b", [128, 6, 128], BF16))
        K.ppt = es.enter_context(nc.sbuf_tensor("ppt", [128, npp], F32))
        K.bct = es.enter_context(nc.sbuf_tensor("bct", [128, nbc], F32))
        K.mod = es.enter_context(nc.sbuf_tensor("mod", [128, DEPTH * 3 * 96], F32))
        K.modA = es.enter_context(nc.sbuf_tensor("modA", [128, DEPTH * 3 * 2 * 16], F32))
        K.b_const = P.buf("const")
        K.b_mod = P.buf("mod")
        K.b_modA = P.buf("modA")
        K.cst = es.enter_context(nc.sbuf_tensor("cst", [128, 4], F32))
        P.op("dve", lambda e: e.memset(K.cst[:, 0:1], EPS), partial=[K.b_const])
        P.op("dve", lambda e: e.memset(K.cst[:, 1:2], -math.pi), partial=[K.b_const])
        P.op("dve", lambda e: e.memset(K.cst[:, 2:3], 0.0), partial=[K.b_const])
        P.dma("sp", K.mats_f[:], K.mats, K.b_const)
        P.dma("pool", K.mats_b[:], K.mats, K.b_const)
        P.dma("sp", K.ppt[:], K.pp, K.b_const)
        P.dma("sp", K.bct[:], K.bc, K.b_const)

        phases = [phase_precast, phase_mod, lambda K: phase_precast(K, 1)]
        for l in range(DEPTH):
            phases.append(lambda K, l=l: phase_filters(K, l))
            for s in range(SPC):
                phases.append(lambda K, l=l, s=s: phase_A(K, l, s))
                phases.append(lambda K, l=l, s=s: phase_att(K, l, s))
                phases.append(lambda K, l=l, s=s: phase_hy(K, l, s))
                phases.append(lambda K, l=l, s=s: phase_pool(K, l, s))
                phases.append(lambda K, l=l, s=s: phase_B(K, l, s))
        import os as _os
        only = _os.environ.get("PH_ONLY")
        only = [int(v) for v in only.split(",")] if only else None
        K.cut = int(_os.environ.get("KB_CUT", "0"))
        for i, ph in enumerate(phases):
            if stop_after is not None and i >= stop_after:
                break
            if only is not None and i not in only:
                continue
            P.push_scope()
            with ExitStack() as pes:
                K.es = pes
                ph(K)
            P.fence()
            P.pop_scope()
        P.emit()
    return nc, P


def ppc(K, name, i=0, n=1, rows=128):
    o, w = K.pp_off[name]
    return K.ppt[0:rows, o + i:o + i + n]


def nextps(K):
    i = K.psi
    K.psi = (K.psi + 1) % 8
    return K.ps[i], K.b_ps[i]


def sb(K, name, shape, dt):
    K.uid = getattr(K, "uid", 0) + 1
    return K.es.enter_context(K.nc.sbuf_tensor(f"{name}_u{K.uid}", list(shape), dt))


def phase_precast(K, part=0):
    P = K.P
    for l in range(DEPTH):
        for n in ("w_in", "w_att_o", "w_hy_o", "w_pool_o", "w_out", "mlp_w1", "mlp_w2"):
            if (part == 0) != (l == 0 and n == "w_in"):
                continue
            rows, cols = K.WSH[n]
            src = getattr(K, n)[l]
            RB = 128
            for r0 in range(0, rows, RB):
                first = (l == 0 and n == "w_in")
                o = P.dma("pool", K.wb[n][l][r0:r0 + RB, :], src[r0:r0 + RB, :], K.b_wb[n][l],
                          reads=[K.b_in] if first else [K.b_in, K.b_modA])
                o.nofence = True


def phase_transpose_in(K):
    P = K.P
    ident = K.mats_f[:, 0, :]
    NS = 2
    xin = [sb(K, f"t0_in{i}", [128, 4, D], F32) for i in range(NS)]
    b_xin = P.bufs("t0_in", NS)
    xo = [sb(K, f"t0_out{i}", [128, 16, 512], F32) for i in range(NS)]
    b_xo = P.bufs("t0_out", NS)
    it = 0
    for s in range(SPC):
        for (src, n_tok, tok0) in ((K.x[s], L, 0), (K.ctx[s], LC, L)):
            for t0 in range(0, n_tok, 512):
                T = min(512, n_tok - t0)
                nsub = T // 128
                i = it % NS
                it += 1
                P.dma("sp", xin[i][:, 0:nsub, :], src[t0:t0 + T, :].rearrange("(a p) d -> p a d", p=128),
                      b_xin[i], reads=[K.b_in], partial=False)
                for c in range(16):
                    ps, bps = nextps(K)
                    for a in range(nsub):
                        P.mm(lambda e, ps=ps, a=a, c=c, i=i: e.transpose(
                            ps[:, a * 128:(a + 1) * 128], xin[i][:, a, c * 128:(c + 1) * 128], ident),
                            [b_xin[i], K.b_const], bps, a == 0, a == nsub - 1)
                    eng = "dve" if c % 2 == 0 else "act"
                    if eng == "dve":
                        P.op("dve", lambda e, ps=ps, c=c, i=i, T=T: e.tensor_copy(xo[i][:, c, 0:T], ps[:, 0:T]),
                             reads=[bps], partial=[b_xo[i]])
                    else:
                        P.op("act", lambda e, ps=ps, c=c, i=i, T=T: e.copy(xo[i][:, c, 0:T], ps[:, 0:T]),
                             reads=[bps], partial=[b_xo[i]])
                P.dma("sp", K.xT[0][s].rearrange("(c p) t -> p c t", p=128)[:, :, tok0 + t0:tok0 + t0 + T],
                      xo[i][:, :, 0:T], K.b_xT[0][s], reads=[b_xo[i]])


def phase_mod(K):
    P = K.P
    nc = K.nc
    sc = sb(K, "pm_sc", [128, 48], F32)
    b_sc = P.buf("pm_sc")
    P.op("act", lambda e: e.activation(sc[:], ppc(K, "cvec", 0, 48), AF.Silu), reads=[K.b_const], writes=[b_sc])
    NS = 2
    wt = [sb(K, f"pm_w{i}", [128, 16, 512], F32) for i in range(NS)]
    b_wt = P.bufs("pm_w", NS)
    s3 = [sb(K, f"pm_s3{i}", [3, 512], F32) for i in range(2)]
    b_s3 = P.bufs("pm_s3", 2)
    ident3 = K.mats_f[0:3, 0, 0:3]
    it = 0
    for l in range(DEPTH):
        for nb in range(24):
            i = it % NS
            it += 1
            P.dma("sp", wt[i][:], K.w_mod[l].rearrange("(k p) n -> p k n", p=128)[:, :, nb * 512:(nb + 1) * 512],
                  b_wt[i], reads=[K.b_in], partial=False)
            ps, bps = nextps(K)
            for k in range(16):
                P.mm(lambda e, ps=ps, i=i, k=k: e.matmul(
                    ps[0:3, :], sc[:, k * 3:k * 3 + 3], wt[i][:, k, :], start=(k == 0), stop=(k == 15)),
                    [b_wt[i], b_sc], bps, k == 0, k == 15)
            P.op("act", lambda e, ps=ps, i=i: e.copy(s3[i][:], ps[0:3, :]), reads=[bps], writes=[b_s3[i]])
            ps2, bps2 = nextps(K)
            for j in range(4):
                P.mm(lambda e, ps2=ps2, i=i, j=j: e.transpose(
                    ps2[:, j * 4:j * 4 + 3], s3[i][0:3, j * 128:(j + 1) * 128], ident3),
                    [b_s3[i], K.b_const], bps2, j == 0, j == 3)
            for who in range(3):
                col = (l * 3 + who) * 96 + nb * 4
                P.op("dve", lambda e, ps2=ps2, who=who, col=col, l=l, nb=nb: e.tensor_tensor(
                    K.mod[:, col:col + 4], ps2[:, 0:16].rearrange("p (j w) -> p j w", w=4)[:, :, who],
                    ppc(K, f"bmod{l}", nb * 4, 4), ALU.add),
                    reads=[bps2, K.b_const], partial=[K.b_mod])
    for l in range(DEPTH):
        for who in range(3):
            base = (l * 3 + who) * 96
            for which, (gname, scoff) in enumerate(((f"n1g{l}", 16), (f"n2g{l}", 64))):
                o = ((l * 3 + who) * 2 + which) * 16
                P.op("dve", lambda e, base=base, scoff=scoff, gname=gname, o=o: e.scalar_tensor_tensor(
                    K.modA[:, o:o + 16], K.mod[:, base + scoff:base + scoff + 16], 1.0, ppc(K, gname, 0, 16),
                    ALU.add, ALU.mult), reads=[K.b_mod, K.b_const], partial=[K.b_modA])
    P.dma("sp", K.moddbg, K.mod[:], K.b_dbg, reads=[K.b_mod])


def modcol(K, l, who, part, c):
    col = (l * 3 + who) * 96 + part * 16 + c
    return K.mod[:, col:col + 1]


def modAcol(K, l, who, which, c):
    o = ((l * 3 + who) * 2 + which) * 16 + c
    return K.modA[:, o:o + 1]


def wring(K, n=3):
    K.wr = [sb(K, f"wr{i}", [128, 8192], BF16) for i in range(n)]
    K.b_wr = K.P.bufs("wr", n)
    K.wri = 0


def wload(K, src_ap, kc, ncol, dep):
    i = K.wri
    K.wri = (K.wri + 1) % len(K.wr)
    view = K.wr[i][:, 0:kc * ncol].rearrange("p (k n) -> p k n", k=kc)
    K.P.dma("sp", view, src_ap, K.b_wr[i], reads=[dep], partial=False)
    return view, K.b_wr[i]


def rsqrt_from_ps(K, out, b_out, ps, bps, scale, rows=128):
    P = K.P
    P.op("act", lambda e: e.activation(out, ps, AF.Ln, bias=K.cst[0:rows, 0:1], scale=scale),
         reads=[bps, K.b_const], writes=[b_out])
    P.op("act", lambda e: e.activation(out, out, AF.Exp, scale=-0.5), reads=[b_out], writes=[b_out])


def make_h(K, xt, b_xt, h, b_h, T, l, who, which, tmp):
    P = K.P
    ones_b = K.mats_b[:, 1, :]
    sq, b_sq, rs, b_rs, t32, b_t32 = tmp["sq"], tmp["b_sq"], tmp["rs"], tmp["b_rs"], tmp["t32"], tmp["b_t32"]
    P.op("act", lambda e: e.activation(sq[:, :, 0:T], xt[:, :, 0:T], AF.Square), reads=[b_xt], writes=[b_sq])
    ps, bps = nextps(K)
    for c in range(16):
        P.mm(lambda e, c=c: e.matmul(ps[:, 0:T], ones_b, sq[:, c, 0:T], start=(c == 0), stop=(c == 15)),
             [b_sq, K.b_const], bps, c == 0, c == 15)
    rsqrt_from_ps(K, rs[:, 0:T], b_rs, ps[:, 0:T], bps, 1.0 / D)
    sh_part = 0 if which == 0 else 3
    for c in range(16):
        j = c % 2
        P.op("dve", lambda e, c=c, j=j: e.scalar_tensor_tensor(
            t32[j][:, 0:T], xt[:, c, 0:T], modAcol(K, l, who, which, c), rs[:, 0:T], ALU.mult, ALU.mult),
            reads=[b_xt, b_rs, K.b_modA], writes=[b_t32[j]])
        P.op("act", lambda e, c=c, j=j: e.activation(
            h[:, c, 0:T], t32[j][:, 0:T], AF.Identity, bias=modcol(K, l, who, sh_part, c), scale=1.0),
            reads=[b_t32[j], K.b_mod], partial=[b_h])


def h_tmp(K, pfx):
    P = K.P
    return dict(sq=sb(K, pfx + "sq", [128, 16, 512], BF16), b_sq=P.buf(pfx + "sq"),
                rs=sb(K, pfx + "rs", [128, 512], F32), b_rs=P.buf(pfx + "rs"),
                t32=[sb(K, pfx + f"t32{j}", [128, 512], F32) for j in range(2)], b_t32=P.bufs(pfx + "t32", 2))


def tiles_of(l, with_ctx=True):
    t = [(t0, 512) for t0 in range(0, L, 512)]
    if with_ctx:
        t.append((L, LC))
    return t


def phase_A(K, l, s):
    P = K.P
    wring(K, 3)
    tmp = h_tmp(K, "pa_")
    xt = [sb(K, f"pa_x{i}", [128, 16, 512], F32) for i in range(2)]
    b_xt = P.bufs("pa_x", 2)
    h = [sb(K, f"pa_h{i}", [128, 16, 512], BF16) for i in range(2)]
    b_h = P.bufs("pa_h", 2)
    st = [sb(K, f"pa_st{i}", [128, 4, 512], BF16) for i in range(2)]
    b_st = P.bufs("pa_st", 2)
    vst = [sb(K, f"pa_vst{i}", [128, 4, 65], BF16) for i in range(2)]
    b_vst = P.bufs("pa_vst", 2)
    for i in range(2):
        P.op("dve", lambda e, i=i: e.memset(vst[i][:], 1.0), writes=[b_vst[i]])
    xsrc = K.xT[l % 2][s].rearrange("(c p) t -> p c t", p=128)
    win = K.wb["w_in"][l].rearrange("(k p) n -> p k n", p=128)
    b_win = K.b_wb["w_in"][l]
    sti = 0
    vsi = 0
    tl = tiles_of(l)

    def pload(ti):
        t0, T = tl[ti]
        i = ti % 2
        P.dma("act", xt[i][:, :, 0:T], xsrc[:, :, t0:t0 + T], b_xt[i], reads=[K.b_xT[l % 2][s]], partial=False)

    def prep(ti):
        t0, T = tl[ti]
        who = 2 if t0 >= L else s
        i = ti % 2
        make_h(K, xt[i], b_xt[i], h[i], b_h[i], T, l, who, 0, tmp)

    pload(0)
    prep(0)
    if len(tl) > 1:
        pload(1)
    for ti, (t0, T) in enumerate(tl):
        is_ctx = t0 >= L
        who = 2 if is_ctx else s
        i = ti % 2
        nblk = 0
        prepped = False
        blocks = [(0, "zq", 0), (512, "zq", 512), (1024, "kv", 0), (1536, "zhy", 0), (2048, "zhy", 512),
                  (2560, "zhy", 1024), (3072, "zpl", 0)]
        for (col0, dst, r0) in blocks:
            if is_ctx and l == DEPTH - 1 and dst != "kv":
                continue
            if nblk == 4 and ti + 1 < len(tl):
                prep(ti + 1)
                prepped = True
            nblk += 1
            wv, b_w = wload(K, win[:, :, col0:col0 + 512], 16, 512, b_win)
            nj = 2 if dst == "kv" else 4
            si = sti % 2
            sti += 1
            for j in range(nj):
                ps, bps = nextps(K)
                for k in range(16):
                    P.mm(lambda e, ps=ps, wv=wv, j=j, k=k, i=i, T=T: e.matmul(
                        ps[:, 0:T], wv[:, k, j * 128:(j + 1) * 128], h[i][:, k, 0:T], start=(k == 0), stop=(k == 15)),
                        [b_w, b_h[i]], bps, k == 0, k == 15)
                P.op("act", lambda e, ps=ps, si=si, j=j, T=T: e.copy(st[si][:, j, 0:T], ps[:, 0:T]),
                     reads=[bps], partial=[b_st[si]])
            dname = "zk" if dst == "kv" else dst
            dten = getattr(K, dname)[s].rearrange("(c p) t -> p c t", p=128)
            c0 = r0 // 128
            P.dma("act", dten[:, c0:c0 + nj, t0:t0 + T], st[si][:, 0:nj, 0:T], K.b_z[s][dname], reads=[b_st[si]])
            if dst == "kv":
                for a in range(T // 128):
                    ps, bps = nextps(K)
                    for k in range(16):
                        P.mm(lambda e, ps=ps, wv=wv, a=a, k=k, i=i: e.matmul(
                            ps[:, 0:256], h[i][:, k, a * 128:(a + 1) * 128], wv[:, k, 256:512],
                            start=(k == 0), stop=(k == 15)), [b_w, b_h[i]], bps, k == 0, k == 15)
                    vi = vsi % 2
                    vsi += 1
                    P.op("dve", lambda e, ps=ps, vi=vi: e.tensor_copy(
                        vst[vi][:, :, 0:64], ps[:, 0:256].rearrange("p (g d) -> p g d", g=4)),
                        reads=[bps], partial=[b_vst[vi]])
                    P.dma("act", K.zv[s][t0 + a * 128:t0 + (a + 1) * 128, :].rearrange("p (g d) -> p g d", g=4),
                          vst[vi][:], K.b_z[s]["zv"], reads=[b_vst[vi]])
        if not prepped and ti + 1 < len(tl):
            prep(ti + 1)
        if ti + 2 < len(tl):
            pload(ti + 2)


def phase_filters(K, l):
    with ExitStack() as es2:
        K.es = es2
        filters_for(K, l, L, K.zz_l, K.dd_l, K.F_l, K.Kf_l, K.b_Kf["l"], "fl")
    K.P.fence()
    if l == 0:
        with ExitStack() as es2:
            K.es = es2
            filters_for(K, l, LC, K.zz_c, K.dd_c, K.F_c, K.Kf_c, K.b_Kf["c"], "fc")


def filters_for(K, l, n_tok, zz, dd, Fm, Kdst, b_Kdst, pfx):
    P = K.P
    nT = n_tok // 128
    TW = min(512, n_tok)
    w0 = sb(K, pfx + "w0", [33, 64], F32)
    w1 = sb(K, pfx + "w1", [64, 2, 64], F32)
    w2 = sb(K, pfx + "w2", [64, 2048], F32)
    b_w = P.buf(pfx + "w")
    P.dma("sp", w0[:], K.filt_w0[l], b_w, reads=[K.b_in])
    P.dma("sp", w1[:], K.filt_w1[l].rearrange("i k n -> k i n"), b_w, reads=[K.b_in])
    P.dma("sp", w2[:], K.filt_w2[l], b_w, reads=[K.b_in])
    zt = sb(K, pfx + "zt", [33, n_tok], F32)
    b_zt = P.buf(pfx + "zt")
    hb = sb(K, pfx + "hb", [128, 2, nT, 512], BF16)
    b_hb = P.buf(pfx + "hb")
    AB = sb(K, pfx + "AB", [128, 2, 2, nT, 512], BF16)
    b_AB = P.buf(pfx + "AB")
    hid = [sb(K, pfx + f"hid{i}", [64, TW], F32) for i in range(2)]
    b_hid = P.bufs(pfx + "hid", 2)
    arg = sb(K, pfx + "arg", [64, TW], F32)
    b_arg = P.buf(pfx + "arg")
    argi = sb(K, pfx + "argi", [64, TW], mybir.dt.int32)
    b_argi = P.buf(pfx + "argi")
    arg2 = sb(K, pfx + "arg2", [64, TW], F32)
    b_arg2 = P.buf(pfx + "arg2")
    dec = [sb(K, pfx + f"dec{i}", [128, 512], F32) for i in range(2)]
    b_dec = P.bufs(pfx + "dec", 2)
    hf = [sb(K, pfx + f"hf{i}", [128, 512], F32) for i in range(2)]
    b_hf = P.bufs(pfx + "hf", 2)
    fbn = f"fb{l}"
    di = 0
    for direction in (1, 0):
        P.dma("sp", zt[:], zz[direction], b_zt, reads=[K.b_in], partial=False)
        for t0 in range(0, n_tok, TW):
            cur_in, b_cur_in, kin = zt[:, t0:t0 + TW], b_zt, 33
            for layer in range(3):
                ps, bps = nextps(K)
                if layer == 0:
                    lhsT = w0[:]
                else:
                    lhsT = w1[:, layer - 1, :]
                P.mm(lambda e, ps=ps, lhsT=lhsT, cur_in=cur_in: e.matmul(ps[0:64, 0:TW], lhsT, cur_in,
                                                                          start=True, stop=True),
                     [b_w, b_cur_in], bps, True, True)
                P.op("dve", lambda e, ps=ps, layer=layer: e.tensor_scalar(
                    arg[:], ps[0:64, 0:TW], ppc(K, fbn, layer, 1, 64), ppc(K, fbn, 3, 1, 64), ALU.add, ALU.mult),
                    reads=[bps, K.b_const], writes=[b_arg])
                P.op("dve", lambda e: e.tensor_single_scalar(argi[:], arg[:], 1.0 / (2.0 * math.pi), ALU.mult),
                     reads=[b_arg], writes=[b_argi])
                P.op("dve", lambda e: e.scalar_tensor_tensor(arg2[:], argi[:], -2.0 * math.pi, arg[:], ALU.mult, ALU.add),
                     reads=[b_arg, b_argi], writes=[b_arg2])
                ho = layer % 2
                P.op("act", lambda e, ho=ho: e.activation(hid[ho][:], arg2[:], AF.Sin, bias=K.cst[0:64, 2:3],
                                                          scale=1.0 - 2e-6),
                     reads=[b_arg2, K.b_const], writes=[b_hid[ho]])
                cur_in, b_cur_in = hid[ho][:], b_hid[ho]
            h3 = cur_in
            for a in range(TW // 128):
                pc = t0 // 128 + a
                d_i = di % 2
                di += 1
                P.dma("sp", dec[d_i][:], dd[direction, pc * 128:(pc + 1) * 128, :], b_dec[d_i], reads=[K.b_in],
                      partial=False)
                for o in range(2):
                    ps, bps = nextps(K)
                    col = o * 1024 + direction * 512
                    P.mm(lambda e, ps=ps, h3=h3, a=a, col=col: e.matmul(
                        ps[:, :], h3[:, a * 128:(a + 1) * 128], w2[:, col:col + 512], start=True, stop=True),
                        [b_w, b_cur_in], bps, True, True)
                    if direction == 1:
                        P.op("dve", lambda e, ps=ps, o=o, pc=pc, d_i=d_i: e.tensor_tensor(
                            hb[:, o, pc, :], ps[:, :], dec[d_i][:], ALU.mult), reads=[bps, b_dec[d_i]], partial=[b_hb])
                    else:
                        P.op("dve", lambda e, ps=ps, o=o, d_i=d_i: e.tensor_tensor(
                            hf[o][:], ps[:, :], dec[d_i][:], ALU.mult), reads=[bps, b_dec[d_i]], writes=[b_hf[o]])
                        P.op("dve", lambda e, o=o, pc=pc: e.tensor_tensor(
                            AB[:, o, 0, pc, :], hf[o][:], hb[:, o, pc, :], ALU.add), reads=[b_hf[o], b_hb],
                            partial=[b_AB])
                        P.op("dve", lambda e, o=o, pc=pc: e.tensor_tensor(
                            AB[:, o, 1, pc, :], hf[o][:], hb[:, o, pc, :], ALU.subtract), reads=[b_hf[o], b_hb],
                            partial=[b_AB])
    M = n_tok
    FW = min(512, M)
    fw = [sb(K, pfx + f"F{i}", [128, nT, FW], BF16) for i in range(3)]
    b_fw = P.bufs(pfx + "F", 3)
    kst = [sb(K, pfx + f"kst{i}", [128, 512], F32) for i in range(2)]
    b_kst = P.bufs(pfx + "kst", 2)
    Fv = Fm.rearrange("(k p) f -> p k f", p=128)
    fi = 0
    ki = 0
    for cs in range(2):
        for f0 in range(0, M, FW):
            i = fi % 3
            fi += 1
            P.dma("sp", fw[i][:], Fv[:, :, cs * M + f0:cs * M + f0 + FW], b_fw[i], reads=[K.b_in], partial=False)
            for fc in range(FW // 128):
                for o in range(2):
                    ps, bps = nextps(K)
                    for k in range(nT):
                        P.mm(lambda e, ps=ps, i=i, k=k, fc=fc, o=o, cs=cs: e.matmul(
                            ps[:, :], fw[i][:, k, fc * 128:(fc + 1) * 128], AB[:, o, cs, k, :],
                            start=(k == 0), stop=(k == nT - 1)), [b_fw[i], b_AB], bps, k == 0, k == nT - 1)
                    kk = ki % 2
                    ki += 1
                    P.op("act", lambda e, ps=ps, kk=kk: e.copy(kst[kk][:], ps[:, :]), reads=[bps], writes=[b_kst[kk]])
                    r0 = f0 + fc * 128
                    P.dma("sp", Kdst[o, cs, r0:r0 + 128, :], kst[kk][:], b_Kdst, reads=[b_kst[kk]])


def phase_hy(K, l, s):
    with ExitStack() as es2:
        K.es = es2
        hyena_for(K, l, s, 0, L, K.F_l, K.FT_l, K.Kf_l, K.b_Kf["l"], "hl")
    if l == 0:
        K.P.fence()
        with ExitStack() as es2:
            K.es = es2
            hyena_for(K, l, s, L, LC, K.F_c, K.FT_c, K.Kf_c, K.b_Kf["c"], "hc")


def hyena_for(K, l, s, tok0, n_tok, Fm, FTm, Kf, b_Kf, pfx):
    P = K.P
    nT = n_tok // 128
    M = n_tok
    nF = M // 128
    ident_b = K.mats_b[:, 0, :]
    tokU = sb(K, pfx + "tokU", [128, nT, 1536], BF16)
    b_tokU = P.buf(pfx + "tokU")
    outer_es = K.es
    pro_es = ExitStack()
    K.es = pro_es
    ub = [sb(K, pfx + f"ub{i}", [128, n_tok + 2], BF16) for i in range(2)]
    b_ub = P.bufs(pfx + "ub", 2)
    for i in range(2):
        P.op("pool", lambda e, i=i: e.memset(ub[i][:, 0:1], 0.0), partial=[b_ub[i]])
        P.op("pool", lambda e, i=i: e.memset(ub[i][:, n_tok + 1:n_tok + 2], 0.0), partial=[b_ub[i]])
    c1 = [sb(K, pfx + f"c1{i}", [128, n_tok], F32) for i in range(2)]
    b_c1 = P.bufs(pfx + "c1", 2)
    uc = [sb(K, pfx + f"uc{i}", [128, n_tok], BF16) for i in range(2)]
    b_uc = P.bufs(pfx + "uc", 2)
    zsrc = K.zhy[s].rearrange("(c p) t -> p c t", p=128)
    hcw, hcb = f"hcw{l}", f"hcb{l}"
    G = min(8, nT)
    for c in range(12):
        i = c % 2
        P.dma("sp", ub[i][:, 1:n_tok + 1], zsrc[:, c, tok0:tok0 + n_tok], b_ub[i], reads=[K.b_z[s]["zhy"]])
        P.op("dve", lambda e, i=i, c=c: e.tensor_scalar(
            c1[0][:], ub[i][:, 1:n_tok + 1], ppc(K, hcw, c * 3 + 1), ppc(K, hcb, c), ALU.mult, ALU.add),
            reads=[b_ub[i], K.b_const], writes=[b_c1[0]])
        P.op("dve", lambda e, i=i, c=c: e.scalar_tensor_tensor(
            c1[1][:], ub[i][:, 0:n_tok], ppc(K, hcw, c * 3 + 0), c1[0][:], ALU.mult, ALU.add),
            reads=[b_ub[i], b_c1[0], K.b_const], writes=[b_c1[1]])
        P.op("dve", lambda e, i=i, c=c: e.scalar_tensor_tensor(
            uc[i][:], ub[i][:, 2:n_tok + 2], ppc(K, hcw, c * 3 + 2), c1[1][:], ALU.mult, ALU.add),
            reads=[b_ub[i], b_c1[1], K.b_const], writes=[b_uc[i]])
        for g0 in range(0, nT, G):
            ps, bps = nextps(K)
            psb = ps[:].bitcast(BF16)
            for a in range(G):
                P.mm(lambda e, psb=psb, a=a, i=i, g0=g0: e.transpose(
                    psb[:, a * 128:(a + 1) * 128], uc[i][:, (g0 + a) * 128:(g0 + a + 1) * 128], ident_b),
                    [b_uc[i], K.b_const], bps, a == 0, a == G - 1)
            P.op("act", lambda e, psb=psb, g0=g0, c=c: e.copy(
                tokU[:, g0:g0 + G, c * 128:(c + 1) * 128], psb[:, 0:G * 128].rearrange("p (a n) -> p a n", a=G)),
                reads=[bps], partial=[b_tokU])
    pro_es.close()
    K.es = outer_es
    P.fence()
    FW = min(256, M)
    fw = [sb(K, pfx + f"F{i}", [128, nT, FW], BF16) for i in range(4)]
    b_fw = P.bufs(pfx + "F", 4)
    TWI = 128
    ftw = [sb(K, pfx + f"FT{i}", [128, 2 * nF, TWI], BF16) for i in range(2)]
    b_ftw = P.bufs(pfx + "FT", 2)
    kt = [sb(K, pfx + f"kt{i}", [128, 2, 512], F32) for i in range(2)]
    b_kt = P.bufs(pfx + "kt", 2)
    tmp = [sb(K, pfx + f"tmp{i}", [128, 512], F32) for i in range(4)]
    b_tmp = P.bufs(pfx + "tmp", 4)
    Pf = sb(K, pfx + "Pf", [128, 2 * nF, 512], BF16)
    b_Pf = P.buf(pfx + "Pf")
    z1 = sb(K, pfx + "z1", [128, nT, 512], BF16)
    b_z1 = P.buf(pfx + "z1")
    z2 = sb(K, pfx + "z2", [128, nT, 512], BF16)
    b_z2 = P.buf(pfx + "z2")
    Fv = Fm.rearrange("(k p) f -> p k f", p=128)
    FTv = FTm.rearrange("(k p) t -> p k t", p=128)
    nbc_sink = 2 * 16
    fi = 0
    ki = 0
    fti = 0
    for o in range(2):
        zin, b_zin = (tokU[:, :, 0:512], b_tokU) if o == 0 else (z1[:], b_z1)
        gate = tokU[:, :, 512 * (o + 1):512 * (o + 2)]
        zout, b_zout = (z1, b_z1) if o == 0 else (z2, b_z2)
        dcol = nbc_sink + (l * 2 + o) * 512
        for f0 in range(0, M, FW):
            ia = fi % 4
            ib = (fi + 1) % 4
            fi += 2
            P.dma("sp", fw[ia][:], Fv[:, :, f0:f0 + FW], b_fw[ia], reads=[K.b_in], partial=False)
            P.dma("sp", fw[ib][:], Fv[:, :, M + f0:M + f0 + FW], b_fw[ib], reads=[K.b_in], partial=False)
            for fc in range(FW // 128):
                fch = (f0 // 128) + fc
                kk = ki % 2
                ki += 1
                P.dma("sp", kt[kk][:], Kf[o, :, fch * 128:(fch + 1) * 128, :].rearrange("c p n -> p c n"),
                      b_kt[kk], reads=[b_Kf], partial=False)
                psc, bpsc = nextps(K)
                pss, bpss = nextps(K)
                for (pp_, bpp_, iw) in ((psc, bpsc, ia), (pss, bpss, ib)):
                    for k in range(nT):
                        P.mm(lambda e, pp_=pp_, iw=iw, k=k, fc=fc, zin=zin: e.matmul(
                            pp_[:, :], fw[iw][:, k, fc * 128:(fc + 1) * 128], zin[:, k, :],
                            start=(k == 0), stop=(k == nT - 1)), [b_fw[iw], b_zin], bpp_, k == 0, k == nT - 1)
                P.op("dve", lambda e, psc=psc, kk=kk: e.tensor_tensor(tmp[0][:], psc[:, :], kt[kk][:, 0, :], ALU.mult),
                     reads=[bpsc, b_kt[kk]], writes=[b_tmp[0]])
                P.op("dve", lambda e, pss=pss, kk=kk: e.tensor_tensor(tmp[1][:], pss[:, :], kt[kk][:, 1, :], ALU.mult),
                     reads=[bpss, b_kt[kk]], writes=[b_tmp[1]])
                P.op("pool", lambda e, fch=fch: e.tensor_tensor(Pf[:, fch, :], tmp[0][:], tmp[1][:], ALU.subtract),
                     reads=[b_tmp[0], b_tmp[1]], partial=[b_Pf])
                P.op("dve", lambda e, psc=psc, kk=kk: e.tensor_tensor(tmp[2][:], psc[:, :], kt[kk][:, 1, :], ALU.mult),
                     reads=[bpsc, b_kt[kk]], writes=[b_tmp[2]])
                P.op("dve", lambda e, pss=pss, kk=kk: e.tensor_tensor(tmp[3][:], pss[:, :], kt[kk][:, 0, :], ALU.mult),
                     reads=[bpss, b_kt[kk]], writes=[b_tmp[3]])
                P.op("pool", lambda e, fch=fch: e.tensor_tensor(Pf[:, nF + fch, :], tmp[2][:], tmp[3][:], ALU.add),
                     reads=[b_tmp[2], b_tmp[3]], partial=[b_Pf])
        for t0 in range(0, n_tok, TWI):
            it = fti % 2
            fti += 1
            P.dma("sp", ftw[it][:], FTv[:, :, t0:t0 + TWI], b_ftw[it], reads=[K.b_in], partial=False)
            for a in range(TWI // 128):
                tch = t0 // 128 + a
                ps, bps = nextps(K)
                for f in range(2 * nF):
                    P.mm(lambda e, ps=ps, it=it, f=f, a=a: e.matmul(
                        ps[:, :], ftw[it][:, f, a * 128:(a + 1) * 128], Pf[:, f, :],
                        start=(f == 0), stop=(f == 2 * nF - 1)), [b_ftw[it], b_Pf], bps, f == 0, f == 2 * nF - 1)
                P.op("pool", lambda e, tch=tch, zin=zin, dcol=dcol: e.tensor_tensor(
                    tmp[0][:], zin[:, tch, :], K.bct[:, dcol:dcol + 512], ALU.mult),
                    reads=[b_zin, K.b_const], writes=[b_tmp[0]])
                P.op("dve", lambda e, ps=ps: e.scalar_tensor_tensor(
                    tmp[1][:], ps[:, :], 1.0 / n_tok, tmp[0][:], ALU.mult, ALU.add),
                    reads=[bps, b_tmp[0]], writes=[b_tmp[1]])
                P.op("pool", lambda e, tch=tch, gate=gate, zout=zout: e.tensor_tensor(
                    zout[:, tch, :], tmp[1][:], gate[:, tch, :], ALU.mult),
                    reads=[b_tmp[1], b_tokU], partial=[b_zout])
    TG = min(4, nT)
    yst = [sb(K, pfx + f"yst{i}", [128, 4, TG * 128], BF16) for i in range(2)]
    b_yst = P.bufs(pfx + "yst", 2)
    ydst = K.ycat[s].rearrange("(c p) t -> p c t", p=128)
    yi = 0
    for g0 in range(0, nT, TG):
        i = yi % 2
        yi += 1
        for c4 in range(4):
            ps, bps = nextps(K)
            psb = ps[:].bitcast(BF16)
            for a in range(TG):
                P.mm(lambda e, psb=psb, a=a, g0=g0, c4=c4: e.transpose(
                    psb[:, a * 128:(a + 1) * 128], z2[:, g0 + a, c4 * 128:(c4 + 1) * 128], ident_b),
                    [b_z2, K.b_const], bps, a == 0, a == TG - 1)
            P.op("act", lambda e, psb=psb, i=i, c4=c4: e.copy(yst[i][:, c4, :], psb[:, 0:TG * 128]),
                 reads=[bps], partial=[b_yst[i]])
        P.dma("sp", ydst[:, 8:12, tok0 + g0 * 128:tok0 + (g0 + TG) * 128], yst[i][:], K.b_ycat[s]["hy"],
              reads=[b_yst[i]])


def phase_pool(K, l, s):
    with ExitStack() as es2:
        K.es = es2
        pool_for(K, l, s, 0, L, K.pinv_l, "pl")
    if l == 0:
        K.P.fence()
        with ExitStack() as es2:
            K.es = es2
            pool_for(K, l, s, L, LC, K.pinv_c, "pc")


def pool_for(K, l, s, tok0, n_tok, pinv, pfx):
    P = K.P
    PADW = n_tok + 32
    EW = n_tok + 16
    u = sb(K, pfx + "u", [128, PADW], BF16)
    b_u = P.buf(pfx + "u")
    inv = sb(K, pfx + "inv", [128, n_tok], F32)
    b_inv = P.buf(pfx + "inv")
    wa = sb(K, pfx + "wa", [128, EW], F32)
    wb = sb(K, pfx + "wb", [128, EW], F32)
    b_wa, b_wb = P.buf(pfx + "wa"), P.buf(pfx + "wb")
    diff = sb(K, pfx + "diff", [128, n_tok], BF16)
    b_diff = P.buf(pfx + "diff")
    pw = sb(K, pfx + "pw", [128, 4, 128], BF16)
    b_pw = P.buf(pfx + "pw")
    yst = [sb(K, pfx + f"yst{i}", [128, 512], BF16) for i in range(2)]
    b_yst = P.bufs(pfx + "yst", 2)
    P.dma("pool", pw[:], K.pool_w[l].rearrange("g c d -> c g d"), b_pw, reads=[K.b_in])
    P.op("dve", lambda e: e.memset(u[:, 0:16], 0.0), partial=[b_u])
    P.op("dve", lambda e: e.memset(u[:, 16 + n_tok:PADW], 0.0), partial=[b_u])
    zsrc = K.zpl[s].rearrange("(c p) t -> p c t", p=128)
    ydst = K.ycat[s].rearrange("(c p) t -> p c t", p=128)
    yi = 0
    for g in range(4):
        P.dma("sp", u[:, 16:16 + n_tok], zsrc[:, g, tok0:tok0 + n_tok], b_u, reads=[K.b_z[s]["zpl"]])
        P.dma("sp", inv[:], pinv[:, g, :], b_inv, reads=[K.b_in], partial=False)
        P.op("dve", lambda e: e.tensor_tensor(wa[:], u[:, 7:7 + EW], u[:, 8:8 + EW], ALU.add),
             reads=[b_u], writes=[b_wa])
        cur, b_cur, oth, b_oth = wa, b_wa, wb, b_wb
        step = 1
        for lvl in range(g):
            lo = (1, 3, 7)[lvl]
            P.op("dve", lambda e, cur=cur, oth=oth, lo=lo, step=step: e.tensor_tensor(
                oth[:, lo:EW - lo], cur[:, lo - step:EW - lo - step], cur[:, lo + step:EW - lo + step], ALU.add),
                reads=[b_cur], writes=[b_oth])
            cur, b_cur, oth, b_oth = oth, b_oth, cur, b_cur
            step *= 2
        P.op("dve", lambda e, cur=cur, oth=oth: e.tensor_tensor(oth[:, 0:n_tok], cur[:, 8:8 + n_tok], inv[:], ALU.mult),
             reads=[b_cur, b_inv], writes=[b_oth])
        P.op("dve", lambda e, oth=oth: e.tensor_tensor(diff[:], oth[:, 0:n_tok], u[:, 16:16 + n_tok], ALU.subtract),
             reads=[b_oth, b_u], writes=[b_diff])
        for t0 in range(0, n_tok, 512):
            T = min(512, n_tok - t0)
            ps, bps = nextps(K)
            P.mm(lambda e, ps=ps, g=g, t0=t0, T=T: e.matmul(ps[:, 0:T], pw[:, g, :], diff[:, t0:t0 + T],
                                                           start=True, stop=True), [b_pw, b_diff], bps, True, True)
            i = yi % 2
            yi += 1
            P.op("act", lambda e, ps=ps, i=i, g=g, T=T: e.activation(
                yst[i][:, 0:T], ps[:, 0:T], AF.Copy, scale=ppc(K, f"psc{l}", g)), reads=[bps, K.b_const],
                writes=[b_yst[i]])
            P.dma("sp", ydst[:, 12 + g, tok0 + t0:tok0 + t0 + T], yst[i][:, 0:T], K.b_ycat[s]["pool"],
                  reads=[b_yst[i]])


def phase_att(K, l, s):
    P = K.P
    with_ctx_q = (l == 0)
    ident_b = K.mats_b[:, 0, :]
    blk_b = K.mats_b[:, 2, :]
    perm_b = K.mats_b[:, 3, :]
    maskA = K.mats_b[:, 4, :]
    maskB = K.mats_b[:, 5, :]
    NTI = 5
    qT = sb(K, "at_q", [128, 8, TT], BF16)
    b_q = [[P.buf(f"at_q{c}_{t}") for t in range(NTI)] for c in range(8)]
    kd = [sb(K, f"at_k{g}", [128, TT], BF16) for g in range(4)]
    b_k = [[P.buf(f"at_k{g}_{t}") for t in range(NTI)] for g in range(4)]
    Va = sb(K, "at_v", [128, 18, 260], BF16)
    b_v = P.buf("at_v")
    rC = sb(K, "at_rC", [128, L], F32)
    rS = sb(K, "at_rS", [128, L], F32)
    b_rope = P.buf("at_rope")
    esink = sb(K, "at_es", [128, 16], F32)
    b_es = P.buf("at_es")
    P.dma("sp", rC[:], K.ropeC, b_rope, reads=[K.b_in])
    P.dma("sp", rS[:], K.ropeS, b_rope, reads=[K.b_in])
    P.op("act", lambda e: e.activation(esink[:], K.bct[:, l * 16:(l + 1) * 16], AF.Exp), reads=[K.b_const],
         writes=[b_es])
    P.dma("sp", Va[:], K.zv[s].rearrange("(c p) n -> p c n", p=128), b_v, reads=[K.b_z[s]["zv"]])
    zq = K.zq[s].rearrange("(c p) t -> p c t", p=128)
    tiles = [(t0, 512) for t0 in range(0, L, 512)] + [(L, LC)]
    for ti, (t0, T) in enumerate(tiles):
        for c in range(8):
            if ti == 4 and not with_ctx_q:
                continue
            P.dma("sp", qT[:, c, t0:t0 + T], zq[:, c, t0:t0 + T], b_q[c][ti], reads=[K.b_z[s]["zq"]])
        for g in range(4):
            for half in range(2):
                P.dma("sp", kd[g][half * 64:(half + 1) * 64, t0:t0 + T], K.zk[s][g * 64:(g + 1) * 64, t0:t0 + T],
                      b_k[g][ti], reads=[K.b_z[s]["zk"]])
    sq = [sb(K, f"at_sq{i}", [128, 512], BF16) for i in range(2)]
    b_sq = P.bufs("at_sq", 2)
    rs = [sb(K, f"at_rs{i}", [128, 512], F32) for i in range(2)]
    b_rs = P.bufs("at_rs", 2)
    qn = [sb(K, f"at_qn{i}", [128, 512], BF16) for i in range(2)]
    b_qn = P.bufs("at_qn", 2)
    t1 = [sb(K, f"at_t1{i}", [128, 512], F32) for i in range(2)]
    b_t1 = P.bufs("at_t1", 2)
    t2 = [sb(K, f"at_t2{i}", [128, 512], F32) for i in range(2)]
    b_t2 = P.bufs("at_t2", 2)
    it = 0
    items = []
    for ti, (t0, T) in enumerate(tiles):
        for c in range(8):
            if ti == 4 and not with_ctx_q:
                continue
            items.append((qT[:, c, t0:t0 + T], b_q[c][ti], f"qg{l}", ti, t0, T))
        for g in range(4):
            items.append((kd[g][:, t0:t0 + T], b_k[g][ti], f"kg{l}", ti, t0, T))
    for (xa, b_x, gname, ti, t0, T) in items:
        i = it % 2
        it += 1
        P.op("act", lambda e, xa=xa, i=i, T=T: e.activation(sq[i][:, 0:T], xa, AF.Square), reads=[b_x],
             writes=[b_sq[i]])
        ps, bps = nextps(K)
        P.mm(lambda e, ps=ps, i=i, T=T: e.matmul(ps[:, 0:T], blk_b, sq[i][:, 0:T], start=True, stop=True),
             [b_sq[i], K.b_const], bps, True, True)
        rsqrt_from_ps(K, rs[i][:, 0:T], b_rs[i], ps[:, 0:T], bps, 1.0 / 64)
        if ti == 4:
            P.op("dve", lambda e, xa=xa, i=i, T=T, gname=gname: e.scalar_tensor_tensor(
                xa, xa, ppc(K, gname), rs[i][:, 0:T], ALU.mult, ALU.mult), reads=[b_x, b_rs[i], K.b_const],
                writes=[b_x])
            continue
        P.op("dve", lambda e, xa=xa, i=i, T=T, gname=gname: e.scalar_tensor_tensor(
            qn[i][:, 0:T], xa, ppc(K, gname), rs[i][:, 0:T], ALU.mult, ALU.mult), reads=[b_x, b_rs[i], K.b_const],
            writes=[b_qn[i]])
        ps2, bps2 = nextps(K)
        P.mm(lambda e, ps2=ps2, i=i, T=T: e.matmul(ps2[:, 0:T], perm_b, qn[i][:, 0:T], start=True, stop=True),
             [b_qn[i], K.b_const], bps2, True, True)
        P.op("pool", lambda e, i=i, t0=t0, T=T: e.tensor_tensor(t1[i][:, 0:T], qn[i][:, 0:T], rC[:, t0:t0 + T], ALU.mult),
             reads=[b_qn[i], b_rope], writes=[b_t1[i]])
        P.op("dve", lambda e, ps2=ps2, i=i, t0=t0, T=T: e.tensor_tensor(t2[i][:, 0:T], ps2[:, 0:T], rS[:, t0:t0 + T],
                                                                         ALU.mult),
             reads=[bps2, b_rope], writes=[b_t2[i]])
        P.op("pool", lambda e, xa=xa, i=i, T=T: e.tensor_tensor(xa, t1[i][:, 0:T], t2[i][:, 0:T], ALU.add),
             reads=[b_t1[i], b_t2[i]], writes=[b_x])
    Pt = [sb(K, f"at_P{i}", [128, 5, 128], BF16) for i in range(2)]
    b_P = P.bufs("at_P", 2)
    yt = [sb(K, f"at_y{i}", [128, 16, 64], BF16) for i in range(2)]
    b_y = P.bufs("at_y", 2)
    dsum = [sb(K, f"at_ds{i}", [128, 16], F32) for i in range(2)]
    b_ds = P.bufs("at_ds", 2)
    yst = [sb(K, f"at_yst{i}", [128, 8, 512], BF16) for i in range(2)]
    b_yst = P.bufs("at_yst", 2)
    ydst = K.ycat[s].rearrange("(c p) t -> p c t", p=128)
    blocks = list(range(16)) + ([16, 17] if with_ctx_q else [])
    pi = 0
    sli = 0
    sci = 0
    ysi = 0
    grp_start = 0
    for bi_, n in enumerate(blocks):
        is_cq = n >= 16
        tq = n // 4
        if is_cq:
            loc = []
        else:
            loc = [j for j in (n - 1, n, n + 1) if 0 <= j < 16]
        chunks = [(j, idx) for idx, j in enumerate(loc)] + [(16, 3), (17, 4)]
        yi = bi_ % 2
        for h in range(16):
            g = h // 4
            base = (h % 2) * 64
            c = h // 2
            qa = qT[base:base + 64, c, n * 128:(n + 1) * 128]
            p_i = pi % 2
            pi += 1
            if loc:
                bl = sli % 2
                sli += 1
                Sl, b_Sl = K.ps[bl], K.b_ps[bl]
                ops = []
                for idx, j in enumerate(loc):
                    ops.append((idx, j, None))
                    if j == n - 1:
                        ops.append((idx, j, maskA))
                    elif j == n + 1:
                        ops.append((idx, j, maskB))
                for oi, (idx, j, msk) in enumerate(ops):
                    first, last = oi == 0, oi == len(ops) - 1
                    has_mask = (j != n)
                    if msk is None:
                        P.mm(lambda e, Sl=Sl, idx=idx, j=j, g=g, base=base, qa=qa, has_mask=has_mask: e.matmul(
                            Sl[:, idx * 128:(idx + 1) * 128], kd[g][base:base + 64, j * 128:(j + 1) * 128], qa,
                            start=True, stop=not has_mask), [b_k[g][j // 4], b_q[c][tq]], b_Sl, first, last)
                    else:
                        P.mm(lambda e, Sl=Sl, idx=idx, msk=msk: e.matmul(
                            Sl[:, idx * 128:(idx + 1) * 128], ident_b, msk, start=False, stop=True),
                            [K.b_const], b_Sl, first, last)
                nl = len(loc)
                P.op("act", lambda e, Sl=Sl, p_i=p_i, nl=nl: e.activation(
                    Pt[p_i][:, 0:nl, :], Sl[:, 0:nl * 128].rearrange("p (a n) -> p a n", a=nl), AF.Exp, scale=0.125),
                    reads=[b_Sl], partial=[b_P[p_i]])
            bc_ = 2 + sci % 2
            sci += 1
            Sc, b_Sc = K.ps[bc_], K.b_ps[bc_]
            for ci in range(2):
                j = 16 + ci
                P.mm(lambda e, Sc=Sc, ci=ci, j=j, g=g, base=base, qa=qa: e.matmul(
                    Sc[:, ci * 128:(ci + 1) * 128], kd[g][base:base + 64, j * 128:(j + 1) * 128], qa,
                    start=True, stop=True), [b_k[g][4], b_q[c][tq]], b_Sc, ci == 0, ci == 1)
            P.op("act", lambda e, Sc=Sc, p_i=p_i: e.activation(
                Pt[p_i][:, 3:5, :], Sc[:, 0:256].rearrange("p (a n) -> p a n", a=2), AF.Exp, scale=0.125),
                reads=[b_Sc], partial=[b_P[p_i]])
            ob = 5 + h // 7
            off = (h % 7) * 65
            O, b_O = K.ps[ob], K.b_ps[ob]
            hb_first = (h % 7 == 0)
            hb_last = (h % 7 == 6) or (h == 15)
            for ki_, (j, slot) in enumerate(chunks):
                P.mm(lambda e, O=O, off=off, p_i=p_i, slot=slot, j=j, g=g, ki_=ki_, nch=len(chunks): e.matmul(
                    O[:, off:off + 65], Pt[p_i][:, slot, :], Va[:, j, g * 65:(g + 1) * 65],
                    start=(ki_ == 0), stop=(ki_ == nch - 1)), [b_P[p_i], b_v], b_O,
                    hb_first and ki_ == 0, hb_last and ki_ == len(chunks) - 1)
        for bk, (h0, nh) in enumerate(((0, 7), (7, 7), (14, 2))):
            O, b_O = K.ps[5 + bk], K.b_ps[5 + bk]
            Ov = O[:, 0:nh * 65].rearrange("p (h d) -> p h d", d=65)
            P.op("dve", lambda e, Ov=Ov, h0=h0, nh=nh, yi=yi: e.tensor_tensor(
                dsum[yi][:, h0:h0 + nh], Ov[:, :, 64], esink[:, h0:h0 + nh], ALU.add),
                reads=[b_O, b_es], partial=[b_ds[yi]])
            P.op("dve", lambda e, h0=h0, nh=nh, yi=yi: e.reciprocal(dsum[yi][:, h0:h0 + nh], dsum[yi][:, h0:h0 + nh]),
                 reads=[b_ds[yi]], writes=[b_ds[yi]])
            for hh in range(nh):
                P.op("dve", lambda e, Ov=Ov, hh=hh, h0=h0, yi=yi: e.tensor_scalar(
                    yt[yi][:, h0 + hh, :], Ov[:, hh, 0:64], dsum[yi][:, h0 + hh:h0 + hh + 1], None, ALU.mult),
                    reads=[b_O, b_ds[yi]], partial=[b_y[yi]])
        Tb = K.ps[4][:].bitcast(BF16)
        b_T = K.b_ps[4]
        yflat = yt[yi][:].rearrange("p h d -> p (h d)")
        for c in range(8):
            P.mm(lambda e, Tb=Tb, c=c, yflat=yflat: e.transpose(
                Tb[:, c * 128:(c + 1) * 128], yflat[:, c * 128:(c + 1) * 128], ident_b),
                [b_y[yi], K.b_const], b_T, c == 0, c == 7)
        a = n % 4
        ys = ysi % 2
        P.op("act", lambda e, Tb=Tb, ys=ys, a=a: e.copy(
            yst[ys][:, :, a * 128:(a + 1) * 128], Tb[:, 0:1024].rearrange("p (c n) -> p c n", c=8)),
            reads=[b_T], partial=[b_yst[ys]])
        end_grp = (a == 3) or (n == blocks[-1])
        if end_grp:
            g0 = (n // 4) * 4
            w = (a + 1) * 128
            P.dma("sp", ydst[:, 0:8, g0 * 128:g0 * 128 + w], yst[ys][:, :, 0:w], K.b_ycat[s]["att"],
                  reads=[b_yst[ys]])
            ysi += 1


def phase_B(K, l, s):
    P = K.P
    last = (l == DEPTH - 1)
    wring(K, 3)
    tmp = h_tmp(K, "pb_")
    ident_f = K.mats_f[:, 0, :]
    xt = sb(K, "pb_x", [128, 16, 512], F32)
    b_xt = P.buf("pb_x")
    h = sb(K, "pb_h", [128, 16, 512], BF16)
    b_h = P.buf("pb_h")
    yc = sb(K, "pb_yc", [128, 16, 512], BF16)
    b_yc = P.buf("pb_yc")
    m = sb(K, "pb_m", [128, 16, 512], BF16)
    b_m = P.buf("pb_m")
    gs = [sb(K, f"pb_gs{i}", [128, 512], BF16) for i in range(2)]
    b_gs = P.bufs("pb_gs", 2)
    acc = sb(K, "pb_acc", [128, 4, 512], F32)
    b_acc = P.bufs("pb_acc", 4)
    tr = [sb(K, f"pb_tr{i}", [128, 512], F32) for i in range(2)]
    b_tr = P.bufs("pb_tr", 2)
    ev = [sb(K, f"pb_ev{i}", [128, 512], F32) for i in range(4)]
    b_ev = P.bufs("pb_ev", 4)
    ei_ = 0
    hid = [sb(K, f"pb_hid{i}", [128, 4, 512], BF16) for i in range(2)]
    b_hid = P.bufs("pb_hid", 2)
    b_ost = [P.buf("pb_ost")] * 2
    xsrc = K.xT[l % 2][s].rearrange("(c p) t -> p c t", p=128)
    xdst = K.xT[(l + 1) % 2][s].rearrange("(c p) t -> p c t", p=128)
    ysrc = K.ycat[s].rearrange("(c p) t -> p c t", p=128)
    win = K.wb["w_in"][l].rearrange("(k p) n -> p k n", p=128)
    b_win = K.b_wb["w_in"][l]
    branches = [(GATE_OFF, K.wb["w_att_o"][l].rearrange("(k p) n -> p k n", p=128), 8, 0, K.b_wb["w_att_o"][l]),
                (GATE_OFF + D, K.wb["w_hy_o"][l].rearrange("(k p) n -> p k n", p=128), 4, 8, K.b_wb["w_hy_o"][l]),
                (GATE_OFF + 2 * D, K.wb["w_pool_o"][l].rearrange("(k p) n -> p k n", p=128), 4, 12,
                 K.b_wb["w_pool_o"][l])]
    wout = K.wb["w_out"][l].rearrange("(k p) n -> p k n", p=128)
    w1v = K.wb["mlp_w1"][l].rearrange("(k p) n -> p k n", p=128)
    gi = 0
    ti_ = 0
    hi_ = 0
    oi = 0
    for (t0, T) in tiles_of(l, with_ctx=not last):
        is_ctx = t0 >= L
        who = 2 if is_ctx else s
        P.dma("act", xt[:, :, 0:T], xsrc[:, :, t0:t0 + T], b_xt, reads=[K.b_xT[l % 2][s]], partial=False)
        P.dma("act", yc[:, :, 0:T], ysrc[:, :, t0:t0 + T], b_yc,
              reads=[K.b_ycat[s]["att"], K.b_ycat[s]["hy"], K.b_ycat[s]["pool"]], partial=False)
        make_h(K, xt, b_xt, h, b_h, T, l, who, 0, tmp)
        for J in range(4):
            for r, (gcol, wo_ap, kr, yoff, b_wsrc) in enumerate(branches):
                wg, b_wg = wload(K, win[:, :, gcol + J * 512:gcol + (J + 1) * 512], 16, 512, b_win)
                wo, b_wo = wload(K, wo_ap[:, :, J * 512:(J + 1) * 512], kr, 512, b_wsrc)
                for j in range(4):
                    psg, bpsg = nextps(K)
                    for k in range(16):
                        P.mm(lambda e, psg=psg, wg=wg, j=j, k=k, T=T: e.matmul(
                            psg[:, 0:T], wg[:, k, j * 128:(j + 1) * 128], h[:, k, 0:T], start=(k == 0), stop=(k == 15)),
                            [b_wg, b_h], bpsg, k == 0, k == 15)
                    pso, bpso = nextps(K)
                    for k in range(kr):
                        P.mm(lambda e, pso=pso, wo=wo, j=j, k=k, T=T, yoff=yoff, kr=kr: e.matmul(
                            pso[:, 0:T], wo[:, k, j * 128:(j + 1) * 128], yc[:, yoff + k, 0:T],
                            start=(k == 0), stop=(k == kr - 1)), [b_wo, b_yc], bpso, k == 0, k == kr - 1)
                    g_i = gi % 2
                    gi += 1
                    P.op("act", lambda e, psg=psg, g_i=g_i, T=T: e.activation(gs[g_i][:, 0:T], psg[:, 0:T], AF.Sigmoid),
                         reads=[bpsg], writes=[b_gs[g_i]])
                    if r == 0:
                        P.op("dve", lambda e, pso=pso, g_i=g_i, j=j, T=T: e.tensor_tensor(
                            acc[:, j, 0:T], pso[:, 0:T], gs[g_i][:, 0:T], ALU.mult), reads=[bpso, b_gs[g_i]],
                            writes=[b_acc[j]])
                    else:
                        t_i = ti_ % 2
                        ti_ += 1
                        P.op("dve", lambda e, pso=pso, g_i=g_i, t_i=t_i, T=T: e.tensor_tensor(
                            tr[t_i][:, 0:T], pso[:, 0:T], gs[g_i][:, 0:T], ALU.mult), reads=[bpso, b_gs[g_i]],
                            writes=[b_tr[t_i]])
                        if r == 1:
                            P.op("pool", lambda e, t_i=t_i, j=j, T=T: e.tensor_tensor(
                                acc[:, j, 0:T], acc[:, j, 0:T], tr[t_i][:, 0:T], ALU.add), reads=[b_tr[t_i], b_acc[j]],
                                writes=[b_acc[j]])
                        else:
                            P.op("pool", lambda e, t_i=t_i, j=j, J=J, T=T: e.tensor_tensor(
                                m[:, 4 * J + j, 0:T], acc[:, j, 0:T], tr[t_i][:, 0:T], ALU.add),
                                reads=[b_tr[t_i], b_acc[j]], partial=[b_m])
        if K.cut == 1:
            P.dma("sp", xdst[:, :, t0:t0 + T], xt[:, :, 0:T], K.b_xT[(l + 1) % 2][s], reads=[b_xt, b_m])
            continue
        for J in range(4):
            wo, b_wo = wload(K, wout[:, :, J * 512:(J + 1) * 512], 16, 512, K.b_wb["w_out"][l])
            for j in range(4):
                cc = 4 * J + j
                ps, bps = nextps(K)
                for k in range(16):
                    P.mm(lambda e, ps=ps, wo=wo, j=j, k=k, T=T: e.matmul(
                        ps[:, 0:T], wo[:, k, j * 128:(j + 1) * 128], m[:, k, 0:T], start=(k == 0), stop=(k == 15)),
                        [b_wo, b_m], bps, k == 0, k == 15)
                e_i = ei_ % 4
                ei_ += 1
                P.op("act", lambda e, ps=ps, cc=cc, T=T, who=who, e_i=e_i: e.activation(
                    ev[e_i][:, 0:T], ps[:, 0:T], AF.Copy, scale=modcol(K, l, who, 2, cc)),
                    reads=[bps, K.b_mod], writes=[b_ev[e_i]])
                P.op("pool", lambda e, cc=cc, T=T, e_i=e_i: e.tensor_tensor(
                    xt[:, cc, 0:T], xt[:, cc, 0:T], ev[e_i][:, 0:T], ALU.add),
                    reads=[b_ev[e_i], b_xt], writes=[b_xt])
        if K.cut == 2:
            P.dma("sp", xdst[:, :, t0:t0 + T], xt[:, :, 0:T], K.b_xT[(l + 1) % 2][s], reads=[b_xt, b_m])
            continue
        make_h(K, xt, b_xt, h, b_h, T, l, who, 1, tmp)
        if K.cut == 3:
            P.dma("sp", xdst[:, :, t0:t0 + T], xt[:, :, 0:T], K.b_xT[(l + 1) % 2][s], reads=[b_xt, b_h])
            continue
        for Hb in range(16):
            w1, b_w1 = wload(K, w1v[:, :, Hb * 512:(Hb + 1) * 512], 16, 512, K.b_wb["mlp_w1"][l])
            w2, b_w2 = wload(K, K.wb["mlp_w2"][l][Hb * 512:(Hb + 1) * 512, :].rearrange("(k p) n -> p k n", p=128),
                             4, D, K.b_wb["mlp_w2"][l])
            h_i = hi_ % 2
            hi_ += 1
            for jj in range(4):
                ps, bps = nextps(K)
                for k in range(16):
                    P.mm(lambda e, ps=ps, w1=w1, jj=jj, k=k, T=T: e.matmul(
                        ps[:, 0:T], w1[:, k, jj * 128:(jj + 1) * 128], h[:, k, 0:T], start=(k == 0), stop=(k == 15)),
                        [b_w1, b_h], bps, k == 0, k == 15)
                t_i = ti_ % 2
                ti_ += 1
                P.op("act", lambda e, ps=ps, t_i=t_i, T=T: e.activation(tr[t_i][:, 0:T], ps[:, 0:T], AF.Square),
                     reads=[bps], writes=[b_tr[t_i]])
                P.op("dve", lambda e, ps=ps, t_i=t_i, h_i=h_i, jj=jj, T=T: e.scalar_tensor_tensor(
                    hid[h_i][:, jj, 0:T], ps[:, 0:T], 0.0, tr[t_i][:, 0:T], ALU.is_gt, ALU.mult),
                    reads=[bps, b_tr[t_i]], partial=[b_hid[h_i]])
            if K.cut == 4:
                P.dma("sp", xdst[:, 0:4, t0:t0 + T], hid[h_i][:, :, 0:T].bitcast(F32) if False else xt[:, 0:4, 0:T],
                      K.b_xT[(l + 1) % 2][s], reads=[b_xt, b_hid[h_i], b_w2])
                continue
            for j in range(16):
                ps, bps = nextps(K)
                for k in range(4):
                    P.mm(lambda e, ps=ps, w2=w2, j=j, k=k, T=T, h_i=h_i: e.matmul(
                        ps[:, 0:T], w2[:, k, j * 128:(j + 1) * 128], hid[h_i][:, k, 0:T], start=(k == 0), stop=(k == 3)),
                        [b_w2, b_hid[h_i]], bps, k == 0, k == 3)
                if K.cut == 7:
                    P.op("dve", lambda e, ps=ps, j=j, T=T, who=who: e.scalar_tensor_tensor(
                        tr[0][:, 0:T], ps[:, 0:T], modcol(K, l, who, 5, j), xt[:, j, 0:T], ALU.mult, ALU.add),
                        reads=[bps, K.b_mod, b_xt], writes=[b_tr[0]])
                    continue
                if K.cut == 8:
                    P.op("dve", lambda e, ps=ps, j=j, T=T, who=who: e.tensor_tensor(
                        xt[:, j, 0:T], ps[:, 0:T], xt[:, j, 0:T], ALU.add),
                        reads=[bps, b_xt], writes=[b_xt])
                    continue
                if K.cut == 6:
                    P.op("dve", lambda e, ps=ps, T=T: e.tensor_copy(tr[0][:, 0:T], ps[:, 0:T]),
                         reads=[bps], writes=[b_tr[0]])
                    continue
                e_i = ei_ % 4
                ei_ += 1
                P.op("act", lambda e, ps=ps, j=j, T=T, who=who, e_i=e_i: e.activation(
                    ev[e_i][:, 0:T], ps[:, 0:T], AF.Copy, scale=modcol(K, l, who, 5, j)),
                    reads=[bps, K.b_mod], writes=[b_ev[e_i]])
                P.op("pool", lambda e, j=j, T=T, e_i=e_i: e.tensor_tensor(
                    xt[:, j, 0:T], xt[:, j, 0:T], ev[e_i][:, 0:T], ALU.add),
                    reads=[b_ev[e_i], b_xt], writes=[b_xt])
        if not last:
            P.dma("act", xdst[:, :, t0:t0 + T], xt[:, :, 0:T], K.b_xT[(l + 1) % 2][s], reads=[b_xt])
            if K.cut in (5, 6, 7, 8):
                break
        else:
            P.dma("act", K.out[s].rearrange("(c p) t -> p c t", p=128)[:, :, t0:t0 + T], xt[:, :, 0:T], K.b_out,
                  reads=[b_xt])


def make_in_maps(inp, cores):
    c = _consts()
    f32 = lambda a: np.ascontiguousarray(np.asarray(a, np.float32))
    shared = {k: f32(inp[k]) for k in ("w_mod", "w_in", "filt_w0", "filt_w1", "filt_w2", "pool_w", "w_att_o",
                                       "w_hy_o", "w_pool_o", "w_out", "mlp_w1", "mlp_w2")}
    shared.update(c)
    shared["bc"] = _bc_layout(inp)
    maps = []
    pp_off = None
    for core in cores:
        pp = _pp_layout(inp, core)
        m = dict(shared)
        xs = np.concatenate([inp["x"][core * SPC:(core + 1) * SPC], inp["ctx"][core * SPC:(core + 1) * SPC]], axis=1)
        m["xTin"] = f32(xs.transpose(0, 2, 1))
        m["pp"] = pp.build()
        pp_off = (pp.off, pp.n)
        maps.append(m)
    return maps, pp_off, shared["bc"].shape[1]


def kernel(**inputs):
    inp = {k: np.asarray(v) for k, v in inputs.items()}
    cores = list(range(NCORES))
    maps, (off, npp), nbc = make_in_maps(inp, cores)
    nc, P = build_program(off, npp, nbc)
    res = run_bass_kernel_spmd(nc, maps, core_ids=cores)
    outs = [np.asarray(r["outT"]).transpose(0, 2, 1) for r in res.results]
    return np.ascontiguousarray(np.concatenate(outs, axis=0).astype(np.float32))
```

```python
import math
from contextlib import ExitStack
import numpy as np
import ml_dtypes
import concourse.bass as bass
import concourse.mybir as mybir
from concourse.bass_utils import run_bass_kernel_spmd

F32 = mybir.dt.float32
BF16 = mybir.dt.bfloat16
ALU = mybir.AluOpType
AF = mybir.ActivationFunctionType

D = 2048
L = 2048
LC = 256
TT = L + LC
DEPTH = 2
NCORES = 8
SPC = 2
EPS = 1e-6
IN_W = 9728
Q_OFF, K_OFF, V_OFF, HY_OFF, POOL_OFF, GATE_OFF = 0, 1024, 1280, 1536, 3072, 3584
D_FF = 8192
NPBF = np.dtype(ml_dtypes.bfloat16)


class Buf:
    __slots__ = ("name", "W", "R", "G", "sem", "cnt")

    def __init__(self, name, init_readers=None):
        self.name = name
        self.W = {}
        self.R = dict(init_readers) if init_readers else {}
        self.G = {}
        self.sem = None
        self.cnt = 0


class Op:
    __slots__ = ("eng", "fn", "deps", "dma", "token", "signal", "pos", "sigval", "key", "nofence")


class Prog:
    ENGS = ("pe", "act", "dve", "pool", "sp")

    def __init__(self, nc):
        self.nc = nc
        self.ops = {e: [] for e in self.ENGS}
        self.sems = {}
        self.dma_sems = []
        self.clock = {e: {} for e in self.ENGS}
        self.last_dma = {}
        self.fence_readers = {}
        self.nsem = 0
        self.scope = None
        self.free_sems = []
        self.final_waits = {}

    def buf(self, name):
        b = Buf(name, self.fence_readers)
        if self.scope is not None:
            self.scope.append(b)
        return b

    def push_scope(self):
        self.scope = []

    def pop_scope(self):
        for b in self.scope:
            if b.sem is not None:
                self.free_sems.append((b.sem, b.cnt))
                if b in self.dma_sems:
                    self.dma_sems.remove(b)
                self.final_waits[id(b.sem)] = (b.sem, b.cnt)
        self.scope = None

    def bufs(self, name, n):
        return [self.buf(f"{name}{i}") for i in range(n)]

    def fence(self):
        fr = {}
        for e in ("pe", "act", "dve", "pool"):
            for op in reversed(self.ops[e]):
                if not op.dma:
                    fr[("e", e)] = op
                    break
        for k, op in self.last_dma.items():
            if getattr(op, "nofence", False):
                continue
            fr[("d", k)] = op
        self.fence_readers = fr

    def _new_sem(self, name):
        s = self.nc.alloc_semaphore(name)
        self.nsem += 1
        return s

    def _add(self, eng, fn, reads, writes, partial, dma_dest, mm_first, mm_last):
        op = Op()
        op.nofence = False
        op.eng = eng
        op.fn = fn
        op.dma = dma_dest is not None
        op.signal = False
        op.sigval = None
        op.token = None
        deps = {}

        def add_deps(d):
            for k, a in d.items():
                old = deps.get(k)
                if old is None or self._later(a, old):
                    deps[k] = a

        if op.dma:
            b = dma_dest
            if b.sem is None:
                if self.free_sems:
                    b.sem, b.cnt = self.free_sems.pop()
                else:
                    b.sem = self._new_sem("d" + str(self.nsem))
                self.dma_sems.append(b)
            b.cnt += 16
            op.token = (b.sem, b.cnt)
            op.key = ("d", id(b.sem))
        else:
            op.key = ("e", eng)
        for b in reads:
            add_deps(b.W)
        for b in writes:
            if mm_first is False:
                pass
            else:
                add_deps(b.W)
                add_deps(b.R)
        for b in partial:
            if b.R:
                b.G = dict(b.R)
            add_deps(b.G)
        for b in reads:
            b.R[op.key] = op
        for b in writes:
            if mm_last is False:
                if mm_first:
                    b.W = {}
                    b.R = {}
                continue
            b.W = {op.key: op}
            b.R = {}
            b.G = {}
        for b in partial:
            if b.R:
                b.W = {op.key: op}
                b.R = {}
            else:
                b.W[op.key] = op
        clk = self.clock[eng]
        final = []
        op.pos = len(self.ops[eng])
        for k, a in deps.items():
            if a is op:
                continue
            if a.dma:
                sem, val = a.token
                if clk.get(k, 0) >= val:
                    continue
                clk[k] = val
                final.append(a)
            else:
                if a.eng == "pe" and eng == "pe":
                    continue
                if clk.get(k, -1) >= a.pos:
                    continue
                clk[k] = a.pos
                a.signal = True
                final.append(a)
        op.deps = final
        self.ops[eng].append(op)
        if op.dma:
            self.last_dma[id(op.token[0])] = op
        return op

    @staticmethod
    def _later(a, b):
        if a.dma:
            return a.token[1] > b.token[1]
        return a.pos > b.pos

    def op(self, eng, fn, reads=(), writes=(), partial=()):
        return self._add(eng, fn, reads, writes, partial, None, None, None)

    def mm(self, fn, reads, out, first, last):
        return self._add("pe", fn, reads, (out,), (), None, first, last)

    def dma(self, eng, out_ap, in_ap, dst, reads=(), partial=True, **kw):
        fn = lambda e: e.dma_start(out=out_ap, in_=in_ap, **kw)
        if partial:
            return self._add(eng, fn, reads, (), (dst,), dst, None, None)
        return self._add(eng, fn, reads, (dst,), (), dst, None, None)

    def emit(self):
        nc = self.nc
        esem = {e: self._new_sem("e_" + e) for e in ("pe", "act", "dve", "pool")}
        for e in ("pe", "act", "dve", "pool"):
            n = 0
            for op in self.ops[e]:
                if not op.dma and op.signal:
                    n += 1
                    op.sigval = n
        handles = {"pe": "tensor", "act": "scalar", "dve": "vector", "pool": "gpsimd", "sp": "sync"}
        stats = {}
        with nc.Block() as block:
            for e in self.ENGS:
                ops = self.ops[e]
                if not ops and e != "sp":
                    continue

                def body(h, ops=ops, e=e):
                    nw = 0
                    for op in ops:
                        for a in op.deps:
                            if a.dma:
                                h.wait_ge(a.token[0], a.token[1])
                            else:
                                h.wait_ge(esem[a.eng], a.sigval)
                            nw += 1
                        ins = op.fn(h)
                        if op.dma:
                            ins.then_inc(op.token[0], 16)
                        elif op.signal:
                            ins.then_inc(esem[e], 1)
                    if e == "sp":
                        fw = dict(self.final_waits)
                        for b in self.dma_sems:
                            fw[id(b.sem)] = (b.sem, b.cnt)
                        for (sm, cnt) in fw.values():
                            h.wait_ge(sm, cnt)
                    stats[e] = (len(ops), nw)

                getattr(block, handles[e])(body)
        self.stats = stats


def _bf(a):
    return np.ascontiguousarray(a.astype(NPBF))


_CONST_CACHE = {}


def _dft_consts(n_tok):
    n = 2 * n_tok
    t = np.arange(n_tok, dtype=np.float64)[:, None]
    f = np.arange(n_tok, dtype=np.float64)[None, :]
    ang = 2.0 * np.pi * (f + 0.5) * t / n
    Fm = np.concatenate([np.cos(ang), np.sin(ang)], axis=1)
    return _bf(Fm), _bf(Fm.T)


def _filter_consts(n_tok):
    t = np.linspace(0.0, 1.0, n_tok, dtype=np.float32)[:, None]
    w = (2.0 * math.pi * np.arange(n_tok, dtype=np.float32)[:, None] / n_tok).astype(np.float32)
    bands = np.linspace(1e-4, 15, 16, dtype=np.float32)[None, :]
    z = np.concatenate([t, np.cos(bands * w), -np.sin(bands * w)], axis=-1).astype(np.float32)
    deltas = np.linspace(math.log(1e-2) / 1.5, math.log(1e-2) / 0.3, 512, dtype=np.float32)
    decay = np.exp(-t * np.abs(deltas)[None, :]).astype(np.float32)
    zs = np.zeros_like(z)
    zs[1:] = z[:-1]
    ds = np.zeros_like(decay)
    ds[1:] = decay[:-1]
    zz = np.stack([z.T, zs.T], 0)
    dd = np.stack([decay, ds], 0)
    return np.ascontiguousarray(zz), np.ascontiguousarray(dd)


def _pool_inv(n_tok):
    t = np.arange(n_tok)
    out = np.zeros((4, n_tok), np.float32)
    for g, win in enumerate((2, 4, 8, 16)):
        a = np.clip(t - win // 2, 0, n_tok)
        b = np.clip(t + win // 2, 0, n_tok)
        out[g] = 1.0 / (b - a).astype(np.float32)
    return np.ascontiguousarray(np.broadcast_to(out[None], (128, 4, n_tok)))


def _rope_tabs():
    rows = L // 64
    row = np.repeat(np.arange(rows, dtype=np.float32), 64)
    col = np.tile(np.arange(64, dtype=np.float32), rows)
    inv = (10000.0 ** (-np.arange(16, dtype=np.float32) / 16)).astype(np.float32)
    ar = row[:, None] * inv
    ac = col[:, None] * inv
    C = np.zeros((64, L), np.float32)
    S = np.zeros((64, L), np.float32)
    for ax, a in enumerate((ar, ac)):
        c = np.cos(a).T
        s = np.sin(a).T
        C[ax * 32:ax * 32 + 16] = c
        C[ax * 32 + 16:ax * 32 + 32] = c
        S[ax * 32:ax * 32 + 16] = -s
        S[ax * 32 + 16:ax * 32 + 32] = s
    C = np.concatenate([C, C], 0)
    S = np.concatenate([S, S], 0)
    return np.ascontiguousarray(C), np.ascontiguousarray(S)


def _misc_mats():
    ident = np.eye(128, dtype=np.float32)
    ones = np.ones((128, 128), np.float32)
    blk = np.zeros((128, 128), np.float32)
    blk[:64, :64] = 1
    blk[64:, 64:] = 1
    perm = np.zeros((128, 128), np.float32)
    for p in range(128):
        r = p % 32
        q = p - r + (r + 16) % 32
        perm[q, p] = 1.0
    ki = np.arange(128)[:, None]
    qi = np.arange(128)[None, :]
    maskA = np.where(qi > ki, -30000.0, 0.0).astype(np.float32)
    maskB = np.where(ki > qi, -30000.0, 0.0).astype(np.float32)
    m = np.stack([ident, ones, blk, perm, maskA, maskB], 0)
    return np.ascontiguousarray(m.transpose(1, 0, 2))


def _consts():
    if "c" in _CONST_CACHE:
        return _CONST_CACHE["c"]
    c = {}
    c["F_l"], c["FT_l"] = _dft_consts(L)
    c["F_c"], c["FT_c"] = _dft_consts(LC)
    c["zz_l"], c["dd_l"] = _filter_consts(L)
    c["zz_c"], c["dd_c"] = _filter_consts(LC)
    c["pinv_l"] = _pool_inv(L)
    c["pinv_c"] = _pool_inv(LC)
    c["ropeC"], c["ropeS"] = _rope_tabs()
    c["mats"] = _misc_mats()
    _CONST_CACHE["c"] = c
    return c


class PP:
    def __init__(self):
        self.cols = []
        self.off = {}
        self.n = 0

    def add(self, name, arr):
        arr = np.asarray(arr, np.float32).reshape(128, -1)
        self.off[name] = (self.n, arr.shape[1])
        self.cols.append(arr)
        self.n += arr.shape[1]

    def build(self):
        return np.ascontiguousarray(np.concatenate(self.cols, axis=1))


def _chunked(v, nch):
    return np.asarray(v, np.float32).reshape(nch, 128).T


def _pp_layout(inp, core):
    pp = PP()
    b0 = core * SPC
    cv = np.stack([inp["c"][b0], inp["c"][b0 + 1], inp["c_ctx"]], 0)
    pp.add("cvec", cv.reshape(3, 16, 128).transpose(2, 1, 0).reshape(128, 48))
    for l in range(DEPTH):
        pp.add(f"n1g{l}", _chunked(inp["norm1_g"][l], 16))
        pp.add(f"n2g{l}", _chunked(inp["norm2_g"][l], 16))
        pp.add(f"bmod{l}", _chunked(inp["b_mod"][l], 96))
        pp.add(f"qg{l}", np.tile(inp["q_norm_g"][l], 2).reshape(128, 1))
        pp.add(f"kg{l}", np.tile(inp["k_norm_g"][l], 2).reshape(128, 1))
        pp.add(f"hcw{l}", inp["hy_conv_w"][l].reshape(3, 12, 128).transpose(2, 1, 0).reshape(128, 36))
        pp.add(f"hcb{l}", _chunked(inp["hy_conv_b"][l], 12))
        pp.add(f"psc{l}", _chunked(inp["pool_scale"][l], 4))
        fb = np.zeros((128, 4), np.float32)
        fb[:64, 0] = inp["filt_b0"][l]
        fb[:64, 1] = inp["filt_b1"][l][0]
        fb[:64, 2] = inp["filt_b1"][l][1]
        fb[:64, 3] = inp["filt_freq"][l]
        pp.add(f"fb{l}", fb)
    return pp


def _bc_layout(inp):
    a = np.concatenate([inp["sink"].reshape(-1), inp["hy_bias"].reshape(-1)]).astype(np.float32)
    return np.ascontiguousarray(np.broadcast_to(a[None], (128, a.size)))


class Ctx:
    pass


def _stub(*a, **k):
    return None


phase_filters = phase_att = phase_hy = phase_pool = phase_B = _stub


def build_program(pp_off, npp, nbc, stop_after=None, debug=False):
    nc = bass.Bass("TRN2", target_bir_lowering=False)
    P = Prog(nc)
    K = Ctx()
    K.nc, K.P, K.pp_off = nc, P, pp_off
    okind = "ExternalOutput" if debug else "Internal"

    def din(name, shape, dt=F32):
        return nc.dram_tensor(name, list(shape), dt, kind="ExternalInput").ap()

    def dscr(name, shape, dt, dbg=True):
        return nc.dram_tensor(name, list(shape), dt, kind=(okind if dbg else "Internal")).ap()

    K.xTin = din("xTin", [SPC, D, TT])
    K.pp = din("pp", [128, npp])
    K.bc = din("bc", [128, nbc])
    K.w_mod = din("w_mod", [DEPTH, D, 6 * D])
    K.w_in = din("w_in", [DEPTH, D, IN_W])
    K.filt_w0 = din("filt_w0", [DEPTH, 33, 64])
    K.filt_w1 = din("filt_w1", [DEPTH, 2, 64, 64])
    K.filt_w2 = din("filt_w2", [DEPTH, 64, 2048])
    K.pool_w = din("pool_w", [DEPTH, 4, 128, 128])
    K.w_att_o = din("w_att_o", [DEPTH, 1024, D])
    K.w_hy_o = din("w_hy_o", [DEPTH, 512, D])
    K.w_pool_o = din("w_pool_o", [DEPTH, 512, D])
    K.w_out = din("w_out", [DEPTH, D, D])
    K.mlp_w1 = din("mlp_w1", [DEPTH, D, D_FF])
    K.mlp_w2 = din("mlp_w2", [DEPTH, D_FF, D])
    K.F_l = din("F_l", [L, 2 * L], BF16)
    K.FT_l = din("FT_l", [2 * L, L], BF16)
    K.F_c = din("F_c", [LC, 2 * LC], BF16)
    K.FT_c = din("FT_c", [2 * LC, LC], BF16)
    K.zz_l = din("zz_l", [2, 33, L])
    K.dd_l = din("dd_l", [2, L, 512])
    K.zz_c = din("zz_c", [2, 33, LC])
    K.dd_c = din("dd_c", [2, LC, 512])
    K.pinv_l = din("pinv_l", [128, 4, L])
    K.pinv_c = din("pinv_c", [128, 4, LC])
    K.ropeC = din("ropeC", [128, L])
    K.ropeS = din("ropeS", [128, L])
    K.mats = din("mats", [128, 6, 128])
    K.out = nc.dram_tensor("outT", [SPC, D, L], F32, kind="ExternalOutput").ap()
    K.xT = [[K.xTin[s] for s in range(SPC)], [dscr(f"xT1_{s}", [D, TT], F32) for s in range(SPC)]]
    K.zq = [dscr(f"zq{s}", [1024, TT], BF16) for s in range(SPC)]
    K.zk = [dscr(f"zk{s}", [256, TT], BF16) for s in range(SPC)]
    K.zv = [dscr(f"zv{s}", [TT, 4 * 65], BF16) for s in range(SPC)]
    K.zhy = [dscr(f"zhy{s}", [1536, TT], BF16) for s in range(SPC)]
    K.zpl = [dscr(f"zpl{s}", [512, TT], BF16) for s in range(SPC)]
    K.ycat = [dscr(f"ycat{s}", [D, TT], BF16) for s in range(SPC)]
    K.Kf_l = dscr("Kf_l", [2, 2, L, 512], F32)
    K.Kf_c = dscr("Kf_c", [2, 2, LC, 512], F32)
    K.moddbg = dscr("moddbg", [128, DEPTH * 3 * 96], F32)
    WSH = {"w_in": (D, IN_W), "w_att_o": (1024, D), "w_hy_o": (512, D), "w_pool_o": (512, D), "w_out": (D, D),
           "mlp_w1": (D, D_FF), "mlp_w2": (D_FF, D)}
    K.wb = {n: [nc.dram_tensor(f"wb_{n}{l}", list(sh), BF16, kind="Internal").ap() for l in range(DEPTH)]
            for n, sh in WSH.items()}
    K.b_wb = {n: [P.buf(f"wb_{n}{l}") for l in range(DEPTH)] for n in WSH}
    K.WSH = WSH
    K.b_xT = [[P.buf(f"xT{a}{s}") for s in range(SPC)] for a in range(2)]
    K.b_z = [{n: P.buf(n + str(s)) for n in ("zq", "zk", "zv", "zhy", "zpl")} for s in range(SPC)]
    K.b_ycat = [{n: P.buf("y" + n + str(s)) for n in ("att", "hy", "pool")} for s in range(SPC)]
    K.b_Kf = {"l": P.buf("Kf_l"), "c": P.buf("Kf_c")}
    K.b_out = P.buf("out")
    K.b_dbg = P.buf("dbg")
    K.b_in = P.buf("inputs")

    with ExitStack() as es:
        K.ps = [es.enter_context(nc.psum_tensor(f"ps{i}", [128, 512], F32)) for i in range(8)]
        K.b_ps = P.bufs("ps", 8)
        K.psi = 0
        K.mats_f = es.enter_context(nc.sbuf_tensor("mats_f", [128, 6, 128], F32))
        K.mats_b = es.enter_context(nc.sbuf_tensor("mats_b", [128, 6, 128], BF16))
        K.ppt = es.enter_context(nc.sbuf_tensor("ppt", [128, npp], F32))
        K.bct = es.enter_context(nc.sbuf_tensor("bct", [128, nbc], F32))
        K.mod = es.enter_context(nc.sbuf_tensor("mod", [128, DEPTH * 3 * 96], F32))
        K.modA = es.enter_context(nc.sbuf_tensor("modA", [128, DEPTH * 3 * 2 * 16], F32))
        K.b_const = P.buf("const")
        K.b_mod = P.buf("mod")
        K.b_modA = P.buf("modA")
        K.cst = es.enter_context(nc.sbuf_tensor("cst", [128, 4], F32))
        P.op("dve", lambda e: e.memset(K.cst[:, 0:1], EPS), partial=[K.b_const])
        P.op("dve", lambda e: e.memset(K.cst[:, 1:2], -math.pi), partial=[K.b_const])
        P.op("dve", lambda e: e.memset(K.cst[:, 2:3], 0.0), partial=[K.b_const])
        P.dma("sp", K.mats_f[:], K.mats, K.b_const)
        P.dma("pool", K.mats_b[:], K.mats, K.b_const)
        P.dma("sp", K.ppt[:], K.pp, K.b_const)
        P.dma("sp", K.bct[:], K.bc, K.b_const)

        phases = [phase_precast, phase_mod, lambda K: phase_precast(K, 1)]
        for l in range(DEPTH):
            phases.append(lambda K, l=l: phase_filters(K, l))
            for s in range(SPC):
                phases.append(lambda K, l=l, s=s: phase_A(K, l, s))
                phases.append(lambda K, l=l, s=s: phase_att(K, l, s))
                phases.append(lambda K, l=l, s=s: phase_hy(K, l, s))
                phases.append(lambda K, l=l, s=s: phase_pool(K, l, s))
                phases.append(lambda K, l=l, s=s: phase_B(K, l, s))
        import os as _os
        only = _os.environ.get("PH_ONLY")
        only = [int(v) for v in only.split(",")] if only else None
        K.cut = int(_os.environ.get("KB_CUT", "0"))
        for i, ph in enumerate(phases):
            if stop_after is not None and i >= stop_after:
                break
            if only is not None and i not in only:
                continue
            P.push_scope()
            with ExitStack() as pes:
                K.es = pes
                ph(K)
            P.fence()
            P.pop_scope()
        P.emit()
    return nc, P


def ppc(K, name, i=0, n=1, rows=128):
    o, w = K.pp_off[name]
    return K.ppt[0:rows, o + i:o + i + n]


def nextps(K):
    i = K.psi
    K.psi = (K.psi + 1) % 8
    return K.ps[i], K.b_ps[i]


def sb(K, name, shape, dt):
    K.uid = getattr(K, "uid", 0) + 1
    return K.es.enter_context(K.nc.sbuf_tensor(f"{name}_u{K.uid}", list(shape), dt))


def phase_precast(K, part=0):
    P = K.P
    for l in range(DEPTH):
        for n in ("w_in", "w_att_o", "w_hy_o", "w_pool_o", "w_out", "mlp_w1", "mlp_w2"):
            if (part == 0) != (l == 0 and n == "w_in"):
                continue
            rows, cols = K.WSH[n]
            src = getattr(K, n)[l]
            RB = 128
            for r0 in range(0, rows, RB):
                first = (l == 0 and n == "w_in")
                o = P.dma("pool", K.wb[n][l][r0:r0 + RB, :], src[r0:r0 + RB, :], K.b_wb[n][l],
                          reads=[K.b_in] if first else [K.b_in, K.b_modA])
                o.nofence = True


def phase_transpose_in(K):
    P = K.P
    ident = K.mats_f[:, 0, :]
    NS = 2
    xin = [sb(K, f"t0_in{i}", [128, 4, D], F32) for i in range(NS)]
    b_xin = P.bufs("t0_in", NS)
    xo = [sb(K, f"t0_out{i}", [128, 16, 512], F32) for i in range(NS)]
    b_xo = P.bufs("t0_out", NS)
    it = 0
    for s in range(SPC):
        for (src, n_tok, tok0) in ((K.x[s], L, 0), (K.ctx[s], LC, L)):
            for t0 in range(0, n_tok, 512):
                T = min(512, n_tok - t0)
                nsub = T // 128
                i = it % NS
                it += 1
                P.dma("sp", xin[i][:, 0:nsub, :], src[t0:t0 + T, :].rearrange("(a p) d -> p a d", p=128),
                      b_xin[i], reads=[K.b_in], partial=False)
                for c in range(16):
                    ps, bps = nextps(K)
                    for a in range(nsub):
                        P.mm(lambda e, ps=ps, a=a, c=c, i=i: e.transpose(
                            ps[:, a * 128:(a + 1) * 128], xin[i][:, a, c * 128:(c + 1) * 128], ident),
                            [b_xin[i], K.b_const], bps, a == 0, a == nsub - 1)
                    eng = "dve" if c % 2 == 0 else "act"
                    if eng == "dve":
                        P.op("dve", lambda e, ps=ps, c=c, i=i, T=T: e.tensor_copy(xo[i][:, c, 0:T], ps[:, 0:T]),
                             reads=[bps], partial=[b_xo[i]])
                    else:
                        P.op("act", lambda e, ps=ps, c=c, i=i, T=T: e.copy(xo[i][:, c, 0:T], ps[:, 0:T]),
                             reads=[bps], partial=[b_xo[i]])
                P.dma("sp", K.xT[0][s].rearrange("(c p) t -> p c t", p=128)[:, :, tok0 + t0:tok0 + t0 + T],
                      xo[i][:, :, 0:T], K.b_xT[0][s], reads=[b_xo[i]])


def phase_mod(K):
    P = K.P
    nc = K.nc
    sc = sb(K, "pm_sc", [128, 48], F32)
    b_sc = P.buf("pm_sc")
    P.op("act", lambda e: e.activation(sc[:], ppc(K, "cvec", 0, 48), AF.Silu), reads=[K.b_const], writes=[b_sc])
    NS = 2
    wt = [sb(K, f"pm_w{i}", [128, 16, 512], F32) for i in range(NS)]
    b_wt = P.bufs("pm_w", NS)
    s3 = [sb(K, f"pm_s3{i}", [3, 512], F32) for i in range(2)]
    b_s3 = P.bufs("pm_s3", 2)
    ident3 = K.mats_f[0:3, 0, 0:3]
    it = 0
    for l in range(DEPTH):
        for nb in range(24):
            i = it % NS
            it += 1
            P.dma("sp", wt[i][:], K.w_mod[l].rearrange("(k p) n -> p k n", p=128)[:, :, nb * 512:(nb + 1) * 512],
                  b_wt[i], reads=[K.b_in], partial=False)
            ps, bps = nextps(K)
            for k in range(16):
                P.mm(lambda e, ps=ps, i=i, k=k: e.matmul(
                    ps[0:3, :], sc[:, k * 3:k * 3 + 3], wt[i][:, k, :], start=(k == 0), stop=(k == 15)),
                    [b_wt[i], b_sc], bps, k == 0, k == 15)
            P.op("act", lambda e, ps=ps, i=i: e.copy(s3[i][:], ps[0:3, :]), reads=[bps], writes=[b_s3[i]])
            ps2, bps2 = nextps(K)
            for j in range(4):
                P.mm(lambda e, ps2=ps2, i=i, j=j: e.transpose(
                    ps2[:, j * 4:j * 4 + 3], s3[i][0:3, j * 128:(j + 1) * 128], ident3),
                    [b_s3[i], K.b_const], bps2, j == 0, j == 3)
            for who in range(3):
                col = (l * 3 + who) * 96 + nb * 4
                P.op("dve", lambda e, ps2=ps2, who=who, col=col, l=l, nb=nb: e.tensor_tensor(
                    K.mod[:, col:col + 4], ps2[:, 0:16].rearrange("p (j w) -> p j w", w=4)[:, :, who],
                    ppc(K, f"bmod{l}", nb * 4, 4), ALU.add),
                    reads=[bps2, K.b_const], partial=[K.b_mod])
    for l in range(DEPTH):
        for who in range(3):
            base = (l * 3 + who) * 96
            for which, (gname, scoff) in enumerate(((f"n1g{l}", 16), (f"n2g{l}", 64))):
                o = ((l * 3 + who) * 2 + which) * 16
                P.op("dve", lambda e, base=base, scoff=scoff, gname=gname, o=o: e.scalar_tensor_tensor(
                    K.modA[:, o:o + 16], K.mod[:, base + scoff:base + scoff + 16], 1.0, ppc(K, gname, 0, 16),
                    ALU.add, ALU.mult), reads=[K.b_mod, K.b_const], partial=[K.b_modA])
    P.dma("sp", K.moddbg, K.mod[:], K.b_dbg, reads=[K.b_mod])


def modcol(K, l, who, part, c):
    col = (l * 3 + who) * 96 + part * 16 + c
    return K.mod[:, col:col + 1]


def modAcol(K, l, who, which, c):
    o = ((l * 3 + who) * 2 + which) * 16 + c
    return K.modA[:, o:o + 1]


def wring(K, n=3):
    K.wr = [sb(K, f"wr{i}", [128, 8192], BF16) for i in range(n)]
    K.b_wr = K.P.bufs("wr", n)
    K.wri = 0


def wload(K, src_ap, kc, ncol, dep):
    i = K.wri
    K.wri = (K.wri + 1) % len(K.wr)
    view = K.wr[i][:, 0:kc * ncol].rearrange("p (k n) -> p k n", k=kc)
    K.P.dma("sp", view, src_ap, K.b_wr[i], reads=[dep], partial=False)
    return view, K.b_wr[i]


def rsqrt_from_ps(K, out, b_out, ps, bps, scale, rows=128):
    P = K.P
    P.op("act", lambda e: e.activation(out, ps, AF.Ln, bias=K.cst[0:rows, 0:1], scale=scale),
         reads=[bps, K.b_const], writes=[b_out])
    P.op("act", lambda e: e.activation(out, out, AF.Exp, scale=-0.5), reads=[b_out], writes=[b_out])


def make_h(K, xt, b_xt, h, b_h, T, l, who, which, tmp):
    P = K.P
    ones_b = K.mats_b[:, 1, :]
    sq, b_sq, rs, b_rs, t32, b_t32 = tmp["sq"], tmp["b_sq"], tmp["rs"], tmp["b_rs"], tmp["t32"], tmp["b_t32"]
    P.op("act", lambda e: e.activation(sq[:, :, 0:T], xt[:, :, 0:T], AF.Square), reads=[b_xt], writes=[b_sq])
    ps, bps = nextps(K)
    for c in range(16):
        P.mm(lambda e, c=c: e.matmul(ps[:, 0:T], ones_b, sq[:, c, 0:T], start=(c == 0), stop=(c == 15)),
             [b_sq, K.b_const], bps, c == 0, c == 15)
    rsqrt_from_ps(K, rs[:, 0:T], b_rs, ps[:, 0:T], bps, 1.0 / D)
    sh_part = 0 if which == 0 else 3
    for c in range(16):
        j = c % 2
        P.op("dve", lambda e, c=c, j=j: e.scalar_tensor_tensor(
            t32[j][:, 0:T], xt[:, c, 0:T], modAcol(K, l, who, which, c), rs[:, 0:T], ALU.mult, ALU.mult),
            reads=[b_xt, b_rs, K.b_modA], writes=[b_t32[j]])
        P.op("act", lambda e, c=c, j=j: e.activation(
            h[:, c, 0:T], t32[j][:, 0:T], AF.Identity, bias=modcol(K, l, who, sh_part, c), scale=1.0),
            reads=[b_t32[j], K.b_mod], partial=[b_h])


def h_tmp(K, pfx):
    P = K.P
    return dict(sq=sb(K, pfx + "sq", [128, 16, 512], BF16), b_sq=P.buf(pfx + "sq"),
                rs=sb(K, pfx + "rs", [128, 512], F32), b_rs=P.buf(pfx + "rs"),
                t32=[sb(K, pfx + f"t32{j}", [128, 512], F32) for j in range(2)], b_t32=P.bufs(pfx + "t32", 2))


def tiles_of(l, with_ctx=True):
    t = [(t0, 512) for t0 in range(0, L, 512)]
    if with_ctx:
        t.append((L, LC))
    return t


def phase_A(K, l, s):
    P = K.P
    wring(K, 3)
    tmp = h_tmp(K, "pa_")
    xt = [sb(K, f"pa_x{i}", [128, 16, 512], F32) for i in range(2)]
    b_xt = P.bufs("pa_x", 2)
    h = [sb(K, f"pa_h{i}", [128, 16, 512], BF16) for i in range(2)]
    b_h = P.bufs("pa_h", 2)
    st = [sb(K, f"pa_st{i}", [128, 4, 512], BF16) for i in range(2)]
    b_st = P.bufs("pa_st", 2)
    vst = [sb(K, f"pa_vst{i}", [128, 4, 65], BF16) for i in range(2)]
    b_vst = P.bufs("pa_vst", 2)
    for i in range(2):
        P.op("dve", lambda e, i=i: e.memset(vst[i][:], 1.0), writes=[b_vst[i]])
    xsrc = K.xT[l % 2][s].rearrange("(c p) t -> p c t", p=128)
    win = K.wb["w_in"][l].rearrange("(k p) n -> p k n", p=128)
    b_win = K.b_wb["w_in"][l]
    sti = 0
    vsi = 0
    tl = tiles_of(l)

    def pload(ti):
        t0, T = tl[ti]
        i = ti % 2
        P.dma("act", xt[i][:, :, 0:T], xsrc[:, :, t0:t0 + T], b_xt[i], reads=[K.b_xT[l % 2][s]], partial=False)

    def prep(ti):
        t0, T = tl[ti]
        who = 2 if t0 >= L else s
        i = ti % 2
        make_h(K, xt[i], b_xt[i], h[i], b_h[i], T, l, who, 0, tmp)

    pload(0)
    prep(0)
    if len(tl) > 1:
        pload(1)
    for ti, (t0, T) in enumerate(tl):
        is_ctx = t0 >= L
        who = 2 if is_ctx else s
        i = ti % 2
        nblk = 0
        prepped = False
        blocks = [(0, "zq", 0), (512, "zq", 512), (1024, "kv", 0), (1536, "zhy", 0), (2048, "zhy", 512),
                  (2560, "zhy", 1024), (3072, "zpl", 0)]
        for (col0, dst, r0) in blocks:
            if is_ctx and l == DEPTH - 1 and dst != "kv":
                continue
            if nblk == 4 and ti + 1 < len(tl):
                prep(ti + 1)
                prepped = True
            nblk += 1
            wv, b_w = wload(K, win[:, :, col0:col0 + 512], 16, 512, b_win)
            nj = 2 if dst == "kv" else 4
            si = sti % 2
            sti += 1
            for j in range(nj):
                ps, bps = nextps(K)
                for k in range(16):
                    P.mm(lambda e, ps=ps, wv=wv, j=j, k=k, i=i, T=T: e.matmul(
                        ps[:, 0:T], wv[:, k, j * 128:(j + 1) * 128], h[i][:, k, 0:T], start=(k == 0), stop=(k == 15)),
                        [b_w, b_h[i]], bps, k == 0, k == 15)
                P.op("act", lambda e, ps=ps, si=si, j=j, T=T: e.copy(st[si][:, j, 0:T], ps[:, 0:T]),
                     reads=[bps], partial=[b_st[si]])
            dname = "zk" if dst == "kv" else dst
            dten = getattr(K, dname)[s].rearrange("(c p) t -> p c t", p=128)
            c0 = r0 // 128
            P.dma("act", dten[:, c0:c0 + nj, t0:t0 + T], st[si][:, 0:nj, 0:T], K.b_z[s][dname], reads=[b_st[si]])
            if dst == "kv":
                for a in range(T // 128):
                    ps, bps = nextps(K)
                    for k in range(16):
                        P.mm(lambda e, ps=ps, wv=wv, a=a, k=k, i=i: e.matmul(
                            ps[:, 0:256], h[i][:, k, a * 128:(a + 1) * 128], wv[:, k, 256:512],
                            start=(k == 0), stop=(k == 15)), [b_w, b_h[i]], bps, k == 0, k == 15)
                    vi = vsi % 2
                    vsi += 1
                    P.op("dve", lambda e, ps=ps, vi=vi: e.tensor_copy(
                        vst[vi][:, :, 0:64], ps[:, 0:256].rearrange("p (g d) -> p g d", g=4)),
                        reads=[bps], partial=[b_vst[vi]])
                    P.dma("act", K.zv[s][t0 + a * 128:t0 + (a + 1) * 128, :].rearrange("p (g d) -> p g d", g=4),
                          vst[vi][:], K.b_z[s]["zv"], reads=[b_vst[vi]])
        if not prepped and ti + 1 < len(tl):
            prep(ti + 1)
        if ti + 2 < len(tl):
            pload(ti + 2)


def phase_filters(K, l):
    with ExitStack() as es2:
        K.es = es2
        filters_for(K, l, L, K.zz_l, K.dd_l, K.F_l, K.Kf_l, K.b_Kf["l"], "fl")
    K.P.fence()
    if l == 0:
        with ExitStack() as es2:
            K.es = es2
            filters_for(K, l, LC, K.zz_c, K.dd_c, K.F_c, K.Kf_c, K.b_Kf["c"], "fc")


def filters_for(K, l, n_tok, zz, dd, Fm, Kdst, b_Kdst, pfx):
    P = K.P
    nT = n_tok // 128
    TW = min(512, n_tok)
    w0 = sb(K, pfx + "w0", [33, 64], F32)
    w1 = sb(K, pfx + "w1", [64, 2, 64], F32)
    w2 = sb(K, pfx + "w2", [64, 2048], F32)
    b_w = P.buf(pfx + "w")
    P.dma("sp", w0[:], K.filt_w0[l], b_w, reads=[K.b_in])
    P.dma("sp", w1[:], K.filt_w1[l].rearrange("i k n -> k i n"), b_w, reads=[K.b_in])
    P.dma("sp", w2[:], K.filt_w2[l], b_w, reads=[K.b_in])
    zt = sb(K, pfx + "zt", [33, n_tok], F32)
    b_zt = P.buf(pfx + "zt")
    hb = sb(K, pfx + "hb", [128, 2, nT, 512], BF16)
    b_hb = P.buf(pfx + "hb")
    AB = sb(K, pfx + "AB", [128, 2, 2, nT, 512], BF16)
    b_AB = P.buf(pfx + "AB")
    hid = [sb(K, pfx + f"hid{i}", [64, TW], F32) for i in range(2)]
    b_hid = P.bufs(pfx + "hid", 2)
    arg = sb(K, pfx + "arg", [64, TW], F32)
    b_arg = P.buf(pfx + "arg")
    argi = sb(K, pfx + "argi", [64, TW], mybir.dt.int32)
    b_argi = P.buf(pfx + "argi")
    arg2 = sb(K, pfx + "arg2", [64, TW], F32)
    b_arg2 = P.buf(pfx + "arg2")
    dec = [sb(K, pfx + f"dec{i}", [128, 512], F32) for i in range(2)]
    b_dec = P.bufs(pfx + "dec", 2)
    hf = [sb(K, pfx + f"hf{i}", [128, 512], F32) for i in range(2)]
    b_hf = P.bufs(pfx + "hf", 2)
    fbn = f"fb{l}"
    di = 0
    for direction in (1, 0):
        P.dma("sp", zt[:], zz[direction], b_zt, reads=[K.b_in], partial=False)
        for t0 in range(0, n_tok, TW):
            cur_in, b_cur_in, kin = zt[:, t0:t0 + TW], b_zt, 33
            for layer in range(3):
                ps, bps = nextps(K)
                if layer == 0:
                    lhsT = w0[:]
                else:
                    lhsT = w1[:, layer - 1, :]
                P.mm(lambda e, ps=ps, lhsT=lhsT, cur_in=cur_in: e.matmul(ps[0:64, 0:TW], lhsT, cur_in,
                                                                          start=True, stop=True),
                     [b_w, b_cur_in], bps, True, True)
                P.op("dve", lambda e, ps=ps, layer=layer: e.tensor_scalar(
                    arg[:], ps[0:64, 0:TW], ppc(K, fbn, layer, 1, 64), ppc(K, fbn, 3, 1, 64), ALU.add, ALU.mult),
                    reads=[bps, K.b_const], writes=[b_arg])
                P.op("dve", lambda e: e.tensor_single_scalar(argi[:], arg[:], 1.0 / (2.0 * math.pi), ALU.mult),
                     reads=[b_arg], writes=[b_argi])
                P.op("dve", lambda e: e.scalar_tensor_tensor(arg2[:], argi[:], -2.0 * math.pi, arg[:], ALU.mult, ALU.add),
                     reads=[b_arg, b_argi], writes=[b_arg2])
                ho = layer % 2
                P.op("act", lambda e, ho=ho: e.activation(hid[ho][:], arg2[:], AF.Sin, bias=K.cst[0:64, 2:3],
                                                          scale=1.0 - 2e-6),
                     reads=[b_arg2, K.b_const], writes=[b_hid[ho]])
                cur_in, b_cur_in = hid[ho][:], b_hid[ho]
            h3 = cur_in
            for a in range(TW // 128):
                pc = t0 // 128 + a
                d_i = di % 2
                di += 1
                P.dma("sp", dec[d_i][:], dd[direction, pc * 128:(pc + 1) * 128, :], b_dec[d_i], reads=[K.b_in],
                      partial=False)
                for o in range(2):
                    ps, bps = nextps(K)
                    col = o * 1024 + direction * 512
                    P.mm(lambda e, ps=ps, h3=h3, a=a, col=col: e.matmul(
                        ps[:, :], h3[:, a * 128:(a + 1) * 128], w2[:, col:col + 512], start=True, stop=True),
                        [b_w, b_cur_in], bps, True, True)
                    if direction == 1:
                        P.op("dve", lambda e, ps=ps, o=o, pc=pc, d_i=d_i: e.tensor_tensor(
                            hb[:, o, pc, :], ps[:, :], dec[d_i][:], ALU.mult), reads=[bps, b_dec[d_i]], partial=[b_hb])
                    else:
                        P.op("dve", lambda e, ps=ps, o=o, d_i=d_i: e.tensor_tensor(
                            hf[o][:], ps[:, :], dec[d_i][:], ALU.mult), reads=[bps, b_dec[d_i]], writes=[b_hf[o]])
                        P.op("dve", lambda e, o=o, pc=pc: e.tensor_tensor(
                            AB[:, o, 0, pc, :], hf[o][:], hb[:, o, pc, :], ALU.add), reads=[b_hf[o], b_hb],
                            partial=[b_AB])
                        P.op("dve", lambda e, o=o, pc=pc: e.tensor_tensor(
                            AB[:, o, 1, pc, :], hf[o][:], hb[:, o, pc, :], ALU.subtract), reads=[b_hf[o], b_hb],
                            partial=[b_AB])
    M = n_tok
    FW = min(512, M)
    fw = [sb(K, pfx + f"F{i}", [128, nT, FW], BF16) for i in range(3)]
    b_fw = P.bufs(pfx + "F", 3)
    kst = [sb(K, pfx + f"kst{i}", [128, 512], F32) for i in range(2)]
    b_kst = P.bufs(pfx + "kst", 2)
    Fv = Fm.rearrange("(k p) f -> p k f", p=128)
    fi = 0
    ki = 0
    for cs in range(2):
        for f0 in range(0, M, FW):
            i = fi % 3
            fi += 1
            P.dma("sp", fw[i][:], Fv[:, :, cs * M + f0:cs * M + f0 + FW], b_fw[i], reads=[K.b_in], partial=False)
            for fc in range(FW // 128):
                for o in range(2):
                    ps, bps = nextps(K)
                    for k in range(nT):
                        P.mm(lambda e, ps=ps, i=i, k=k, fc=fc, o=o, cs=cs: e.matmul(
                            ps[:, :], fw[i][:, k, fc * 128:(fc + 1) * 128], AB[:, o, cs, k, :],
                            start=(k == 0), stop=(k == nT - 1)), [b_fw[i], b_AB], bps, k == 0, k == nT - 1)
                    kk = ki % 2
                    ki += 1
                    P.op("act", lambda e, ps=ps, kk=kk: e.copy(kst[kk][:], ps[:, :]), reads=[bps], writes=[b_kst[kk]])
                    r0 = f0 + fc * 128
                    P.dma("sp", Kdst[o, cs, r0:r0 + 128, :], kst[kk][:], b_Kdst, reads=[b_kst[kk]])


def phase_hy(K, l, s):
    with ExitStack() as es2:
        K.es = es2
        hyena_for(K, l, s, 0, L, K.F_l, K.FT_l, K.Kf_l, K.b_Kf["l"], "hl")
    if l == 0:
        K.P.fence()
        with ExitStack() as es2:
            K.es = es2
            hyena_for(K, l, s, L, LC, K.F_c, K.FT_c, K.Kf_c, K.b_Kf["c"], "hc")


def hyena_for(K, l, s, tok0, n_tok, Fm, FTm, Kf, b_Kf, pfx):
    P = K.P
    nT = n_tok // 128
    M = n_tok
    nF = M // 128
    ident_b = K.mats_b[:, 0, :]
    tokU = sb(K, pfx + "tokU", [128, nT, 1536], BF16)
    b_tokU = P.buf(pfx + "tokU")
    outer_es = K.es
    pro_es = ExitStack()
    K.es = pro_es
    ub = [sb(K, pfx + f"ub{i}", [128, n_tok + 2], BF16) for i in range(2)]
    b_ub = P.bufs(pfx + "ub", 2)
    for i in range(2):
        P.op("pool", lambda e, i=i: e.memset(ub[i][:, 0:1], 0.0), partial=[b_ub[i]])
        P.op("pool", lambda e, i=i: e.memset(ub[i][:, n_tok + 1:n_tok + 2], 0.0), partial=[b_ub[i]])
    c1 = [sb(K, pfx + f"c1{i}", [128, n_tok], F32) for i in range(2)]
    b_c1 = P.bufs(pfx + "c1", 2)
    uc = [sb(K, pfx + f"uc{i}", [128, n_tok], BF16) for i in range(2)]
    b_uc = P.bufs(pfx + "uc", 2)
    zsrc = K.zhy[s].rearrange("(c p) t -> p c t", p=128)
    hcw, hcb = f"hcw{l}", f"hcb{l}"
    G = min(8, nT)
    for c in range(12):
        i = c % 2
        P.dma("sp", ub[i][:, 1:n_tok + 1], zsrc[:, c, tok0:tok0 + n_tok], b_ub[i], reads=[K.b_z[s]["zhy"]])
        P.op("dve", lambda e, i=i, c=c: e.tensor_scalar(
            c1[0][:], ub[i][:, 1:n_tok + 1], ppc(K, hcw, c * 3 + 1), ppc(K, hcb, c), ALU.mult, ALU.add),
            reads=[b_ub[i], K.b_const], writes=[b_c1[0]])
        P.op("dve", lambda e, i=i, c=c: e.scalar_tensor_tensor(
            c1[1][:], ub[i][:, 0:n_tok], ppc(K, hcw, c * 3 + 0), c1[0][:], ALU.mult, ALU.add),
            reads=[b_ub[i], b_c1[0], K.b_const], writes=[b_c1[1]])
        P.op("dve", lambda e, i=i, c=c: e.scalar_tensor_tensor(
            uc[i][:], ub[i][:, 2:n_tok + 2], ppc(K, hcw, c * 3 + 2), c1[1][:], ALU.mult, ALU.add),
            reads=[b_ub[i], b_c1[1], K.b_const], writes=[b_uc[i]])
        for g0 in range(0, nT, G):
            ps, bps = nextps(K)
            psb = ps[:].bitcast(BF16)
            for a in range(G):
                P.mm(lambda e, psb=psb, a=a, i=i, g0=g0: e.transpose(
                    psb[:, a * 128:(a + 1) * 128], uc[i][:, (g0 + a) * 128:(g0 + a + 1) * 128], ident_b),
                    [b_uc[i], K.b_const], bps, a == 0, a == G - 1)
            P.op("act", lambda e, psb=psb, g0=g0, c=c: e.copy(
                tokU[:, g0:g0 + G, c * 128:(c + 1) * 128], psb[:, 0:G * 128].rearrange("p (a n) -> p a n", a=G)),
                reads=[bps], partial=[b_tokU])
    pro_es.close()
    K.es = outer_es
    P.fence()
    FW = min(256, M)
    fw = [sb(K, pfx + f"F{i}", [128, nT, FW], BF16) for i in range(4)]
    b_fw = P.bufs(pfx + "F", 4)
    TWI = 128
    ftw = [sb(K, pfx + f"FT{i}", [128, 2 * nF, TWI], BF16) for i in range(2)]
    b_ftw = P.bufs(pfx + "FT", 2)
    kt = [sb(K, pfx + f"kt{i}", [128, 2, 512], F32) for i in range(2)]
    b_kt = P.bufs(pfx + "kt", 2)
    tmp = [sb(K, pfx + f"tmp{i}", [128, 512], F32) for i in range(4)]
    b_tmp = P.bufs(pfx + "tmp", 4)
    Pf = sb(K, pfx + "Pf", [128, 2 * nF, 512], BF16)
    b_Pf = P.buf(pfx + "Pf")
    z1 = sb(K, pfx + "z1", [128, nT, 512], BF16)
    b_z1 = P.buf(pfx + "z1")
    z2 = sb(K, pfx + "z2", [128, nT, 512], BF16)
    b_z2 = P.buf(pfx + "z2")
    Fv = Fm.rearrange("(k p) f -> p k f", p=128)
    FTv = FTm.rearrange("(k p) t -> p k t", p=128)
    nbc_sink = 2 * 16
    fi = 0
    ki = 0
    fti = 0
    for o in range(2):
        zin, b_zin = (tokU[:, :, 0:512], b_tokU) if o == 0 else (z1[:], b_z1)
        gate = tokU[:, :, 512 * (o + 1):512 * (o + 2)]
        zout, b_zout = (z1, b_z1) if o == 0 else (z2, b_z2)
        dcol = nbc_sink + (l * 2 + o) * 512
        for f0 in range(0, M, FW):
            ia = fi % 4
            ib = (fi + 1) % 4
            fi += 2
            P.dma("sp", fw[ia][:], Fv[:, :, f0:f0 + FW], b_fw[ia], reads=[K.b_in], partial=False)
            P.dma("sp", fw[ib][:], Fv[:, :, M + f0:M + f0 + FW], b_fw[ib], reads=[K.b_in], partial=False)
            for fc in range(FW // 128):
                fch = (f0 // 128) + fc
                kk = ki % 2
                ki += 1
                P.dma("sp", kt[kk][:], Kf[o, :, fch * 128:(fch + 1) * 128, :].rearrange("c p n -> p c n"),
                      b_kt[kk], reads=[b_Kf], partial=False)
                psc, bpsc = nextps(K)
                pss, bpss = nextps(K)
                for (pp_, bpp_, iw) in ((psc, bpsc, ia), (pss, bpss, ib)):
                    for k in range(nT):
                        P.mm(lambda e, pp_=pp_, iw=iw, k=k, fc=fc, zin=zin: e.matmul(
                            pp_[:, :], fw[iw][:, k, fc * 128:(fc + 1) * 128], zin[:, k, :],
                            start=(k == 0), stop=(k == nT - 1)), [b_fw[iw], b_zin], bpp_, k == 0, k == nT - 1)
                P.op("dve", lambda e, psc=psc, kk=kk: e.tensor_tensor(tmp[0][:], psc[:, :], kt[kk][:, 0, :], ALU.mult),
                     reads=[bpsc, b_kt[kk]], writes=[b_tmp[0]])
                P.op("dve", lambda e, pss=pss, kk=kk: e.tensor_tensor(tmp[1][:], pss[:, :], kt[kk][:, 1, :], ALU.mult),
                     reads=[bpss, b_kt[kk]], writes=[b_tmp[1]])
                P.op("pool", lambda e, fch=fch: e.tensor_tensor(Pf[:, fch, :], tmp[0][:], tmp[1][:], ALU.subtract),
                     reads=[b_tmp[0], b_tmp[1]], partial=[b_Pf])
                P.op("dve", lambda e, psc=psc, kk=kk: e.tensor_tensor(tmp[2][:], psc[:, :], kt[kk][:, 1, :], ALU.mult),
                     reads=[bpsc, b_kt[kk]], writes=[b_tmp[2]])
                P.op("dve", lambda e, pss=pss, kk=kk: e.tensor_tensor(tmp[3][:], pss[:, :], kt[kk][:, 0, :], ALU.mult),
                     reads=[bpss, b_kt[kk]], writes=[b_tmp[3]])
                P.op("pool", lambda e, fch=fch: e.tensor_tensor(Pf[:, nF + fch, :], tmp[2][:], tmp[3][:], ALU.add),
                     reads=[b_tmp[2], b_tmp[3]], partial=[b_Pf])
        for t0 in range(0, n_tok, TWI):
            it = fti % 2
            fti += 1
            P.dma("sp", ftw[it][:], FTv[:, :, t0:t0 + TWI], b_ftw[it], reads=[K.b_in], partial=False)
            for a in range(TWI // 128):
                tch = t0 // 128 + a
                ps, bps = nextps(K)
                for f in range(2 * nF):
                    P.mm(lambda e, ps=ps, it=it, f=f, a=a: e.matmul(
                        ps[:, :], ftw[it][:, f, a * 128:(a + 1) * 128], Pf[:, f, :],
                        start=(f == 0), stop=(f == 2 * nF - 1)), [b_ftw[it], b_Pf], bps, f == 0, f == 2 * nF - 1)
                P.op("pool", lambda e, tch=tch, zin=zin, dcol=dcol: e.tensor_tensor(
                    tmp[0][:], zin[:, tch, :], K.bct[:, dcol:dcol + 512], ALU.mult),
                    reads=[b_zin, K.b_const], writes=[b_tmp[0]])
                P.op("dve", lambda e, ps=ps: e.scalar_tensor_tensor(
                    tmp[1][:], ps[:, :], 1.0 / n_tok, tmp[0][:], ALU.mult, ALU.add),
                    reads=[bps, b_tmp[0]], writes=[b_tmp[1]])
                P.op("pool", lambda e, tch=tch, gate=gate, zout=zout: e.tensor_tensor(
                    zout[:, tch, :], tmp[1][:], gate[:, tch, :], ALU.mult),
                    reads=[b_tmp[1], b_tokU], partial=[b_zout])
    TG = min(4, nT)
    yst = [sb(K, pfx + f"yst{i}", [128, 4, TG * 128], BF16) for i in range(2)]
    b_yst = P.bufs(pfx + "yst", 2)
    ydst = K.ycat[s].rearrange("(c p) t -> p c t", p=128)
    yi = 0
    for g0 in range(0, nT, TG):
        i = yi % 2
        yi += 1
        for c4 in range(4):
            ps, bps = nextps(K)
            psb = ps[:].bitcast(BF16)
            for a in range(TG):
                P.mm(lambda e, psb=psb, a=a, g0=g0, c4=c4: e.transpose(
                    psb[:, a * 128:(a + 1) * 128], z2[:, g0 + a, c4 * 128:(c4 + 1) * 128], ident_b),
                    [b_z2, K.b_const], bps, a == 0, a == TG - 1)
            P.op("act", lambda e, psb=psb, i=i, c4=c4: e.copy(yst[i][:, c4, :], psb[:, 0:TG * 128]),
                 reads=[bps], partial=[b_yst[i]])
        P.dma("sp", ydst[:, 8:12, tok0 + g0 * 128:tok0 + (g0 + TG) * 128], yst[i][:], K.b_ycat[s]["hy"],
              reads=[b_yst[i]])


def phase_pool(K, l, s):
    with ExitStack() as es2:
        K.es = es2
        pool_for(K, l, s, 0, L, K.pinv_l, "pl")
    if l == 0:
        K.P.fence()
        with ExitStack() as es2:
            K.es = es2
            pool_for(K, l, s, L, LC, K.pinv_c, "pc")


def pool_for(K, l, s, tok0, n_tok, pinv, pfx):
    P = K.P
    PADW = n_tok + 32
    EW = n_tok + 16
    u = sb(K, pfx + "u", [128, PADW], BF16)
    b_u = P.buf(pfx + "u")
    inv = sb(K, pfx + "inv", [128, n_tok], F32)
    b_inv = P.buf(pfx + "inv")
    wa = sb(K, pfx + "wa", [128, EW], F32)
    wb = sb(K, pfx + "wb", [128, EW], F32)
    b_wa, b_wb = P.buf(pfx + "wa"), P.buf(pfx + "wb")
    diff = sb(K, pfx + "diff", [128, n_tok], BF16)
    b_diff = P.buf(pfx + "diff")
    pw = sb(K, pfx + "pw", [128, 4, 128], BF16)
    b_pw = P.buf(pfx + "pw")
    yst = [sb(K, pfx + f"yst{i}", [128, 512], BF16) for i in range(2)]
    b_yst = P.bufs(pfx + "yst", 2)
    P.dma("pool", pw[:], K.pool_w[l].rearrange("g c d -> c g d"), b_pw, reads=[K.b_in])
    P.op("dve", lambda e: e.memset(u[:, 0:16], 0.0), partial=[b_u])
    P.op("dve", lambda e: e.memset(u[:, 16 + n_tok:PADW], 0.0), partial=[b_u])
    zsrc = K.zpl[s].rearrange("(c p) t -> p c t", p=128)
    ydst = K.ycat[s].rearrange("(c p) t -> p c t", p=128)
    yi = 0
    for g in range(4):
        P.dma("sp", u[:, 16:16 + n_tok], zsrc[:, g, tok0:tok0 + n_tok], b_u, reads=[K.b_z[s]["zpl"]])
        P.dma("sp", inv[:], pinv[:, g, :], b_inv, reads=[K.b_in], partial=False)
        P.op("dve", lambda e: e.tensor_tensor(wa[:], u[:, 7:7 + EW], u[:, 8:8 + EW], ALU.add),
             reads=[b_u], writes=[b_wa])
        cur, b_cur, oth, b_oth = wa, b_wa, wb, b_wb
        step = 1
        for lvl in range(g):
            lo = (1, 3, 7)[lvl]
            P.op("dve", lambda e, cur=cur, oth=oth, lo=lo, step=step: e.tensor_tensor(
                oth[:, lo:EW - lo], cur[:, lo - step:EW - lo - step], cur[:, lo + step:EW - lo + step], ALU.add),
                reads=[b_cur], writes=[b_oth])
            cur, b_cur, oth, b_oth = oth, b_oth, cur, b_cur
            step *= 2
        P.op("dve", lambda e, cur=cur, oth=oth: e.tensor_tensor(oth[:, 0:n_tok], cur[:, 8:8 + n_tok], inv[:], ALU.mult),
             reads=[b_cur, b_inv], writes=[b_oth])
        P.op("dve", lambda e, oth=oth: e.tensor_tensor(diff[:], oth[:, 0:n_tok], u[:, 16:16 + n_tok], ALU.subtract),
             reads=[b_oth, b_u], writes=[b_diff])
        for t0 in range(0, n_tok, 512):
            T = min(512, n_tok - t0)
            ps, bps = nextps(K)
            P.mm(lambda e, ps=ps, g=g, t0=t0, T=T: e.matmul(ps[:, 0:T], pw[:, g, :], diff[:, t0:t0 + T],
                                                           start=True, stop=True), [b_pw, b_diff], bps, True, True)
            i = yi % 2
            yi += 1
            P.op("act", lambda e, ps=ps, i=i, g=g, T=T: e.activation(
                yst[i][:, 0:T], ps[:, 0:T], AF.Copy, scale=ppc(K, f"psc{l}", g)), reads=[bps, K.b_const],
                writes=[b_yst[i]])
            P.dma("sp", ydst[:, 12 + g, tok0 + t0:tok0 + t0 + T], yst[i][:, 0:T], K.b_ycat[s]["pool"],
                  reads=[b_yst[i]])


def phase_att(K, l, s):
    P = K.P
    with_ctx_q = (l == 0)
    ident_b = K.mats_b[:, 0, :]
    blk_b = K.mats_b[:, 2, :]
    perm_b = K.mats_b[:, 3, :]
    maskA = K.mats_b[:, 4, :]
    maskB = K.mats_b[:, 5, :]
    NTI = 5
    qT = sb(K, "at_q", [128, 8, TT], BF16)
    b_q = [[P.buf(f"at_q{c}_{t}") for t in range(NTI)] for c in range(8)]
    kd = [sb(K, f"at_k{g}", [128, TT], BF16) for g in range(4)]
    b_k = [[P.buf(f"at_k{g}_{t}") for t in range(NTI)] for g in range(4)]
    Va = sb(K, "at_v", [128, 18, 260], BF16)
    b_v = P.buf("at_v")
    rC = sb(K, "at_rC", [128, L], F32)
    rS = sb(K, "at_rS", [128, L], F32)
    b_rope = P.buf("at_rope")
    esink = sb(K, "at_es", [128, 16], F32)
    b_es = P.buf("at_es")
    P.dma("sp", rC[:], K.ropeC, b_rope, reads=[K.b_in])
    P.dma("sp", rS[:], K.ropeS, b_rope, reads=[K.b_in])
    P.op("act", lambda e: e.activation(esink[:], K.bct[:, l * 16:(l + 1) * 16], AF.Exp), reads=[K.b_const],
         writes=[b_es])
    P.dma("sp", Va[:], K.zv[s].rearrange("(c p) n -> p c n", p=128), b_v, reads=[K.b_z[s]["zv"]])
    zq = K.zq[s].rearrange("(c p) t -> p c t", p=128)
    tiles = [(t0, 512) for t0 in range(0, L, 512)] + [(L, LC)]
    for ti, (t0, T) in enumerate(tiles):
        for c in range(8):
            if ti == 4 and not with_ctx_q:
                continue
            P.dma("sp", qT[:, c, t0:t0 + T], zq[:, c, t0:t0 + T], b_q[c][ti], reads=[K.b_z[s]["zq"]])
        for g in range(4):
            for half in range(2):
                P.dma("sp", kd[g][half * 64:(half + 1) * 64, t0:t0 + T], K.zk[s][g * 64:(g + 1) * 64, t0:t0 + T],
                      b_k[g][ti], reads=[K.b_z[s]["zk"]])
    sq = [sb(K, f"at_sq{i}", [128, 512], BF16) for i in range(2)]
    b_sq = P.bufs("at_sq", 2)
    rs = [sb(K, f"at_rs{i}", [128, 512], F32) for i in range(2)]
    b_rs = P.bufs("at_rs", 2)
    qn = [sb(K, f"at_qn{i}", [128, 512], BF16) for i in range(2)]
    b_qn = P.bufs("at_qn", 2)
    t1 = [sb(K, f"at_t1{i}", [128, 512], F32) for i in range(2)]
    b_t1 = P.bufs("at_t1", 2)
    t2 = [sb(K, f"at_t2{i}", [128, 512], F32) for i in range(2)]
    b_t2 = P.bufs("at_t2", 2)
    it = 0
    items = []
    for ti, (t0, T) in enumerate(tiles):
        for c in range(8):
            if ti == 4 and not with_ctx_q:
                continue
            items.append((qT[:, c, t0:t0 + T], b_q[c][ti], f"qg{l}", ti, t0, T))
        for g in range(4):
            items.append((kd[g][:, t0:t0 + T], b_k[g][ti], f"kg{l}", ti, t0, T))
    for (xa, b_x, gname, ti, t0, T) in items:
        i = it % 2
        it += 1
        P.op("act", lambda e, xa=xa, i=i, T=T: e.activation(sq[i][:, 0:T], xa, AF.Square), reads=[b_x],
             writes=[b_sq[i]])
        ps, bps = nextps(K)
        P.mm(lambda e, ps=ps, i=i, T=T: e.matmul(ps[:, 0:T], blk_b, sq[i][:, 0:T], start=True, stop=True),
             [b_sq[i], K.b_const], bps, True, True)
        rsqrt_from_ps(K, rs[i][:, 0:T], b_rs[i], ps[:, 0:T], bps, 1.0 / 64)
        if ti == 4:
            P.op("dve", lambda e, xa=xa, i=i, T=T, gname=gname: e.scalar_tensor_tensor(
                xa, xa, ppc(K, gname), rs[i][:, 0:T], ALU.mult, ALU.mult), reads=[b_x, b_rs[i], K.b_const],
                writes=[b_x])
            continue
        P.op("dve", lambda e, xa=xa, i=i, T=T, gname=gname: e.scalar_tensor_tensor(
            qn[i][:, 0:T], xa, ppc(K, gname), rs[i][:, 0:T], ALU.mult, ALU.mult), reads=[b_x, b_rs[i], K.b_const],
            writes=[b_qn[i]])
        ps2, bps2 = nextps(K)
        P.mm(lambda e, ps2=ps2, i=i, T=T: e.matmul(ps2[:, 0:T], perm_b, qn[i][:, 0:T], start=True, stop=True),
             [b_qn[i], K.b_const], bps2, True, True)
        P.op("pool", lambda e, i=i, t0=t0, T=T: e.tensor_tensor(t1[i][:, 0:T], qn[i][:, 0:T], rC[:, t0:t0 + T], ALU.mult),
             reads=[b_qn[i], b_rope], writes=[b_t1[i]])
        P.op("dve", lambda e, ps2=ps2, i=i, t0=t0, T=T: e.tensor_tensor(t2[i][:, 0:T], ps2[:, 0:T], rS[:, t0:t0 + T],
                                                                         ALU.mult),
             reads=[bps2, b_rope], writes=[b_t2[i]])
        P.op("pool", lambda e, xa=xa, i=i, T=T: e.tensor_tensor(xa, t1[i][:, 0:T], t2[i][:, 0:T], ALU.add),
             reads=[b_t1[i], b_t2[i]], writes=[b_x])
    Pt = [sb(K, f"at_P{i}", [128, 5, 128], BF16) for i in range(2)]
    b_P = P.bufs("at_P", 2)
    yt = [sb(K, f"at_y{i}", [128, 16, 64], BF16) for i in range(2)]
    b_y = P.bufs("at_y", 2)
    dsum = [sb(K, f"at_ds{i}", [128, 16], F32) for i in range(2)]
    b_ds = P.bufs("at_ds", 2)
    yst = [sb(K, f"at_yst{i}", [128, 8, 512], BF16) for i in range(2)]
    b_yst = P.bufs("at_yst", 2)
    ydst = K.ycat[s].rearrange("(c p) t -> p c t", p=128)
    blocks = list(range(16)) + ([16, 17] if with_ctx_q else [])
    pi = 0
    sli = 0
    sci = 0
    ysi = 0
    grp_start = 0
    for bi_, n in enumerate(blocks):
        is_cq = n >= 16
        tq = n // 4
        if is_cq:
            loc = []
        else:
            loc = [j for j in (n - 1, n, n + 1) if 0 <= j < 16]
        chunks = [(j, idx) for idx, j in enumerate(loc)] + [(16, 3), (17, 4)]
        yi = bi_ % 2
        def emit_S(h, n=n, loc=loc, tq=tq, bi_=bi_):
            g = h // 4
            base = (h % 2) * 64
            c = h // 2
            qa = qT[base:base + 64, c, n * 128:(n + 1) * 128]
            p_i = (bi_ * 16 + h) % 2
            if loc:
                bl = (bi_ * 16 + h) % 2
                Sl, b_Sl = K.ps[bl], K.b_ps[bl]
                ops = []
                for idx, j in enumerate(loc):
                    ops.append((idx, j, None))
                    if j == n - 1:
                        ops.append((idx, j, maskA))
                    elif j == n + 1:
                        ops.append((idx, j, maskB))
                for oi, (idx, j, msk) in enumerate(ops):
                    first, last = oi == 0, oi == len(ops) - 1
                    has_mask = (j != n)
                    if msk is None:
                        P.mm(lambda e, Sl=Sl, idx=idx, j=j, g=g, base=base, qa=qa, has_mask=has_mask: e.matmul(
                            Sl[:, idx * 128:(idx + 1) * 128], kd[g][base:base + 64, j * 128:(j + 1) * 128], qa,
                            start=True, stop=not has_mask), [b_k[g][j // 4], b_q[c][tq]], b_Sl, first, last)
                    else:
                        P.mm(lambda e, Sl=Sl, idx=idx, msk=msk: e.matmul(
                            Sl[:, idx * 128:(idx + 1) * 128], ident_b, msk, start=False, stop=True),
                            [K.b_const], b_Sl, first, last)
                nl = len(loc)
                P.op("act", lambda e, Sl=Sl, p_i=p_i, nl=nl: e.activation(
                    Pt[p_i][:, 0:nl, :], Sl[:, 0:nl * 128].rearrange("p (a n) -> p a n", a=nl), AF.Exp, scale=0.125),
                    reads=[b_Sl], partial=[b_P[p_i]])
            bc_ = 2 + (bi_ * 16 + h) % 2
            Sc, b_Sc = K.ps[bc_], K.b_ps[bc_]
            for ci in range(2):
                j = 16 + ci
                P.mm(lambda e, Sc=Sc, ci=ci, j=j, g=g, base=base, qa=qa: e.matmul(
                    Sc[:, ci * 128:(ci + 1) * 128], kd[g][base:base + 64, j * 128:(j + 1) * 128], qa,
                    start=True, stop=True), [b_k[g][4], b_q[c][tq]], b_Sc, ci == 0, ci == 1)
            P.op("act", lambda e, Sc=Sc, p_i=p_i: e.activation(
                Pt[p_i][:, 3:5, :], Sc[:, 0:256].rearrange("p (a n) -> p a n", a=2), AF.Exp, scale=0.125),
                reads=[b_Sc], partial=[b_P[p_i]])

        def emit_PV(h, chunks=chunks, bi_=bi_):
            g = h // 4
            p_i = (bi_ * 16 + h) % 2
            ob = 5 + h // 7
            off = (h % 7) * 65
            O, b_O = K.ps[ob], K.b_ps[ob]
            hb_first = (h % 7 == 0)
            hb_last = (h % 7 == 6) or (h == 15)
            for ki_, (j, slot) in enumerate(chunks):
                P.mm(lambda e, O=O, off=off, p_i=p_i, slot=slot, j=j, g=g, ki_=ki_, nch=len(chunks): e.matmul(
                    O[:, off:off + 65], Pt[p_i][:, slot, :], Va[:, j, g * 65:(g + 1) * 65],
                    start=(ki_ == 0), stop=(ki_ == nch - 1)), [b_P[p_i], b_v], b_O,
                    hb_first and ki_ == 0, hb_last and ki_ == len(chunks) - 1)

        emit_S(0)
        for h in range(16):
            if h + 1 < 16:
                emit_S(h + 1)
            emit_PV(h)
        for bk, (h0, nh) in enumerate(((0, 7), (7, 7), (14, 2))):
            O, b_O = K.ps[5 + bk], K.b_ps[5 + bk]
            Ov = O[:, 0:nh * 65].rearrange("p (h d) -> p h d", d=65)
            P.op("dve", lambda e, Ov=Ov, h0=h0, nh=nh, yi=yi: e.tensor_tensor(
                dsum[yi][:, h0:h0 + nh], Ov[:, :, 64], esink[:, h0:h0 + nh], ALU.add),
                reads=[b_O, b_es], partial=[b_ds[yi]])
            P.op("dve", lambda e, h0=h0, nh=nh, yi=yi: e.reciprocal(dsum[yi][:, h0:h0 + nh], dsum[yi][:, h0:h0 + nh]),
                 reads=[b_ds[yi]], writes=[b_ds[yi]])
            for hh in range(nh):
                P.op("dve", lambda e, Ov=Ov, hh=hh, h0=h0, yi=yi: e.tensor_scalar(
                    yt[yi][:, h0 + hh, :], Ov[:, hh, 0:64], dsum[yi][:, h0 + hh:h0 + hh + 1], None, ALU.mult),
                    reads=[b_O, b_ds[yi]], partial=[b_y[yi]])
        Tb = K.ps[4][:].bitcast(BF16)
        b_T = K.b_ps[4]
        yflat = yt[yi][:].rearrange("p h d -> p (h d)")
        for c in range(8):
            P.mm(lambda e, Tb=Tb, c=c, yflat=yflat: e.transpose(
                Tb[:, c * 128:(c + 1) * 128], yflat[:, c * 128:(c + 1) * 128], ident_b),
                [b_y[yi], K.b_const], b_T, c == 0, c == 7)
        a = n % 4
        ys = ysi % 2
        P.op("act", lambda e, Tb=Tb, ys=ys, a=a: e.copy(
            yst[ys][:, :, a * 128:(a + 1) * 128], Tb[:, 0:1024].rearrange("p (c n) -> p c n", c=8)),
            reads=[b_T], partial=[b_yst[ys]])
        end_grp = (a == 3) or (n == blocks[-1])
        if end_grp:
            g0 = (n // 4) * 4
            w = (a + 1) * 128
            P.dma("sp", ydst[:, 0:8, g0 * 128:g0 * 128 + w], yst[ys][:, :, 0:w], K.b_ycat[s]["att"],
                  reads=[b_yst[ys]])
            ysi += 1


def phase_B(K, l, s):
    P = K.P
    last = (l == DEPTH - 1)
    wring(K, 3)
    tmp = h_tmp(K, "pb_")
    ident_f = K.mats_f[:, 0, :]
    xt = sb(K, "pb_x", [128, 16, 512], F32)
    b_xt = P.buf("pb_x")
    h = sb(K, "pb_h", [128, 16, 512], BF16)
    b_h = P.buf("pb_h")
    yc = sb(K, "pb_yc", [128, 16, 512], BF16)
    b_yc = P.buf("pb_yc")
    m = sb(K, "pb_m", [128, 16, 512], BF16)
    b_m = P.buf("pb_m")
    gs = [sb(K, f"pb_gs{i}", [128, 512], BF16) for i in range(2)]
    b_gs = P.bufs("pb_gs", 2)
    acc = sb(K, "pb_acc", [128, 4, 512], F32)
    b_acc = P.bufs("pb_acc", 4)
    tr = [sb(K, f"pb_tr{i}", [128, 512], F32) for i in range(2)]
    b_tr = P.bufs("pb_tr", 2)
    ev = [sb(K, f"pb_ev{i}", [128, 512], F32) for i in range(4)]
    b_ev = P.bufs("pb_ev", 4)
    ei_ = 0
    hid = [sb(K, f"pb_hid{i}", [128, 4, 512], BF16) for i in range(2)]
    b_hid = P.bufs("pb_hid", 2)
    b_ost = [P.buf("pb_ost")] * 2
    xsrc = K.xT[l % 2][s].rearrange("(c p) t -> p c t", p=128)
    xdst = K.xT[(l + 1) % 2][s].rearrange("(c p) t -> p c t", p=128)
    ysrc = K.ycat[s].rearrange("(c p) t -> p c t", p=128)
    win = K.wb["w_in"][l].rearrange("(k p) n -> p k n", p=128)
    b_win = K.b_wb["w_in"][l]
    branches = [(GATE_OFF, K.wb["w_att_o"][l].rearrange("(k p) n -> p k n", p=128), 8, 0, K.b_wb["w_att_o"][l]),
                (GATE_OFF + D, K.wb["w_hy_o"][l].rearrange("(k p) n -> p k n", p=128), 4, 8, K.b_wb["w_hy_o"][l]),
                (GATE_OFF + 2 * D, K.wb["w_pool_o"][l].rearrange("(k p) n -> p k n", p=128), 4, 12,
                 K.b_wb["w_pool_o"][l])]
    wout = K.wb["w_out"][l].rearrange("(k p) n -> p k n", p=128)
    w1v = K.wb["mlp_w1"][l].rearrange("(k p) n -> p k n", p=128)
    gi = 0
    ti_ = 0
    hi_ = 0
    oi = 0
    for (t0, T) in tiles_of(l, with_ctx=not last):
        is_ctx = t0 >= L
        who = 2 if is_ctx else s
        P.dma("act", xt[:, :, 0:T], xsrc[:, :, t0:t0 + T], b_xt, reads=[K.b_xT[l % 2][s]], partial=False)
        P.dma("act", yc[:, :, 0:T], ysrc[:, :, t0:t0 + T], b_yc,
              reads=[K.b_ycat[s]["att"], K.b_ycat[s]["hy"], K.b_ycat[s]["pool"]], partial=False)
        make_h(K, xt, b_xt, h, b_h, T, l, who, 0, tmp)
        for J in range(4):
            for r, (gcol, wo_ap, kr, yoff, b_wsrc) in enumerate(branches):
                wg, b_wg = wload(K, win[:, :, gcol + J * 512:gcol + (J + 1) * 512], 16, 512, b_win)
                wo, b_wo = wload(K, wo_ap[:, :, J * 512:(J + 1) * 512], kr, 512, b_wsrc)
                for j in range(4):
                    psg, bpsg = nextps(K)
                    for k in range(16):
                        P.mm(lambda e, psg=psg, wg=wg, j=j, k=k, T=T: e.matmul(
                            psg[:, 0:T], wg[:, k, j * 128:(j + 1) * 128], h[:, k, 0:T], start=(k == 0), stop=(k == 15)),
                            [b_wg, b_h], bpsg, k == 0, k == 15)
                    pso, bpso = nextps(K)
                    for k in range(kr):
                        P.mm(lambda e, pso=pso, wo=wo, j=j, k=k, T=T, yoff=yoff, kr=kr: e.matmul(
                            pso[:, 0:T], wo[:, k, j * 128:(j + 1) * 128], yc[:, yoff + k, 0:T],
                            start=(k == 0), stop=(k == kr - 1)), [b_wo, b_yc], bpso, k == 0, k == kr - 1)
                    g_i = gi % 2
                    gi += 1
                    P.op("act", lambda e, psg=psg, g_i=g_i, T=T: e.activation(gs[g_i][:, 0:T], psg[:, 0:T], AF.Sigmoid),
                         reads=[bpsg], writes=[b_gs[g_i]])
                    if r == 0:
                        P.op("dve", lambda e, pso=pso, g_i=g_i, j=j, T=T: e.tensor_tensor(
                            acc[:, j, 0:T], pso[:, 0:T], gs[g_i][:, 0:T], ALU.mult), reads=[bpso, b_gs[g_i]],
                            writes=[b_acc[j]])
                    else:
                        t_i = ti_ % 2
                        ti_ += 1
                        P.op("dve", lambda e, pso=pso, g_i=g_i, t_i=t_i, T=T: e.tensor_tensor(
                            tr[t_i][:, 0:T], pso[:, 0:T], gs[g_i][:, 0:T], ALU.mult), reads=[bpso, b_gs[g_i]],
                            writes=[b_tr[t_i]])
                        if r == 1:
                            P.op("pool", lambda e, t_i=t_i, j=j, T=T: e.tensor_tensor(
                                acc[:, j, 0:T], acc[:, j, 0:T], tr[t_i][:, 0:T], ALU.add), reads=[b_tr[t_i], b_acc[j]],
                                writes=[b_acc[j]])
                        else:
                            P.op("pool", lambda e, t_i=t_i, j=j, J=J, T=T: e.tensor_tensor(
                                m[:, 4 * J + j, 0:T], acc[:, j, 0:T], tr[t_i][:, 0:T], ALU.add),
                                reads=[b_tr[t_i], b_acc[j]], partial=[b_m])
        if K.cut == 1:
            P.dma("sp", xdst[:, :, t0:t0 + T], xt[:, :, 0:T], K.b_xT[(l + 1) % 2][s], reads=[b_xt, b_m])
            continue
        for J in range(4):
            wo, b_wo = wload(K, wout[:, :, J * 512:(J + 1) * 512], 16, 512, K.b_wb["w_out"][l])
            for j in range(4):
                cc = 4 * J + j
                ps, bps = nextps(K)
                for k in range(16):
                    P.mm(lambda e, ps=ps, wo=wo, j=j, k=k, T=T: e.matmul(
                        ps[:, 0:T], wo[:, k, j * 128:(j + 1) * 128], m[:, k, 0:T], start=(k == 0), stop=(k == 15)),
                        [b_wo, b_m], bps, k == 0, k == 15)
                e_i = ei_ % 4
                ei_ += 1
                P.op("act", lambda e, ps=ps, cc=cc, T=T, who=who, e_i=e_i: e.activation(
                    ev[e_i][:, 0:T], ps[:, 0:T], AF.Copy, scale=modcol(K, l, who, 2, cc)),
                    reads=[bps, K.b_mod], writes=[b_ev[e_i]])
                P.op("pool", lambda e, cc=cc, T=T, e_i=e_i: e.tensor_tensor(
                    xt[:, cc, 0:T], xt[:, cc, 0:T], ev[e_i][:, 0:T], ALU.add),
                    reads=[b_ev[e_i], b_xt], writes=[b_xt])
        if K.cut == 2:
            P.dma("sp", xdst[:, :, t0:t0 + T], xt[:, :, 0:T], K.b_xT[(l + 1) % 2][s], reads=[b_xt, b_m])
            continue
        make_h(K, xt, b_xt, h, b_h, T, l, who, 1, tmp)
        if K.cut == 3:
            P.dma("sp", xdst[:, :, t0:t0 + T], xt[:, :, 0:T], K.b_xT[(l + 1) % 2][s], reads=[b_xt, b_h])
            continue
        for Hb in range(16):
            w1, b_w1 = wload(K, w1v[:, :, Hb * 512:(Hb + 1) * 512], 16, 512, K.b_wb["mlp_w1"][l])
            w2, b_w2 = wload(K, K.wb["mlp_w2"][l][Hb * 512:(Hb + 1) * 512, :].rearrange("(k p) n -> p k n", p=128),
                             4, D, K.b_wb["mlp_w2"][l])
            h_i = hi_ % 2
            hi_ += 1
            for jj in range(4):
                ps, bps = nextps(K)
                for k in range(16):
                    P.mm(lambda e, ps=ps, w1=w1, jj=jj, k=k, T=T: e.matmul(
                        ps[:, 0:T], w1[:, k, jj * 128:(jj + 1) * 128], h[:, k, 0:T], start=(k == 0), stop=(k == 15)),
                        [b_w1, b_h], bps, k == 0, k == 15)
                t_i = ti_ % 2
                ti_ += 1
                P.op("act", lambda e, ps=ps, t_i=t_i, T=T: e.activation(tr[t_i][:, 0:T], ps[:, 0:T], AF.Square),
                     reads=[bps], writes=[b_tr[t_i]])
                P.op("dve", lambda e, ps=ps, t_i=t_i, h_i=h_i, jj=jj, T=T: e.scalar_tensor_tensor(
                    hid[h_i][:, jj, 0:T], ps[:, 0:T], 0.0, tr[t_i][:, 0:T], ALU.is_gt, ALU.mult),
                    reads=[bps, b_tr[t_i]], partial=[b_hid[h_i]])
            if K.cut == 4:
                P.dma("sp", xdst[:, 0:4, t0:t0 + T], hid[h_i][:, :, 0:T].bitcast(F32) if False else xt[:, 0:4, 0:T],
                      K.b_xT[(l + 1) % 2][s], reads=[b_xt, b_hid[h_i], b_w2])
                continue
            for j in range(16):
                ps, bps = nextps(K)
                for k in range(4):
                    P.mm(lambda e, ps=ps, w2=w2, j=j, k=k, T=T, h_i=h_i: e.matmul(
                        ps[:, 0:T], w2[:, k, j * 128:(j + 1) * 128], hid[h_i][:, k, 0:T], start=(k == 0), stop=(k == 3)),
                        [b_w2, b_hid[h_i]], bps, k == 0, k == 3)
                if K.cut == 7:
                    P.op("dve", lambda e, ps=ps, j=j, T=T, who=who: e.scalar_tensor_tensor(
                        tr[0][:, 0:T], ps[:, 0:T], modcol(K, l, who, 5, j), xt[:, j, 0:T], ALU.mult, ALU.add),
                        reads=[bps, K.b_mod, b_xt], writes=[b_tr[0]])
                    continue
                if K.cut == 8:
                    P.op("dve", lambda e, ps=ps, j=j, T=T, who=who: e.tensor_tensor(
                        xt[:, j, 0:T], ps[:, 0:T], xt[:, j, 0:T], ALU.add),
                        reads=[bps, b_xt], writes=[b_xt])
                    continue
                if K.cut == 6:
                    P.op("dve", lambda e, ps=ps, T=T: e.tensor_copy(tr[0][:, 0:T], ps[:, 0:T]),
                         reads=[bps], writes=[b_tr[0]])
                    continue
                e_i = ei_ % 4
                ei_ += 1
                P.op("act", lambda e, ps=ps, j=j, T=T, who=who, e_i=e_i: e.activation(
                    ev[e_i][:, 0:T], ps[:, 0:T], AF.Copy, scale=modcol(K, l, who, 5, j)),
                    reads=[bps, K.b_mod], writes=[b_ev[e_i]])
                P.op("pool", lambda e, j=j, T=T, e_i=e_i: e.tensor_tensor(
                    xt[:, j, 0:T], xt[:, j, 0:T], ev[e_i][:, 0:T], ALU.add),
                    reads=[b_ev[e_i], b_xt], writes=[b_xt])
        if not last:
            P.dma("act", xdst[:, :, t0:t0 + T], xt[:, :, 0:T], K.b_xT[(l + 1) % 2][s], reads=[b_xt])
            if K.cut in (5, 6, 7, 8):
                break
        else:
            P.dma("act", K.out[s].rearrange("(c p) t -> p c t", p=128)[:, :, t0:t0 + T], xt[:, :, 0:T], K.b_out,
                  reads=[b_xt])


def make_in_maps(inp, cores):
    c = _consts()
    f32 = lambda a: np.ascontiguousarray(np.asarray(a, np.float32))
    shared = {k: f32(inp[k]) for k in ("w_mod", "w_in", "filt_w0", "filt_w1", "filt_w2", "pool_w", "w_att_o",
                                       "w_hy_o", "w_pool_o", "w_out", "mlp_w1", "mlp_w2")}
    shared.update(c)
    shared["bc"] = _bc_layout(inp)
    maps = []
    pp_off = None
    for core in cores:
        pp = _pp_layout(inp, core)
        m = dict(shared)
        xs = np.concatenate([inp["x"][core * SPC:(core + 1) * SPC], inp["ctx"][core * SPC:(core + 1) * SPC]], axis=1)
        m["xTin"] = f32(xs.transpose(0, 2, 1))
        m["pp"] = pp.build()
        pp_off = (pp.off, pp.n)
        maps.append(m)
    return maps, pp_off, shared["bc"].shape[1]


def kernel(**inputs):
    inp = {k: np.asarray(v) for k, v in inputs.items()}
    cores = list(range(NCORES))
    maps, (off, npp), nbc = make_in_maps(inp, cores)
    nc, P = build_program(off, npp, nbc)
    res = run_bass_kernel_spmd(nc, maps, core_ids=cores)
    outs = [np.asarray(r["outT"]).transpose(0, 2, 1) for r in res.results]
    return np.ascontiguousarray(np.concatenate(outs, axis=0).astype(np.float32))
```
